# Optimizing a Trainium2 kernel written in Bass

```python
import functools
import jax, jax.numpy as jnp
from jax import lax
import numpy as np

D_MODEL = 1024
BATCH = 2
SEQ = 8192
DEPTH = 2

GRID_W = 64
CTX_LEN = 256
CHUNK = 64
ROPE_THETA = 10000.0

RET_HEADS = 4
RET_DH = 64
RET_W = RET_HEADS * RET_DH
GLA_HEADS = 4
GLA_DK = 48
GLA_DV = 96
GLA_KW = GLA_HEADS * GLA_DK
GLA_VW = GLA_HEADS * GLA_DV
GLA_RANK = 16
GLA_TAU = 16.0
RWKV_HEADS = 6
RWKV_DH = 64
RWKV_W = RWKV_HEADS * RWKV_DH
RWKV_W_RANK = 64
RWKV_A_RANK = 64
RWKV_G_RANK = 128

MIX_W = RET_W + GLA_VW + RWKV_W
RET_COLS = (RET_W, RET_W, RET_W, RET_W)
GLA_COLS = (GLA_KW, GLA_KW, GLA_VW, GLA_VW, GLA_RANK, GLA_RANK)
RWKV_COLS = (RWKV_W, RWKV_W, RWKV_W, RWKV_W_RANK, RWKV_W_RANK, RWKV_A_RANK, RWKV_G_RANK)
RET_IN = 4 * RET_W
GLA_IN = 2 * GLA_KW + 2 * GLA_VW + 2 * GLA_RANK
RWKV_IN = 3 * RWKV_W + 2 * RWKV_W_RANK + RWKV_A_RANK + RWKV_G_RANK
IN_W = RET_IN + GLA_IN + RWKV_IN
FFN_HIDDEN = -(-8 * D_MODEL // (3 * 256)) * 256

kernel_name = "hybrid_retnet_gla_rwkv7_prefix_dit_block"


def _split(u, widths):
    return jnp.split(u, [int(i) for i in np.cumsum(widths)[:-1]], axis=-1)


def _heads(u, n):
    b, t, w = u.shape
    return u.reshape(b, t, n, w // n).transpose(0, 2, 1, 3)


def _merge(o):
    b, h, t, d = o.shape
    return o.transpose(0, 2, 1, 3).reshape(b, t, h * d)


def _rmsnorm(x, g, eps=1e-6):
    xf = x.astype(jnp.float32)
    y = xf * lax.rsqrt(jnp.mean(xf * xf, axis=-1, keepdims=True) + eps)
    return (y * g.astype(jnp.float32)).astype(x.dtype)


def _norm_heads(o, eps):
    mu = jnp.mean(o, axis=-1, keepdims=True)
    var = jnp.mean(jnp.square(o - mu), axis=-1, keepdims=True)
    return (o - mu) * lax.rsqrt(var + eps)


def _modulate(h, shift, scale):
    return h * (1.0 + scale) + shift


def _swiglu(h, w_gate, w_up, w_down):
    return (jax.nn.silu(h @ w_gate) * (h @ w_up)) @ w_down


def _rope_2d(x, row, col):
    half = x.shape[-1] // 2
    nf = half // 2
    inv = ROPE_THETA ** (-jnp.arange(nf, dtype=jnp.float32) / nf)

    def rot(xh, pos):
        ang = pos[:, None] * inv
        cos, sin = jnp.cos(ang), jnp.sin(ang)
        x1, x2 = xh[..., :nf], xh[..., nf:]
        return jnp.concatenate([x1 * cos - x2 * sin, x1 * sin + x2 * cos], axis=-1)

    return jnp.concatenate([rot(x[..., :half], row), rot(x[..., half:], col)], axis=-1)


def _centred_conv(u, w):
    return lax.conv_general_dilated(
        u, w.astype(u.dtype)[:, None, :], window_strides=(1,), padding=((1, 1),),
        dimension_numbers=("NWC", "WIO", "NWC"), feature_group_count=u.shape[-1])


def _chunks(a):
    b, h, t, d = a.shape
    return a.reshape(b, h, t // CHUNK, CHUNK, d).transpose(2, 0, 1, 3, 4)


def _unchunk(o):
    n, b, h, c, d = o.shape
    return o.transpose(1, 2, 0, 3, 4).reshape(b, h, n * c, d)


def _flip(a):
    return jnp.flip(a, axis=2)


def _bidirectional(scan_f, scan_b, ctx_f, ctx_b, lat_f, lat_b, s0):
    oc_f, sc_f = scan_f(ctx_f, s0)
    oc_b, sc_b = scan_b(tuple(_flip(a) for a in ctx_b), s0)
    ol_f, _ = scan_f(lat_f, sc_f)
    ol_b, _ = scan_b(tuple(_flip(a) for a in lat_b), sc_b)
    return oc_f + _flip(oc_b), ol_f + _flip(ol_b)


def _retention_scan(seq, s0, log_gamma):
    pos = jnp.arange(CHUNK, dtype=jnp.float32)
    rel = pos[:, None] - pos[None, :]
    intra = jnp.where(rel >= 0, jnp.exp(log_gamma[:, None, None] * jnp.maximum(rel, 0.0)), 0.0)
    q_dec = jnp.exp(log_gamma[:, None] * (pos + 1.0))[:, :, None]
    k_dec = jnp.exp(log_gamma[:, None] * (CHUNK - 1.0 - pos))[:, :, None]
    c_dec = jnp.exp(log_gamma * CHUNK)[:, None, None]

    def step(s, blk):
        qb, kb, vb = blk
        sc = jnp.einsum("bhid,bhjd->bhij", qb, kb) * intra
        o = jnp.einsum("bhij,bhjv->bhiv", sc, vb) + jnp.einsum("bhid,bhdv->bhiv", qb * q_dec, s)
        s = s * c_dec + jnp.einsum("bhjd,bhjv->bhdv", kb * k_dec, vb)
        return s, o

    s, o = lax.scan(step, s0, tuple(_chunks(a) for a in seq))
    return _unchunk(o), s


def _gla_scan(seq, s0):
    causal = jnp.tril(jnp.ones((CHUNK, CHUNK), dtype=bool))

    def step(s, blk):
        qb, kb, vb, gb = blk
        cum = jnp.cumsum(gb, axis=2)
        rel = cum[:, :, :, None, :] - cum[:, :, None, :, :]
        decay = jnp.exp(jnp.where(causal[:, :, None], rel, -jnp.inf))
        sc = jnp.einsum("bhid,bhjd,bhijd->bhij", qb, kb, decay)
        o = jnp.einsum("bhij,bhjv->bhiv", sc, vb) + jnp.einsum("bhid,bhdv->bhiv", qb * jnp.exp(cum), s)
        last = cum[:, :, -1:, :]
        s = s * jnp.exp(last[:, :, 0, :, None]) + jnp.einsum("bhjd,bhjv->bhdv", kb * jnp.exp(last - cum), vb)
        return s, o

    s, o = lax.scan(step, s0, tuple(_chunks(a) for a in seq))
    return _unchunk(o), s


def _rwkv_scan(seq, s0):
    def step(s, inp):
        rt, wt, kt, vt, kkt, at = inp
        sa = jnp.einsum("bhvk,bhk->bhv", s, -kkt)
        s = s * wt[:, :, None, :] + sa[..., None] * (kkt * at)[:, :, None, :] + vt[..., None] * kt[:, :, None, :]
        return s, jnp.einsum("bhvk,bhk->bhv", s, rt)

    s, o = lax.scan(step, s0, tuple(jnp.moveaxis(a, 2, 0) for a in seq))
    return jnp.moveaxis(o, 0, 2), s


def _retention_mixer(u_ctx, u_lat, row, col, ret_norm, ctx_out):
    def prep(u, rotate):
        q, k, v, g = _split(u, RET_COLS)
        q, k, v = (_heads(a, RET_HEADS) for a in (q, k, v))
        if rotate:
            q, k = _rope_2d(q, row, col), _rope_2d(k, row, col)
        return (q, k * RET_DH ** -0.5, v), g

    seq_c, g_c = prep(u_ctx, False)
    seq_l, g_l = prep(u_lat, True)
    log_gamma = jnp.log1p(-jnp.exp2(-5.0 - jnp.arange(RET_HEADS, dtype=jnp.float32)))
    scan_f = functools.partial(_retention_scan, log_gamma=log_gamma)
    scan_b = functools.partial(_retention_scan, log_gamma=jnp.flip(log_gamma))
    s0 = jnp.zeros((u_lat.shape[0], RET_HEADS, RET_DH, RET_DH), jnp.float32)
    o_c, o_l = _bidirectional(scan_f, scan_b, seq_c, seq_c, seq_l, seq_l, s0)

    def finish(o, g):
        return jax.nn.silu(g) * (_merge(_norm_heads(o, 1e-5)) * ret_norm)

    return (finish(o_c, g_c) if ctx_out else None), finish(o_l, g_l)


def _gla_mixer(u_ctx, u_lat, wa2_f, ba_f, wa2_b, ba_b, gla_norm, ctx_out):
    def prep(u):
        q, k, v, g, lr_f, lr_b = _split(u, GLA_COLS)
        q = _heads(q, GLA_HEADS) * GLA_DK ** -0.5
        k = _heads(k, GLA_HEADS)
        v = _heads(v, GLA_HEADS)
        la_f = _heads(jax.nn.log_sigmoid(lr_f @ wa2_f + ba_f) / GLA_TAU, GLA_HEADS)
        la_b = _heads(jax.nn.log_sigmoid(lr_b @ wa2_b + ba_b) / GLA_TAU, GLA_HEADS)
        return (q, k, v, la_f), (q, k, v, la_b), g

    cf, cb, g_c = prep(u_ctx)
    lf, lb, g_l = prep(u_lat)
    s0 = jnp.zeros((u_lat.shape[0], GLA_HEADS, GLA_DK, GLA_DV), jnp.float32)
    o_c, o_l = _bidirectional(_gla_scan, _gla_scan, cf, cb, lf, lb, s0)

    def finish(o, g):
        y = o * lax.rsqrt(jnp.mean(o * o, axis=-1, keepdims=True) + 1e-5) * gla_norm
        return jax.nn.silu(g) * _merge(y)

    return (finish(o_c, g_c) if ctx_out else None), finish(o_l, g_l)


def _rwkv_mixer(u_ctx, u_lat, conv_w, w0_f, w2_f, w0_b, w2_b, a0, a2, g2, k_k, k_a, r_k, ln_w, ln_b, ctx_out):
    def decay(lr, w0, w2):
        log_w = -jax.nn.softplus(-(w0 + jnp.tanh(lr) @ w2)) - 0.5
        return jnp.exp(-jnp.exp(log_w))

    def prep(u):
        u = _centred_conv(u, conv_w)
        r, k, v, lr_wf, lr_wb, lr_a, lr_g = _split(u, RWKV_COLS)
        a = jax.nn.sigmoid(a0 + lr_a @ a2)
        g = jax.nn.sigmoid(lr_g) @ g2
        kk = _heads(k * k_k, RWKV_HEADS)
        kk = kk * lax.rsqrt(jnp.maximum(jnp.sum(kk * kk, axis=-1, keepdims=True), 1e-24))
        k = k * (1.0 + (a - 1.0) * k_a)
        r, k, v, a = (_heads(t, RWKV_HEADS) for t in (r, k, v, a))
        wf = _heads(decay(lr_wf, w0_f, w2_f), RWKV_HEADS)
        wb = _heads(decay(lr_wb, w0_b, w2_b), RWKV_HEADS)
        bonus = jnp.sum(r * k * r_k[:, None, :], axis=-1, keepdims=True) * v
        return (r, wf, k, v, kk, a), (r, wb, k, v, kk, a), g, bonus

    cf, cb, g_c, bo_c = prep(u_ctx)
    lf, lb, g_l, bo_l = prep(u_lat)
    s0 = jnp.zeros((u_lat.shape[0], RWKV_HEADS, RWKV_DH, RWKV_DH), jnp.float32)
    o_c, o_l = _bidirectional(_rwkv_scan, _rwkv_scan, cf, cb, lf, lb, s0)

    def finish(o, g, bonus):
        y = _merge(_norm_heads(o, 64e-5)) * ln_w + ln_b + _merge(bonus)
        return y * g

    return (finish(o_c, g_c, bo_c) if ctx_out else None), finish(o_l, g_l, bo_l)


def setup_inputs(seed: int = 0) -> dict:
    key = jax.random.key(seed)
    keys = iter(jax.random.split(key, 48))
    L, D = DEPTH, D_MODEL

    def rnd(shape, scale):
        return jax.random.normal(next(keys), shape, jnp.float32) * scale

    def gain(shape):
        return 1.0 + rnd(shape, 0.02)

    decay_base = jnp.linspace(-6.5, -1.5, RWKV_W, dtype=jnp.float32)[None, :]
    shift_taps = jnp.array([0.0, 1.0, 0.0], jnp.float32)[None, :, None]
    return {
        "x": rnd((BATCH, SEQ, D), 1.0),
        "c": rnd((BATCH, D), 1.0),
        "ctx": rnd((BATCH, CTX_LEN, D), 1.0),
        "c_ctx": rnd((D,), 1.0),
        "w_mod": rnd((L, D, 6 * D), D ** -0.5),
        "b_mod": rnd((L, 6 * D), 0.02),
        "norm_mix_pre": gain((L, D)),
        "norm_mix_post": gain((L, D)),
        "norm_ffn_pre": gain((L, D)),
        "norm_ffn_post": gain((L, D)),
        "w_in": rnd((L, D, IN_W), D ** -0.5),
        "ret_norm": gain((L, RET_W)),
        "gla_wa2_f": rnd((L, GLA_RANK, GLA_KW), GLA_RANK ** -0.5),
        "gla_ba_f": rnd((L, GLA_KW), 0.1),
        "gla_wa2_b": rnd((L, GLA_RANK, GLA_KW), GLA_RANK ** -0.5),
        "gla_ba_b": rnd((L, GLA_KW), 0.1),
        "gla_norm": gain((L, GLA_DV)),
        "rw_conv": rnd((L, 3, RWKV_IN), 0.2) + shift_taps,
        "rw_w0_f": decay_base + rnd((L, RWKV_W), 0.1),
        "rw_w2_f": rnd((L, RWKV_W_RANK, RWKV_W), 0.1),
        "rw_w0_b": decay_base + rnd((L, RWKV_W), 0.1),
        "rw_w2_b": rnd((L, RWKV_W_RANK, RWKV_W), 0.1),
        "rw_a0": rnd((L, RWKV_W), 0.1),
        "rw_a2": rnd((L, RWKV_A_RANK, RWKV_W), 0.5 * RWKV_A_RANK ** -0.5),
        "rw_g2": rnd((L, RWKV_G_RANK, RWKV_W), RWKV_G_RANK ** -0.5),
        "rw_k_k": 0.85 + rnd((L, RWKV_W), 0.02),
        "rw_k_a": 1.0 + rnd((L, RWKV_W), 0.02),
        "rw_r_k": rnd((L, RWKV_HEADS, RWKV_DH), 0.1),
        "rw_ln_w": gain((L, RWKV_W)),
        "rw_ln_b": rnd((L, RWKV_W), 0.02),
        "w_out": rnd((L, MIX_W, D), MIX_W ** -0.5),
        "w_ffn_gate": rnd((L, D, FFN_HIDDEN), D ** -0.5),
        "w_ffn_up": rnd((L, D, FFN_HIDDEN), D ** -0.5),
        "w_ffn_down": rnd((L, FFN_HIDDEN, D), FFN_HIDDEN ** -0.5),
    }


def reference(x, c, ctx, c_ctx, w_mod, b_mod, norm_mix_pre, norm_mix_post, norm_ffn_pre, norm_ffn_post,
              w_in, ret_norm, gla_wa2_f, gla_ba_f, gla_wa2_b, gla_ba_b, gla_norm, rw_conv,
              rw_w0_f, rw_w2_f, rw_w0_b, rw_w2_b, rw_a0, rw_a2, rw_g2, rw_k_k, rw_k_a, rw_r_k,
              rw_ln_w, rw_ln_b, w_out, w_ffn_gate, w_ffn_up, w_ffn_down):
    t = x.shape[1]
    rows = t // GRID_W
    row = jnp.repeat(jnp.arange(rows, dtype=jnp.float32), GRID_W)
    col = jnp.tile(jnp.arange(GRID_W, dtype=jnp.float32), rows)

    for l in range(DEPTH):
        ctx_out = l < DEPTH - 1
        mod_l = jax.nn.silu(c) @ w_mod[l] + b_mod[l]
        mod_c = jax.nn.silu(c_ctx) @ w_mod[l] + b_mod[l]
        sh1, sc1, gt1, sh2, sc2, gt2 = jnp.split(mod_l[:, None, :], 6, axis=-1)
        csh1, csc1, cgt1, csh2, csc2, cgt2 = jnp.split(mod_c[None, None, :], 6, axis=-1)

        u_lat = (_modulate(_rmsnorm(x, norm_mix_pre[l]), sh1, sc1) @ w_in[l]).astype(jnp.float32)
        u_ctx = (_modulate(_rmsnorm(ctx, norm_mix_pre[l]), csh1, csc1) @ w_in[l]).astype(jnp.float32)
        ua_c, ub_c, uc_c = _split(u_ctx, (RET_IN, GLA_IN, RWKV_IN))
        ua_l, ub_l, uc_l = _split(u_lat, (RET_IN, GLA_IN, RWKV_IN))
        ya_c, ya_l = _retention_mixer(ua_c, ua_l, row, col, ret_norm[l], ctx_out)
        yb_c, yb_l = _gla_mixer(ub_c, ub_l, gla_wa2_f[l], gla_ba_f[l], gla_wa2_b[l], gla_ba_b[l],
                                gla_norm[l], ctx_out)
        yc_c, yc_l = _rwkv_mixer(uc_c, uc_l, rw_conv[l], rw_w0_f[l], rw_w2_f[l], rw_w0_b[l], rw_w2_b[l],
                                 rw_a0[l], rw_a2[l], rw_g2[l], rw_k_k[l], rw_k_a[l], rw_r_k[l],
                                 rw_ln_w[l], rw_ln_b[l], ctx_out)
        y_lat = jnp.concatenate([ya_l, yb_l, yc_l], axis=-1).astype(x.dtype) @ w_out[l]
        x = x + gt1 * _rmsnorm(y_lat, norm_mix_post[l])

        f_lat = _swiglu(_modulate(_rmsnorm(x, norm_ffn_pre[l]), sh2, sc2), w_ffn_gate[l], w_ffn_up[l], w_ffn_down[l])
        x = x + gt2 * _rmsnorm(f_lat, norm_ffn_post[l])

        if ctx_out:
            y_ctx = jnp.concatenate([ya_c, yb_c, yc_c], axis=-1).astype(ctx.dtype) @ w_out[l]
            ctx = ctx + cgt1 * _rmsnorm(y_ctx, norm_mix_post[l])
            f_ctx = _swiglu(_modulate(_rmsnorm(ctx, norm_ffn_pre[l]), csh2, csc2), w_ffn_gate[l], w_ffn_up[l], w_ffn_down[l])
            ctx = ctx + cgt2 * _rmsnorm(f_ctx, norm_ffn_post[l])
    return x
```

```python
import contextlib
import numpy as np
import concourse.bass as bass
import concourse.mybir as mybir

F32 = mybir.dt.float32
BF16 = mybir.dt.bfloat16
AF = mybir.ActivationFunctionType
ALU = mybir.AluOpType
AX = mybir.AxisListType


class Res:
    __slots__ = ("w", "r", "name")

    def __init__(self, name=""):
        self.w = None
        self.r = {}
        self.name = name


class V:
    __slots__ = ("ap", "res")

    def __init__(self, ap, res):
        self.ap = ap
        self.res = res

    def __getitem__(self, idx):
        return V(self.ap[idx], self.res)

    def bc(self, shape):
        return V(self.ap.broadcast_to(list(shape)), self.res)

    def rr(self, pat, **kw):
        return V(self.ap.rearrange(pat, **kw), self.res)

    def sub(self, name=""):
        return V(self.ap, Res(name))


class Prog:
    ENG = ("pe", "act", "dve", "pool", "sp")

    def __init__(self, nc):
        self.nc = nc
        self.es = contextlib.ExitStack()
        self.h = {"pe": nc.tensor, "act": nc.scalar, "dve": nc.vector, "pool": nc.gpsimd, "sp": nc.sync}
        self.sem = {}
        self.cnt = {}
        self.ops = {e: [] for e in self.ENG}
        self.seen = {e: {} for e in self.ENG}
        for e in self.ENG:
            self.sem[e] = self.es.enter_context(nc.semaphore("s_" + e))
            self.cnt[e] = 0
        self.dq = {}
        for q in ("sp", "act", "pool"):
            sems = []
            for i in range(4):
                k = "d_%s%d" % (q, i)
                self.sem[k] = self.es.enter_context(nc.semaphore(k))
                self.cnt[k] = 0
                sems.append(k)
            self.dq[q] = [sems, 0]
        self.n_tiles = 0
        self.banks = []
        self.bi = 0

    def init_banks(self, n=7):
        self.banks = [self.ps([128, 512], F32, name="bank%d" % i) for i in range(n)]

    def bk(self):
        b = self.banks[self.bi % len(self.banks)]
        self.bi += 1
        return b

    def sb(self, shape, dtype=F32, name=None):
        self.n_tiles += 1
        name = name or ("t%d" % self.n_tiles)
        t = self.es.enter_context(self.nc.sbuf_tensor(name, list(shape), dtype))
        return V(t[tuple(slice(None) for _ in shape)], Res(name))

    def ps(self, shape, dtype=F32, name=None):
        self.n_tiles += 1
        name = name or ("p%d" % self.n_tiles)
        t = self.es.enter_context(self.nc.psum_tensor(name, list(shape), dtype))
        return V(t[tuple(slice(None) for _ in shape)], Res(name))

    def dram(self, name, shape, dtype=F32, kind="ExternalInput"):
        t = self.nc.dram_tensor(name, list(shape), dtype, kind=kind)
        return V(t.ap(), Res(name))

    def _deps(self, eng, reads, writes, skip_same=False):
        need = {}

        def req(tok):
            if tok is None:
                return
            k, v = tok
            if skip_same and k == eng:
                return
            if need.get(k, 0) < v:
                need[k] = v

        for r in reads:
            req(r.w)
        for w in writes:
            req(w.w)
            for k, v in w.r.items():
                req((k, v))
        waits = []
        seen = self.seen[eng]
        for k, v in need.items():
            if seen.get(k, 0) < v:
                seen[k] = v
                waits.append((k, v))
        return waits

    def _commit(self, tok, reads, writes):
        k, v = tok
        for r in reads:
            if r.r.get(k, 0) < v:
                r.r[k] = v
        for w in writes:
            w.w = tok
            w.r = {}

    def op(self, eng, fn, reads, writes, skip_same=False):
        reads = [x.res for x in reads if x is not None and isinstance(x, V)]
        writes = [x.res for x in writes]
        waits = self._deps(eng, reads, writes, skip_same)
        self.cnt[eng] += 1
        tok = (eng, self.cnt[eng])
        self.ops[eng].append((waits, fn, (eng, 1)))
        self._commit(tok, reads, writes)

    def dma(self, q, out, in_, **kw):
        sems, i = self.dq[q]
        k = sems[i % len(sems)]
        self.dq[q][1] = i + 1
        reads = [in_.res]
        writes = [out.res]
        waits = self._deps(q, reads, writes)
        if self.cnt[k] > 0 and self.seen[q].get(k, 0) < self.cnt[k]:
            self.seen[q][k] = self.cnt[k]
            waits.append((k, self.cnt[k]))
        self.cnt[k] += 16
        tok = (k, self.cnt[k])
        o, i_ = out.ap, in_.ap
        self.ops[q].append((waits, lambda e: e.dma_start(out=o, in_=i_, **kw), (k, 16)))
        self._commit(tok, reads, writes)

    def dma_like(self, q, fn, reads, writes):
        sems, i = self.dq[q]
        k = sems[i % len(sems)]
        self.dq[q][1] = i + 1
        reads = [x.res for x in reads]; writes = [x.res for x in writes]
        waits = self._deps(q, reads, writes)
        if self.cnt[k] > 0 and self.seen[q].get(k, 0) < self.cnt[k]:
            self.seen[q][k] = self.cnt[k]
            waits.append((k, self.cnt[k]))
        self.cnt[k] += 16
        tok = (k, self.cnt[k])
        self.ops[q].append((waits, fn, (k, 16)))
        self._commit(tok, reads, writes)

    def mm(self, out, lhsT, rhs, start=True, stop=True, **kw):
        o, a, b = out.ap, lhsT.ap, rhs.ap
        self.op("pe", lambda e: e.matmul(o, a, b, start=start, stop=stop, **kw), [lhsT, rhs], [out], skip_same=True)

    def tr(self, out, in_, ident):
        o, a, b = out.ap, in_.ap, ident.ap
        self.op("pe", lambda e: e.transpose(o, a, b), [in_, ident], [out], skip_same=True)

    def act(self, out, in_, func, bias=None, scale=None, accum_out=None, eng="act"):
        o, a = out.ap, in_.ap
        kw = {}
        rd = [in_]
        if bias is not None:
            if isinstance(bias, V):
                rd.append(bias); kw["bias"] = bias.ap
            else:
                kw["bias"] = float(bias)
        if scale is not None:
            if isinstance(scale, V):
                rd.append(scale); kw["scale"] = scale.ap
            else:
                kw["scale"] = float(scale)
        wr = [out]
        if accum_out is not None:
            wr.append(accum_out); kw["accum_out"] = accum_out.ap
        self.op("act", lambda e: e.activation(o, a, func, **kw), rd, wr)

    def tt(self, out, in0, in1, op, eng="dve"):
        o, a, b = out.ap, in0.ap, in1.ap
        self.op(eng, lambda e: e.tensor_tensor(o, a, b, op), [in0, in1], [out])

    def ts(self, out, in0, s1, op0, s2=None, op1=None, accum_out=None, eng="dve"):
        o, a = out.ap, in0.ap
        rd = [in0]
        x1 = s1
        if isinstance(s1, V):
            rd.append(s1); x1 = s1.ap
        x2 = s2
        if isinstance(s2, V):
            rd.append(s2); x2 = s2.ap
        kw = {}
        if op1 is not None:
            kw["op1"] = op1
        wr = [out]
        if accum_out is not None:
            wr.append(accum_out); kw["accum_out"] = accum_out.ap
        self.op(eng, lambda e: e.tensor_scalar(o, a, x1, x2, op0, **kw), rd, wr)

    def stt(self, out, in0, scalar, in1, op0, op1, eng="dve"):
        o, a, b = out.ap, in0.ap, in1.ap
        rd = [in0, in1]
        s = scalar
        if isinstance(scalar, V):
            rd.append(scalar); s = scalar.ap
        self.op(eng, lambda e: e.scalar_tensor_tensor(o, a, s, b, op0, op1), rd, [out])

    def copy(self, out, in_, eng="dve"):
        o, a = out.ap, in_.ap
        if eng == "act":
            self.op("act", lambda e: e.copy(o, a), [in_], [out])
        else:
            self.op(eng, lambda e: e.tensor_copy(o, a), [in_], [out])

    def memset(self, out, val, eng="dve"):
        o = out.ap
        self.op(eng, lambda e: e.memset(o, val), [], [out])

    def reduce(self, out, in_, op=ALU.add, axis=AX.X, eng="dve"):
        o, a = out.ap, in_.ap
        self.op(eng, lambda e: e.tensor_reduce(o, a, axis, op), [in_], [out])

    def recip(self, out, in_):
        o, a = out.ap, in_.ap
        self.op("dve", lambda e: e.reciprocal(o, a), [in_], [out])

    def scan(self, out, d0, d1, initial, op0, op1):
        o, a, b = out.ap, d0.ap, d1.ap
        rd = [d0, d1]
        ini = initial
        if isinstance(initial, V):
            rd.append(initial); ini = initial.ap
        self.op("dve", lambda e: e.tensor_tensor_scan(o, a, b, ini, op0, op1), rd, [out])

    def emit(self, final_tokens):
        nc = self.nc
        fw = {}
        for r in final_tokens:
            if r.w is not None:
                k, v = r.w
                fw[k] = max(fw.get(k, 0), v)
        with nc.Block() as block:
            def mk(e):
                def body(engh):
                    for waits, fn, inc in self.ops[e]:
                        for k, v in waits:
                            engh.wait_ge(self.sem[k], v)
                        ins = fn(engh)
                        ins.then_inc(self.sem[inc[0]], inc[1])
                    if e == "sp":
                        for k, v in fw.items():
                            engh.wait_ge(self.sem[k], v)
                return body
            block.tensor(mk("pe"))
            block.scalar(mk("act"))
            block.vector(mk("dve"))
            block.gpsimd(mk("pool"))
            block.sync(mk("sp"))
        self.es.close()

import math

NT = 17
NCOL = 2180
RW0 = 2208
RCH = [(0, 128), (128, 256), (256, 384), (384, 512), (512, 640), (640, 768), (768, 896), (896, 1024), (1024, 1152),
       (1152, 1280), (1280, 1344), (1344, 1472)]
SUMW = 1412
SOFF_RET, SOFF_GLA, SOFF_RW = 0, 256, 256 + 388


def tile_col0(i):
    return 1 + 128 * i if i < 16 else 2051


def build_l1():
    nc = bass.Bass("TRN2", target_bir_lowering=False)
    P = Prog(nc)
    D = P.dram
    xs = D("xs", [NT, 128, 1024]); xh = D("xh", [34, 1024]); hfl = D("hfl", [128, 34]); cT = D("cT", [128, 16])
    wmod = D("wmod", [1024, 2048]); bmod = D("bmod", [2, 2048]); gpre = D("gpre", [128, 8]); win = D("win", [1024, 3680])
    convp = D("convp", [128, 36]); cm = D("cm", [128, 6, 128]); rope = D("rope", [NT, 128, 1024])
    retD = D("retD", [128, 8, 128]); retQD = D("retQD", [128, 8, 128]); retkd = D("retkd", [128, 8])
    gwa = D("gwa", [16, 384]); gba = D("gba", [1, 384])
    rw2 = D("rw2", [128, 384]); rw0 = D("rw0", [1, 768]); ra2 = D("ra2", [64, 384]); ra0 = D("ra0", [1, 384])
    rg2 = D("rg2", [128, 384]); rvec = D("rvec", [3, 384]); lvd = D("lv", [128, 7, 384])
    o_tok = D("o_tok", [NT, 128, 2432], kind="ExternalOutput")
    o_tokA, o_tokB = o_tok, o_tok
    o_rt = D("o_rt", [NT, 64, 28, 128], kind="ExternalOutput")
    o_sum = D("o_sum", [NT, 64, 2 * SUMW], kind="ExternalOutput")
    P.init_banks(6)
    pO = P.ps([128, 512], F32, name="pO")
    ptb = P.ps([128, 8, 128], BF16, name="ptb")

    cmt = P.sb([128, 6, 128], name="cmt"); P.dma("sp", cmt, cm)
    ident, ones = cmt[:, 0, :], cmt[:, 5, :]
    TRI = [cmt[:, 1, :], cmt[:, 2, :]]
    STR = [cmt[:, 3, :], cmt[:, 4, :]]
    identb = P.sb([128, 128], BF16, name="identb"); P.copy(identb, ident)
    MASK1 = P.sb([128, 2, 256], name="mask1")
    for d in range(2):
        P.copy(MASK1[:, d, 0:128], STR[d], eng="pool"); P.copy(MASK1[:, d, 128:256], TRI[d], eng="pool")
    retDt = P.sb([128, 8, 128], name="retDt"); P.dma("act", retDt, retD)
    retQDt = P.sb([128, 8, 128], name="retQDt"); P.dma("act", retQDt, retQD)
    retkdt = P.sb([128, 8], name="retkdt"); P.dma("act", retkdt, retkd)
    gwat = P.sb([16, 384], name="gwat"); P.dma("sp", gwat, gwa)
    gbat = P.sb([1, 384], name="gbat"); P.dma("sp", gbat, gba)
    rw2t = P.sb([128, 384], name="rw2t"); P.dma("sp", rw2t, rw2)
    rw0t = P.sb([1, 768], name="rw0t"); P.dma("sp", rw0t, rw0)
    ra2t = P.sb([64, 384], name="ra2t"); P.dma("sp", ra2t, ra2)
    ra0t = P.sb([1, 384], name="ra0t"); P.dma("sp", ra0t, ra0)
    rg2t = P.sb([128, 384], name="rg2t"); P.dma("sp", rg2t, rg2)
    rvt = P.sb([128, 3, 384], name="rvt")
    for j in range(3):
        P.dma("act", rvt[:, j, :], V(rvec.ap[j].partition_broadcast(128), rvec.res))
    cvt = P.sb([128, 36], name="cvt"); P.dma("act", cvt, convp)
    hflt = P.sb([128, 34], name="hflt"); P.dma("act", hflt, hfl)
    gpret = P.sb([128, 8], name="gpret"); P.dma("act", gpret, gpre)
    cTt = P.sb([128, 16], name="cTt"); P.dma("sp", cTt, cT)

    scs = P.sb([128, 16], name="scs")
    P.act(scs, cTt, AF.Silu)
    scv = scs.rr("p (k s) -> p k s", s=2)
    u = P.sb([128, 2208], name="u")
    modrow = u[0:2, 0:2048]
    bms = [P.sb([2, 128], name="bms%d" % i) for i in range(2)]
    wst = [P.sb([128, 8, 128], name="wst%d" % i) for i in range(2)]
    wmr = V(wmod.ap.rearrange("(k p) n -> p k n", p=128), wmod.res)
    for cb in range(16):
        st = wst[cb % 2]
        P.dma("sp" if cb % 2 == 0 else "act", st, wmr[:, :, cb * 128:(cb + 1) * 128])
        pb = P.bk()
        for k in range(8):
            P.mm(pb[0:2, 0:128], scv[:, k, :], st[:, k, :], start=(k == 0), stop=(k == 7))
        P.dma("pool", bms[cb % 2], bmod[:, cb * 128:(cb + 1) * 128])
        P.tt(modrow[:, cb * 128:(cb + 1) * 128], pb[0:2, 0:128], bms[cb % 2], ALU.add)
    pm = P.bk()
    for w in range(2):
        for k in range(8):
            P.mm(pm[:, (w * 8 + k) * 2:(w * 8 + k) * 2 + 2], modrow[0:2, w * 1024 + k * 128: w * 1024 + (k + 1) * 128], ident[0:2, 0:2])
    modp = P.sb([128, 2, 8, 2], name="modp")
    P.copy(modp.rr("p a k s -> p (a k s)"), pm[:, 0:32])
    gs = P.sb([128, 8, 2], name="gs")
    P.ts(gs, modp[:, 1, :, :], 1.0, ALU.add)
    P.tt(gs, gs, gpret.rr("p (k o) -> p k o", o=1).bc([128, 8, 2]), ALU.mult)
    sh = modp[:, 0, :, :]

    Wb = P.sb([128, 8, 2208], BF16, name="Wb")
    winr = V(win.ap.rearrange("(k p) n -> p k n", p=128), win.res)

    def load_w(lo, hi):
        n = (hi - lo + 127) // 128
        for cb in range(n):
            c0, c1 = lo + cb * 128, min(hi, lo + cb * 128 + 128)
            st = wst[cb % 2]
            P.dma("sp" if cb % 2 == 0 else "act", st[:, :, 0:c1 - c0], winr[:, :, c0:c1])
            P.copy(Wb[:, :, c0 - lo:c1 - lo], st[:, :, 0:c1 - c0], eng=("pool" if cb % 2 == 0 else "dve"))

    xb = [P.sb([128, 1024], name="xb%d" % i) for i in range(2)]
    xn = P.sb([128, 1024], BF16, name="xn")
    ss = P.sb([128, 1], name="ss"); rstd = P.sb([128, 1], name="rstd")

    def norm_rows(xt, n):
        P.act(xn[0:n, :], xt, AF.Square, accum_out=ss[0:n, :])
        P.act(rstd[0:n, :], ss[0:n, :], AF.Sqrt, bias=1e-6, scale=1.0 / 1024)
        P.recip(rstd[0:n, :], rstd[0:n, :])
        P.ts(xn[0:n, :], xt, rstd[0:n, :], ALU.mult)

    xht = P.sb([34, 1024], name="xht"); P.dma("sp", xht, xh)
    norm_rows(xht, 34)
    for k in range(8):
        P.tr(ptb[:, k, 0:34], xn[0:34, k * 128:(k + 1) * 128], identb[0:34, 0:34])
    hh = P.sb([128, 8, 34], name="hh")
    for s, (j0, j1) in enumerate(((0, 32), (32, 34))):
        P.tt(hh[:, :, j0:j1], ptb[:, :, j0:j1], gs[:, :, s:s + 1].bc([128, 8, j1 - j0]), ALU.mult)
        P.tt(hh[:, :, j0:j1], hh[:, :, j0:j1], sh[:, :, s:s + 1].bc([128, 8, j1 - j0]), ALU.add)
    P.tt(hh, hh, hflt.rr("p (o j) -> p o j", o=1).bc([128, 8, 34]), ALU.mult)
    hTb = [P.sb([128, 8, 130], BF16, name="hT%d" % i) for i in range(2)]
    hcnt = [0]

    def make_hT(i):
        n = hcnt[0]; hcnt[0] += 1
        hT = hTb[n % 2]
        xt = xb[n % 2]
        P.dma("sp" if n % 2 == 0 else "act", xt, xs[i])
        norm_rows(xt, 128)
        for k in range(8):
            P.tr(ptb[:, k, :], xn[:, k * 128:(k + 1) * 128], identb)
        s = 0 if i < 16 else 1
        for k in range(8):
            P.act(hT[:, k, 1:129], ptb[:, k, :], AF.Identity, scale=gs[:, k, s:s + 1], bias=sh[:, k, s:s + 1])
        P.copy(hT[:, :, 0:1], hh[:, :, 2 * i:2 * i + 1], eng="pool")
        P.copy(hT[:, :, 129:130], hh[:, :, 2 * i + 1:2 * i + 2], eng="pool")
        return hT

    finals = []
    ropet = P.sb([128, 1024], name="ropet")
    otok = P.sb([128, 1280], name="otok")
    RTt = P.sb([64, 16, 128], name="RTt")
    SUMt = P.sb([64, 1536], name="SUMt")
    qkT = P.sb([128, 4, 128], name="qkT")
    PT = [P.sb([128, 128], name="PT%d" % i) for i in range(2)]
    lrT = P.sb([16, 2, 128], name="lrT")
    gT = [P.sb([48, 2, 128], name="gT%d" % i) for i in range(2)]
    ucT = P.sb([128, 9, 128], name="ucT"); ctmp = P.sb([128, 128], name="ctmp"); ctmp2 = P.sb([128, 128], name="ctmp2")
    lrw = P.sb([128, 128], name="lrw"); lra = P.sb([64, 128], name="lra"); lrg = P.sb([128, 128], name="lrg")
    r_t, k_t, v_t, a_t, kk = (u[:, j * 384:(j + 1) * 384] for j in range(5))
    km = P.sb([128, 384], name="km")
    beta = P.sb([128, 384], name="beta"); t384 = P.sb([128, 384], name="t384"); s6 = P.sb([128, 6], name="s6")
    ld = P.sb([128, 384], name="ld"); E1 = P.sb([128, 384], name="E1"); E2 = P.sb([128, 384], name="E2")
    E3 = P.sb([128, 384], name="E3"); Et = P.sb([128, 384], name="Et")
    gla_la, gE1, gE2, gEt, gqt, gkh, gKt = (x[:, 0:192] for x in (ld, E1, E2, Et, E3, km, beta))
    Q4 = P.sb([128, 4, 384], name="Q4")
    Q4f = Q4.rr("p a b -> p (a b)")
    qk, rtmp, kd = Q4f[:, 0:512], Q4f[:, 512:1024], Q4f[:, 1024:1280]
    Bt = P.sb([128, 384], name="Bt"); Kt = P.sb([128, 384], name="Kt")
    FT = P.sb([128, 3, 4, 128], name="FT")
    DEZ = P.sb([64, 6, 128], name="DEZ"); P.memset(DEZ, 0.0, eng="pool")
    NM = [P.sb([128, 256], name="NM%d" % i) for i in range(3)]
    BM = [P.sb([128, 256], name="BM%d" % i) for i in range(3)]
    NA = [[P.sb([128, 256], name="NA%d_%d" % (i, j)) for j in range(1)] for i in range(3)]
    NAs = [P.sb([128, 256], name="NAs%d" % i) for i in range(3)]
    WM = [[P.sb([128, 256], name="WM%d_%d" % (i, j)) for j in range(2)] for i in range(3)]
    P12 = [P.sb([128, 256], name="P12_%d" % i) for i in range(3)]
    lvt = P.sb([128, 7, 384], name="lvt"); P.dma("sp", lvt, lvd)
    II = P.sb([128, 256], name="II")
    P.copy(II[:, 0:128], ident, eng="pool"); P.copy(II[:, 128:256], ident, eng="pool")
    Z = [[P.sb([128, 128], name="Z%d_%d" % (i, j)) for j in range(2)] for i in range(3)]
    C1 = math.exp(-0.5)

    def dout(dst, src):
        d2 = dst.sub()
        P.dma("sp", d2, src)
        finals.append(d2.res)

    load_w(0, 2208)
    for i in range(NT):
        hT = make_hT(i)
        P.dma("act", ropet, rope[i])
        for (a, b) in ((0, 512), (512, 1024), (1024, 1536), (1536, 2048), (2048, 2208)):
            pb = P.bk()
            for k in range(8):
                P.mm(pb[:, 0:b - a], hT[:, k, 1:129], Wb[:, k, a:b], start=(k == 0), stop=(k == 7))
            if (a // 512) % 2 == 0:
                P.copy(u[:, a:b], pb[:, 0:b - a], eng="act")
            else:
                P.copy(u[:, a:b], pb[:, 0:b - a])
        uq = u[:, 0:512]
        uq4 = uq.rr("p (a b c) -> p a b c", b=2, c=16)
        sn4 = ropet[:, 512:1024].rr("p (a b c) -> p a b c", b=2, c=16)
        rt4 = rtmp.rr("p (a b c) -> p a b c", b=2, c=16)
        P.tt(rt4[:, :, 0, :], uq4[:, :, 1, :], sn4[:, :, 0, :], ALU.mult, eng="pool")
        P.tt(rt4[:, :, 1, :], uq4[:, :, 0, :], sn4[:, :, 1, :], ALU.mult, eng="pool")
        P.tt(qk, uq, ropet[:, 0:512], ALU.mult, eng="pool")
        P.tt(qk, qk, rtmp, ALU.add, eng="pool")
        pb = P.bk()
        for c in range(4):
            P.tr(pb[:, c * 128:(c + 1) * 128], qk[:, c * 128:(c + 1) * 128], ident)
        P.copy(qkT.rr("p a b -> p (a b)"), pb, eng="act")
        P.act(otok[:, 640:896], u[:, 768:1024], AF.Silu)
        for d in range(2):
            P.tt(kd.rr("p (h c) -> p h c", c=64), qk[:, 256:512].rr("p (h c) -> p h c", c=64),
                 retkdt[:, d * 4:d * 4 + 4].rr("p (h o) -> p h o", o=1).bc([128, 4, 64]), ALU.mult, eng="pool")
            po = pO
            for h in range(4):
                c, pbase = h // 2, (h % 2) * 64
                pS = P.bk()
                P.mm(pS[:, 0:128], qkT[pbase:pbase + 64, 2 + c, :], qkT[pbase:pbase + 64, c, :])
                pt = PT[h % 2]
                P.tt(pt, pS[:, 0:128], retDt[:, d * 4 + h, :], ALU.mult)
                P.mm(po[:, h * 64:(h + 1) * 64], pt, u[:, 512 + h * 64:512 + (h + 1) * 64])
                pr = P.bk()
                P.mm(pr[0:64, 0:128], qk[:, h * 64:(h + 1) * 64], retQDt[:, d * 4 + h, :])
                P.mm(pr[0:64, 128:192], kd[:, h * 64:(h + 1) * 64], u[:, 512 + h * 64:512 + (h + 1) * 64])
                P.copy(RTt[:, d * 8 + h, :], pr[0:64, 0:128], eng="act")
                P.copy(SUMt[:, d * 644 + h * 64: d * 644 + (h + 1) * 64], pr[0:64, 128:192], eng="act")
            if d == 0:
                P.copy(otok[:, 0:256], po[:, 0:256])
            else:
                P.tt(otok[:, 0:256], otok[:, 0:256], po[:, 0:256], ALU.add)

        pb = P.bk()
        for d in range(2):
            P.tr(pb[0:16, d * 128:(d + 1) * 128], u[:, 2176 + d * 16:2176 + (d + 1) * 16], ident)
        P.copy(lrT.rr("p a b -> p (a b)"), pb[0:16, 0:256], eng="act")
        P.act(otok[:, 896:1280], u[:, 1792:2176], AF.Silu)
        for d in range(2):
            pz = P.bk()
            P.mm(pz[:, 0:192], lrT[:, d, :], gwat[:, d * 192:(d + 1) * 192], start=True, stop=False)
            P.mm(pz[:, 0:192], ones[0:1, :], gbat[0:1, d * 192:(d + 1) * 192], start=False, stop=True)
            P.act(gE1, pz[:, 0:192], AF.Exp, scale=-1.0)
            P.act(gE2, gE1, AF.Ln, bias=1.0)
            P.ts(gla_la, gE2, -1.0 / 16.0, ALU.mult)
            pc = P.bk()
            P.mm(pc[:, 0:192], TRI[d], gla_la)
            P.mm(pc[:, 192:384], ones, gla_la)
            P.act(gE1, pc[:, 0:192], AF.Exp)
            P.act(gE2, pc[:, 0:192], AF.Exp, scale=-1.0)
            P.act(gEt, pc[:, 192:384], AF.Exp)
            P.stt(gqt, u[:, 1024:1216], 48.0 ** -0.5, gE1, ALU.mult, ALU.mult)
            P.tt(gkh, u[:, 1216:1408], gE2, ALU.mult, eng="pool")
            P.tt(gKt, gkh, gEt, ALU.mult, eng="pool")
            po = pO
            for h in range(4):
                hs = slice(h * 48, (h + 1) * 48)
                pq = P.bk()
                P.tr(pq[0:48, 0:128], gqt[:, hs], ident)
                P.tr(pq[0:48, 128:256], gkh[:, hs], ident)
                g2 = gT[h % 2]
                P.copy(g2.rr("p a b -> p (a b)"), pq[0:48, 0:256], eng="act")
                P.copy(RTt[0:48, d * 8 + 4 + h, :], pq[0:48, 0:128], eng="act")
                pS = P.bk()
                P.mm(pS[:, 0:128], g2[:, 1, :], g2[:, 0, :])
                pt = PT[h % 2]
                P.tt(pt, pS[:, 0:128], TRI[d], ALU.mult)
                P.mm(po[:, h * 96:(h + 1) * 96], pt, u[:, 1408 + h * 96:1408 + (h + 1) * 96])
                ph = P.bk()
                P.mm(ph[0:48, 0:96], gKt[:, hs], u[:, 1408 + h * 96:1408 + (h + 1) * 96])
                P.mm(ph[0:48, 96:97], gla_la[:, hs], ones[:, 0:1])
                so = d * 644 + 256 + h * 97
                P.copy(SUMt[0:48, so:so + 96], ph[0:48, 0:96])
                P.act(SUMt[0:48, so + 96:so + 97], ph[0:48, 96:97], AF.Exp)
            if d == 0:
                P.copy(otok[:, 256:640], po[:, 0:384])
            else:
                P.tt(otok[:, 256:640], otok[:, 256:640], po[:, 0:384], ALU.add)

        for (c0, c1), (s0, s1) in (((0, 640), (0, 640)), ((1024, 1664), (640, 1280))):
            dout(o_tok[i][:, c0:c1], otok[:, s0:s1])
        for d in range(2):
            dout(o_rt[i][:, d * 14:d * 14 + 8, :], RTt[:, d * 8:d * 8 + 8, :])
            dout(o_sum[i][:, d * SUMW:d * SUMW + 644], SUMt[:, d * 644:(d + 1) * 644])

    load_w(RW0, 3680)
    for i in range(NT):
        hT = make_hT(i)
        for j, (a, b) in enumerate(RCH):
            M = b - a
            pb = P.bk()
            for k in range(8):
                P.mm(pb[0:M, 0:130], Wb[:, k, a:b], hT[:, k, 0:130], start=(k == 0), stop=(k == 7))
            P.ts(ctmp[0:M, :], pb[0:M, 0:128], cvt[0:M, 3 * j:3 * j + 1], ALU.mult)
            P.stt(ctmp2[0:M, :], pb[0:M, 1:129], cvt[0:M, 3 * j + 1:3 * j + 2], ctmp[0:M, :], ALU.mult, ALU.add)
            if j < 9:
                dst = ucT[:, j, :]
            elif j == 9:
                dst = ctmp[0:M, :]
            elif j == 10:
                dst = lra
            else:
                dst = ctmp[0:M, :]
            P.stt(dst, pb[0:M, 2:130], cvt[0:M, 3 * j + 2:3 * j + 3], ctmp2[0:M, :], ALU.mult, ALU.add)
            if j == 9:
                P.act(lrw, ctmp, AF.Tanh)
            if j == 11:
                P.act(lrg, ctmp, AF.Sigmoid)
        for g3, dstt in enumerate((r_t, k_t, v_t)):
            pb = P.bk()
            for c in range(3):
                P.tr(pb[:, c * 128:(c + 1) * 128], ucT[:, g3 * 3 + c, :], ident)
            P.copy(dstt, pb[:, 0:384], eng=("act" if g3 == 1 else "dve"))

        pa = P.bk()
        P.mm(pa[:, 0:384], lra, ra2t, start=True, stop=False)
        P.mm(pa[:, 0:384], ones[0:1, :], ra0t, start=False, stop=True)
        P.act(a_t, pa[:, 0:384], AF.Sigmoid)
        pg = P.bk()
        P.mm(pg[:, 0:384], lrg, rg2t)
        P.copy(otok[:, 384:768], pg[:, 0:384], eng="act")
        P.tt(kk, k_t, rvt[:, 0, :], ALU.mult, eng="pool")
        P.tt(t384, kk, kk, ALU.mult, eng="pool")
        P.reduce(s6, t384.rr("p (h c) -> p h c", c=64))
        P.ts(s6, s6, 1e-24, ALU.max)
        P.act(s6, s6, AF.Sqrt)
        P.recip(s6, s6)
        P.tt(kk.rr("p (h c) -> p h c", c=64), kk.rr("p (h c) -> p h c", c=64),
             s6.rr("p (h o) -> p h o", o=1).bc([128, 6, 64]), ALU.mult)
        P.stt(t384, a_t, -1.0, rvt[:, 1, :], ALU.add, ALU.mult)
        P.stt(km, t384, 1.0, k_t, ALU.add, ALU.mult)
        P.tt(beta, kk, a_t, ALU.mult, eng="pool")
        P.tt(t384, r_t, km, ALU.mult, eng="pool")
        P.tt(t384, t384, rvt[:, 2, :], ALU.mult, eng="pool")
        P.reduce(s6, t384.rr("p (h c) -> p h c", c=64))
        P.tt(otok[:, 768:1152].rr("p (h c) -> p h c", c=64), v_t.rr("p (h c) -> p h c", c=64),
             s6.rr("p (h o) -> p h o", o=1).bc([128, 6, 64]), ALU.mult)
        for d in range(2):
            pz = P.bk()
            P.mm(pz[:, 0:384], lrw[d * 64:(d + 1) * 64, :], rw2t[d * 64:(d + 1) * 64, :], start=True, stop=False)
            P.mm(pz[:, 0:384], ones[0:1, :], rw0t[0:1, d * 384:(d + 1) * 384], start=False, stop=True)
            P.act(ld, pz[:, 0:384], AF.Sigmoid)
            P.ts(ld, ld, -C1, ALU.mult)
            pc = P.bk(); ptot = P.bk()
            P.mm(pc[:, 0:384], TRI[d], ld)
            P.mm(ptot[:, 0:384], ones, ld)
            P.act(E1, pc[:, 0:384], AF.Exp)
            P.act(E2, pc[:, 0:384], AF.Exp, scale=-1.0)
            P.tt(E3, pc[:, 0:384], ld, ALU.subtract)
            P.act(E3, E3, AF.Exp)
            P.act(Et, ptot[:, 0:384], AF.Exp)
            P.tt(Q4[:, 0, :], beta, E2, ALU.mult)
            P.tt(Q4[:, 1, :], km, E2, ALU.mult, eng="pool")
            P.stt(Q4[:, 2, :], kk, -1.0, E3, ALU.mult, ALU.mult)
            P.tt(Q4[:, 3, :], r_t, E1, ALU.mult, eng="pool")
            P.tt(Bt, Q4[:, 0, :], Et, ALU.mult)
            P.tt(Kt, Q4[:, 1, :], Et, ALU.mult, eng="pool")
            P.tt(DEZ[:, :, 0:64], ident[0:64, 0:64].rr("p (o c) -> p o c", o=1).bc([64, 6, 64]),
                 Et[0:64, :].rr("p (h c) -> p h c", c=64), ALU.mult)
            for c in range(3):
                pb = P.bk()
                for q in range(4):
                    P.tr(pb[:, q * 128:(q + 1) * 128], Q4[:, q, c * 128:(c + 1) * 128], ident)
                P.copy(FT[:, c, :, :].rr("p a b -> p (a b)"), pb, eng=("act" if c % 2 == 0 else "dve"))
            po = pO
            mo = 0 if d == 0 else 128
            for grp in ((0, 1, 2), (3, 4, 5)):
                for h in grp:
                    c, pbase = h // 2, (h % 2) * 64
                    hs = slice(h * 64, (h + 1) * 64)
                    bhT = FT[pbase:pbase + 64, c, 0, :]; khT = FT[pbase:pbase + 64, c, 1, :]
                    alT = FT[pbase:pbase + 64, c, 2, :]
                    alrT = FT[pbase:pbase + 64, c, 2:4, :].rr("p a b -> p (a b)")
                    nm, bm, na, zz = NM[h % 3], BM[h % 3], NA[h % 3], Z[h % 3]
                    p1 = P.bk(); P.mm(p1[:, 0:256], bhT, alrT)
                    P.tt(nm, p1[:, 0:256], MASK1[:, d, :], ALU.mult)
                    p2 = P.bk(); P.mm(p2[:, 0:256], khT, alrT)
                    P.tt(bm, p2[:, 0:256], MASK1[:, d, :], ALU.mult)
                for h in grp:
                    c, pbase = h // 2, (h % 2) * 64
                    hs = slice(h * 64, (h + 1) * 64)
                    bhT = FT[pbase:pbase + 64, c, 0, :]; alT = FT[pbase:pbase + 64, c, 2, :]
                    nm, bm, na, zz = NM[h % 3], BM[h % 3], NA[h % 3], Z[h % 3]
                    p3 = P.bk(); P.mm(p3[:, 0:128], alT, bhT)
                    P.copy(na[0][:, 0:128], nm[:, 0:128], eng="pool")
                    P.tt(na[0][:, 128:256], p3[:, 0:128], STR[1 - d], ALU.mult)
                    p4 = P.bk(); P.mm(p4[:, 0:64], bm[:, 0:128], v_t[:, hs])
                    P.copy(zz[0][:, 0:64], Q4[:, 2, hs], eng="pool")
                    P.copy(zz[0][:, 64:128], p4[:, 0:64], eng="act")
                for h in grp:
                    P.tt(NAs[h % 3], NA[h % 3][0], lvt[:, 0, mo:mo + 256], ALU.mult, eng="pool")
                    P.tt(WM[h % 3][0], NAs[h % 3], II, ALU.add, eng="pool")
                cur = 0
                for lv_ in range(1, 7):
                    last = (lv_ == 6)
                    pPs = {}
                    for h in grp:
                        nas, wm = NAs[h % 3], WM[h % 3]
                        P.tt(nas, NA[h % 3][0], lvt[:, lv_, mo:mo + 256], ALU.mult, eng="pool")
                    for h in grp:
                        nas, wm = NAs[h % 3], WM[h % 3]
                        pP = P.bk(); pPs[h] = pP
                        P.mm(pP[:, 0:128], nas[:, 128:256], wm[cur][:, 0:128])
                        if not last:
                            P.mm(pP[:, 128:256], nas[:, 0:128], wm[cur][:, 128:256])
                    for h in grp:
                        if not last:
                            P.copy(P12[h % 3], pPs[h][:, 0:256], eng="act")
                        else:
                            P.copy(P12[h % 3][:, 0:128], pPs[h][:, 0:128], eng="act")
                    pUs = {}
                    for h in grp:
                        wm, p12 = WM[h % 3], P12[h % 3]
                        pU = P.bk(); pUs[h] = pU
                        P.mm(pU[:, 0:128], wm[cur][:, 128:256], p12[:, 0:128])
                        if not last:
                            P.mm(pU[:, 128:256], wm[cur][:, 0:128], p12[:, 128:256])
                    for h in grp:
                        wm = WM[h % 3]
                        if not last:
                            P.tt(wm[1 - cur], pUs[h][:, 0:256], wm[cur], ALU.add)
                        else:
                            P.tt(wm[1 - cur][:, 0:128], pUs[h][:, 0:128], wm[cur][:, 0:128], ALU.add)
                    cur = 1 - cur
                pzs = {}
                for h in grp:
                    pz2 = P.bk(); pzs[h] = pz2
                    P.mm(pz2[:, 0:128], WM[h % 3][cur][:, 0:128], Z[h % 3][0])
                for h in grp:
                    P.copy(Z[h % 3][1], pzs[h][:, 0:128], eng="act")
                for h in grp:
                    hs = slice(h * 64, (h + 1) * 64)
                    nm, bm = NM[h % 3], BM[h % 3]
                    zf = Z[h % 3][1]
                    X, Y = zf[:, 0:64], zf[:, 64:128]
                    mbT, mkT = nm[:, 128:256], bm[:, 128:256]
                    pr = P.bk()
                    P.mm(pr[0:64, 0:128], Q4[:, 3, hs], ident, start=True, stop=False)
                    P.mm(pr[0:64, 0:128], X, mbT, start=False, stop=True)
                    P.copy(RTt[:, d * 6 + h, :], pr[0:64, 0:128], eng="act")
                    P.mm(po[:, hs], mbT, Y, start=True, stop=False)
                    P.mm(po[:, hs], mkT, v_t[:, hs], start=False, stop=True)
                    pgh = P.bk()
                    P.mm(pgh[0:64, 0:64], X, Bt[:, hs])
                    P.mm(pgh[0:64, 64:128], Bt[:, hs], Y, start=True, stop=False)
                    P.mm(pgh[0:64, 64:128], Kt[:, hs], v_t[:, hs], start=False, stop=True)
                    so = d * 768 + h * 128
                    P.tt(SUMt[:, so:so + 128], pgh[0:64, 0:128], DEZ[:, h, :], ALU.add)
            if d == 0:
                P.copy(otok[:, 0:384], po[:, 0:384])
            else:
                P.tt(otok[:, 0:384], otok[:, 0:384], po[:, 0:384], ALU.add)
        for (c0, c1), (s0, s1) in (((640, 1024), (0, 384)), ((1664, 2432), (384, 1152))):
            dout(o_tok[i][:, c0:c1], otok[:, s0:s1])
        for d in range(2):
            dout(o_rt[i][:, d * 14 + 8:d * 14 + 14, :], RTt[:, d * 6:d * 6 + 6, :])
            dout(o_sum[i][:, d * SUMW + 644:(d + 1) * SUMW], SUMt[:, d * 768:(d + 1) * 768])
    P.emit(finals)
    return nc


NPRE = 50


def mod_setup(P, cT):
    cTt = P.sb([128, 16], name="cTt"); P.dma("sp", cTt, cT)
    scs = P.sb([128, 16], name="scs")
    P.act(scs, cTt, AF.Silu)
    scv = scs.rr("p (k s) -> p k s", s=2)
    modrow = P.sb([2, 1024], name="modrow")
    bms = [P.sb([2, 128], name="bms%d" % i) for i in range(2)]
    return scv, bms, modrow


def mod_block(P, ms, wmod, bmod, c0, wst):
    scv, bms, modrow = ms
    wmr = V(wmod.ap.rearrange("(k p) n -> p k n", p=128), wmod.res)
    for cb in range(8):
        st = wst[cb % 2]
        cc = c0 + cb * 128
        P.dma("sp" if cb % 2 == 0 else "act", st, wmr[:, :, cc:cc + 128])
        pb = P.bk()
        for k in range(8):
            P.mm(pb[0:2, 0:128], scv[:, k, :], st[:, k, :], start=(k == 0), stop=(k == 7))
        P.dma("pool", bms[cb % 2], bmod[:, cc:cc + 128])
        P.tt(modrow[:, cb * 128:(cb + 1) * 128], pb[0:2, 0:128], bms[cb % 2], ALU.add)
    return modrow


def load_w_bf16(P, dst, src, kch, ncols, wst):
    sr = V(src.ap.rearrange("(k p) n -> p k n", p=128), src.res)
    n = 0
    for k0 in range(0, kch, 8):
        k1 = min(kch, k0 + 8)
        for c0 in range(0, ncols, 128):
            c1 = min(ncols, c0 + 128)
            st = wst[n % 2]
            P.dma("sp" if n % 2 == 0 else "act", st[:, 0:k1 - k0, 0:c1 - c0], sr[:, k0:k1, c0:c1])
            P.copy(dst[:, k0:k1, c0:c1], st[:, 0:k1 - k0, 0:c1 - c0], eng=("pool" if n % 2 == 0 else "dve"))
            n += 1


def rms_finish(P, py, xt, gg, xo, tmp, ss2, rstd, junk):
    for cb in range(2):
        P.act(junk[:, cb * 512:(cb + 1) * 512], py[cb], AF.Square, accum_out=ss2[:, cb:cb + 1])
    P.tt(rstd, ss2[:, 0:1], ss2[:, 1:2], ALU.add)
    P.act(rstd, rstd, AF.Sqrt, bias=1e-6, scale=1.0 / 1024)
    P.recip(rstd, rstd)
    for cb in range(2):
        P.stt(tmp[:, cb * 512:(cb + 1) * 512], py[cb], rstd, gg[:, cb * 512:(cb + 1) * 512], ALU.mult, ALU.mult)
    P.tt(xo, tmp, xt, ALU.add, eng="pool")


def build_l2(cdec):
    nc = bass.Bass("TRN2", target_bir_lowering=False)
    P = Prog(nc)
    D = P.dram
    xs = D("xs", [NT, 128, 1024]); o_tok = D("o_tok", [NT, 128, 2432]); o_rt = D("o_rt", [NT, 64, 28, 128])
    o_sum = D("o_sum", [NT, 64, 2 * SUMW]); pre = D("pre", [2, NPRE + 1, 64, SUMW])
    cT = D("cT", [128, 16]); wmod = D("wmod", [1024, 1024]); bmod = D("bmod", [2, 1024])
    vecs = D("vecs", [3, 1024]); wout = D("wout", [1024, 1024]); cm = D("cm", [128, 6, 128]); sel = D("sel", [2, 256])
    cdtd = D("cdt", [64, 512])
    xmid = D("xmid", [NT, 128, 1024], kind="ExternalOutput")
    scr = D("scr", [NT, 128, 1024], kind="ExternalOutput")
    P.init_banks(4)
    pA = P.ps([128, 512], F32, name="pA"); pB = P.ps([128, 512], F32, name="pB"); pC = P.ps([128, 512], F32, name="pC")
    ptb = P.ps([128, 8, 128], BF16, name="ptb")
    finals = []

    cmt = P.sb([128, 6, 128], name="cmt"); P.dma("sp", cmt, cm)
    ident = cmt[:, 0, :]
    identb = P.sb([128, 128], BF16, name="identb"); P.copy(identb, ident)
    selt = P.sb([2, 256], name="selt"); P.dma("sp", selt, sel)
    cdt = P.sb([64, 512], name="cdtt"); P.dma("sp", cdt, cdtd)
    vb = P.sb([128, 3, 1024], name="vb")
    for j in range(3):
        P.dma("act", vb[:, j, :], V(vecs.ap[j].partition_broadcast(128), vecs.res))
    wst = [P.sb([128, 8, 128], name="wst%d" % i) for i in range(2)]
    ms = mod_setup(P, cT)
    modrow = mod_block(P, ms, wmod, bmod, 0, wst)
    gg = P.sb([128, 2, 1024], name="gg")
    for s in range(2):
        for cb in range(2):
            pb = P.bk()
            P.mm(pb, selt[0:2, s * 128:(s + 1) * 128], modrow[0:2, cb * 512:(cb + 1) * 512])
            P.tt(gg[:, s, cb * 512:(cb + 1) * 512], pb, vb[:, 0, cb * 512:(cb + 1) * 512], ALU.mult)
    Wo = P.sb([128, 8, 1024], BF16, name="Wo")
    load_w_bf16(P, Wo, wout, 8, 1024, wst)

    STr = P.sb([64, 4, 64], name="STr"); STg = P.sb([64, 4, 96], name="STg"); STw = P.sb([64, 6, 64], name="STw")
    SM = [P.sb([64, SUMW], name="SM%d" % i) for i in range(3)]
    RTs = [P.sb([64, 14, 128], name="RTs%d" % i) for i in range(2)]
    accb = [P.sb([128, 1024], name="accb%d" % i) for i in range(2)]
    gts = [P.sb([128, 1408], name="gts%d" % i) for i in range(2)]
    xb = [P.sb([128, 1024], name="xb%d" % i) for i in range(2)]
    cen = P.sb([128, 1024], name="cen"); sq = P.sb([128, 1024], name="sq")
    ycb = P.sb([128, 1024], BF16, name="ycb"); yT = P.sb([128, 8, 128], BF16, name="yT")
    tmp = P.sb([128, 1024], name="tmp"); xo = [P.sb([128, 1024], name="xo%d" % i) for i in range(2)]
    s14 = P.sb([128, 14], name="s14"); r14 = P.sb([128, 14], name="r14")
    ss2 = P.sb([128, 2], name="ss2"); rstd = P.sb([128, 1], name="rstd")
    scrv = [scr[i].sub() for i in range(NT)]
    cnt = {"sm": 0, "rt": 0, "acc": 0, "fin": 0}

    def reset_states():
        P.memset(STr, 0.0, eng="pool"); P.memset(STg, 0.0, eng="pool"); P.memset(STw, 0.0, eng="pool")

    def update(sm, d):
        P.tt(STr, STr, cdt[:, d * 256:(d + 1) * 256].rr("p (h c) -> p h c", c=64), ALU.mult, eng="pool")
        P.tt(STr, STr, sm[:, 0:256].rr("p (h c) -> p h c", c=64), ALU.add, eng="pool")
        for h in range(4):
            o = 256 + h * 97
            P.stt(STg[0:48, h, :], STg[0:48, h, :], sm[0:48, o + 96:o + 97], sm[0:48, o:o + 96], ALU.mult, ALU.add)
        pw = P.bk()
        for h in range(6):
            o = 644 + h * 128
            P.mm(pw[0:64, h * 64:(h + 1) * 64], sm[:, o:o + 64], STw[:, h, :])
        P.tt(STw, pw[0:64, 0:384].rr("p (h c) -> p h c", c=64),
             sm[:, 644:644 + 768].rr("p (h c) -> p h c", c=128)[:, :, 64:128], ALU.add)

    def load_sm(src):
        sm = SM[cnt["sm"] % 3]; cnt["sm"] += 1
        P.dma("act" if cnt["sm"] % 2 == 0 else "sp", sm, src)
        return sm

    def outputs(i, d, acc):
        rt = RTs[cnt["rt"] % 2]; cnt["rt"] += 1
        P.dma("pool", rt, o_rt[i][:, d * 14:(d + 1) * 14, :])
        for h in range(4):
            P.mm(pA[:, h * 64:(h + 1) * 64], rt[0:64, h, :], STr[:, h, :])
        for h in range(4):
            P.mm(pB[:, h * 96:(h + 1) * 96], rt[0:48, 4 + h, :], STg[0:48, h, :])
        for h in range(6):
            P.mm(pC[:, h * 64:(h + 1) * 64], rt[0:64, 8 + h, :], STw[:, h, :])
        P.tt(acc[:, 0:256], acc[:, 0:256], pA[:, 0:256], ALU.add)
        P.tt(acc[:, 256:640], acc[:, 256:640], pB[:, 0:384], ALU.add)
        P.tt(acc[:, 640:1024], acc[:, 640:1024], pC[:, 0:384], ALU.add)

    def headnorm(o3, c3, H, dh, center, eps, off):
        sv = s14[:, off:off + H]; rv = r14[:, off:off + H]
        src = o3
        if center:
            P.reduce(sv, o3)
            P.ts(sv, sv, -1.0 / dh, ALU.mult)
            P.tt(c3, o3, sv.rr("p (h o) -> p h o", o=1).bc([128, H, dh]), ALU.add)
            src = c3
        q3 = sq[:, 0:H * dh].rr("p (h c) -> p h c", c=dh)
        P.tt(q3, src, src, ALU.mult, eng="pool")
        P.reduce(rv, q3)
        P.act(rv, rv, AF.Sqrt, bias=eps, scale=1.0 / dh)
        P.recip(rv, rv)
        P.tt(c3, src, rv.rr("p (h o) -> p h o", o=1).bc([128, H, dh]), ALU.mult)

    def finish(i, acc):
        n = cnt["fin"]; cnt["fin"] += 1
        g = gts[n % 2]; xt = xb[n % 2]
        P.dma("sp", g, o_tok[i][:, 1024:2432])
        P.dma("act", xt, xs[i])
        headnorm(acc[:, 0:256].rr("p (h c) -> p h c", c=64), cen[:, 0:256].rr("p (h c) -> p h c", c=64), 4, 64, True, 1e-5, 0)
        headnorm(acc[:, 256:640].rr("p (h c) -> p h c", c=96), cen[:, 256:640].rr("p (h c) -> p h c", c=96), 4, 96, False, 1e-5, 4)
        headnorm(acc[:, 640:1024].rr("p (h c) -> p h c", c=64), cen[:, 640:1024].rr("p (h c) -> p h c", c=64), 6, 64, True, 64e-5, 8)
        P.tt(cen, cen, vb[:, 1, :], ALU.mult)
        P.tt(cen[:, 640:1024], cen[:, 640:1024], vb[:, 2, 640:1024], ALU.add, eng="pool")
        P.tt(cen[:, 640:1024], cen[:, 640:1024], g[:, 1024:1408], ALU.add, eng="pool")
        P.tt(ycb, cen, g[:, 0:1024], ALU.mult)
        for k in range(8):
            P.tr(ptb[:, k, :], ycb[:, k * 128:(k + 1) * 128], identb)
        P.copy(yT.rr("p a b -> p (a b)"), ptb.rr("p a b -> p (a b)"), eng="act")
        py = [pA, pB]
        for cb in range(2):
            for k in range(8):
                P.mm(py[cb], yT[:, k, :], Wo[:, k, cb * 512:(cb + 1) * 512], start=(k == 0), stop=(k == 7))
        s = 0 if i < 16 else 1
        x_o = xo[n % 2]
        rms_finish(P, py, xt, gg[:, s, :], x_o, tmp, ss2, rstd, sq)
        dd = xmid[i].sub()
        P.dma("sp", dd, x_o)
        finals.append(dd.res)

    for d in range(2):
        order = list(range(16)) if d == 0 else list(range(15, -1, -1))
        for seg in ("lat", "ctx"):
            reset_states()
            if seg == "lat":
                for s_ in range(NPRE):
                    update(load_sm(pre[d, s_]), d)
                tiles = order
            else:
                update(load_sm(pre[d, NPRE]), d)
                tiles = [16]
            for n_, i in enumerate(tiles):
                acc = accb[cnt["acc"] % 2]; cnt["acc"] += 1
                if d == 0:
                    P.dma("sp", acc, o_tok[i][:, 0:1024])
                else:
                    P.dma("sp", acc, scrv[i])
                outputs(i, d, acc)
                if d == 0:
                    P.dma("act", scrv[i], acc)
                else:
                    finish(i, acc)
                if n_ < len(tiles) - 1:
                    update(load_sm(o_sum[i][:, d * SUMW:(d + 1) * SUMW]), d)
    finals.extend(v.res for v in scrv)
    P.emit(finals)
    return nc


def build_l3():
    nc = bass.Bass("TRN2", target_bir_lowering=False)
    P = Prog(nc)
    D = P.dram
    xs = D("xs", [NT, 128, 1024]); cT = D("cT", [128, 16]); wmod = D("wmod", [1024, 3072]); bmod = D("bmod", [2, 3072])
    gpre = D("gpre", [128, 8]); vecs = D("vecs", [1, 1024]); cm = D("cm", [128, 6, 128]); sel = D("sel", [2, 256])
    wg = D("wg", [1024, 2816]); wu = D("wu", [1024, 2816]); wd = D("wd", [2816, 1024])
    xout = D("xout", [NT, 128, 1024], kind="ExternalOutput")
    P.init_banks(4)
    pA = P.ps([128, 512], F32, name="pA"); pB = P.ps([128, 512], F32, name="pB")
    ptb = P.ps([128, 8, 128], BF16, name="ptb")
    finals = []
    cmt = P.sb([128, 6, 128], name="cmt"); P.dma("sp", cmt, cm)
    ident = cmt[:, 0, :]
    identb = P.sb([128, 128], BF16, name="identb"); P.copy(identb, ident)
    selt = P.sb([2, 256], name="selt"); P.dma("sp", selt, sel)
    gpret = P.sb([128, 8], name="gpret"); P.dma("act", gpret, gpre)
    vb = P.sb([128, 1024], name="vb"); P.dma("act", vb, V(vecs.ap[0].partition_broadcast(128), vecs.res))
    wst = [P.sb([128, 8, 128], name="wst%d" % i) for i in range(2)]
    ms = mod_setup(P, cT)
    pm = P.ps([128, 32], F32, name="pm")
    for w in range(2):
        modrow = mod_block(P, ms, wmod, bmod, w * 1024, wst)
        for k in range(8):
            P.mm(pm[:, (w * 8 + k) * 2:(w * 8 + k) * 2 + 2], modrow[0:2, k * 128:(k + 1) * 128], ident[0:2, 0:2])
    modp = P.sb([128, 2, 8, 2], name="modp")
    P.copy(modp.rr("p a k s -> p (a k s)"), pm[:, 0:32])
    gs = P.sb([128, 8, 2], name="gs")
    P.ts(gs, modp[:, 1, :, :], 1.0, ALU.add)
    P.tt(gs, gs, gpret.rr("p (k o) -> p k o", o=1).bc([128, 8, 2]), ALU.mult)
    sh = modp[:, 0, :, :]
    modrow = mod_block(P, ms, wmod, bmod, 2048, wst)
    gg = P.sb([128, 2, 1024], name="gg")
    for s in range(2):
        for cb in range(2):
            pb = P.bk()
            P.mm(pb, selt[0:2, s * 128:(s + 1) * 128], modrow[0:2, cb * 512:(cb + 1) * 512])
            P.tt(gg[:, s, cb * 512:(cb + 1) * 512], pb, vb[:, cb * 512:(cb + 1) * 512], ALU.mult)
    Wg = P.sb([128, 8, 2816], BF16, name="Wg"); Wu = P.sb([128, 8, 2816], BF16, name="Wu"); Wd = P.sb([128, 22, 1024], BF16, name="Wd")
    load_w_bf16(P, Wg, wg, 8, 2816, wst)
    load_w_bf16(P, Wu, wu, 8, 2816, wst)
    load_w_bf16(P, Wd, wd, 22, 1024, wst)

    xb = [P.sb([128, 1024], name="xb%d" % i) for i in range(3)]
    xn = P.sb([128, 1024], BF16, name="xn")
    ss = P.sb([128, 1], name="ss"); rstd = P.sb([128, 1], name="rstd"); ss2 = P.sb([128, 2], name="ss2")
    hT = P.sb([128, 8, 256], BF16, name="hT")
    hid = P.sb([128, 22, 256], BF16, name="hid")
    sg = [P.sb([128, 256], name="sg%d" % i) for i in range(2)]
    tmp = P.sb([128, 1024], name="tmp"); junk = tmp
    xo = [P.sb([128, 1024], name="xo%d" % i) for i in range(2)]
    groups = [(2 * g, 2 * g + 1) for g in range(8)] + [(16,)]
    nx = 0
    for grp in groups:
        T = len(grp) * 128
        xts = []
        for j, i in enumerate(grp):
            xt = xb[nx % 3]; nx += 1
            xts.append(xt)
            P.dma("sp" if j == 0 else "act", xt, xs[i])
            P.act(xn, xt, AF.Square, accum_out=ss)
            P.act(rstd, ss, AF.Sqrt, bias=1e-6, scale=1.0 / 1024)
            P.recip(rstd, rstd)
            P.ts(xn, xt, rstd, ALU.mult)
            for k in range(8):
                P.tr(ptb[:, k, :], xn[:, k * 128:(k + 1) * 128], identb)
            s = 0 if i < 16 else 1
            for k in range(8):
                P.act(hT[:, k, j * 128:(j + 1) * 128], ptb[:, k, :], AF.Identity, scale=gs[:, k, s:s + 1], bias=sh[:, k, s:s + 1])
        for hc in range(22):
            pg = P.bk(); pu = P.bk()
            for k in range(8):
                P.mm(pg[:, 0:T], Wg[:, k, hc * 128:(hc + 1) * 128], hT[:, k, 0:T], start=(k == 0), stop=(k == 7))
            for k in range(8):
                P.mm(pu[:, 0:T], Wu[:, k, hc * 128:(hc + 1) * 128], hT[:, k, 0:T], start=(k == 0), stop=(k == 7))
            sgt = sg[hc % 2]
            P.act(sgt[:, 0:T], pg[:, 0:T], AF.Silu)
            P.tt(hid[:, hc, 0:T], sgt[:, 0:T], pu[:, 0:T], ALU.mult)
        for j, i in enumerate(grp):
            py = [pA, pB]
            for cb in range(2):
                for hc in range(22):
                    P.mm(py[cb], hid[:, hc, j * 128:(j + 1) * 128], Wd[:, hc, cb * 512:(cb + 1) * 512], start=(hc == 0), stop=(hc == 21))
            s = 0 if i < 16 else 1
            x_o = xo[i % 2]
            rms_finish(P, py, xts[j], gg[:, s, :], x_o, tmp, ss2, rstd, junk)
            dd = xout[i].sub()
            P.dma("sp", dd, x_o)
            finals.append(dd.res)
    P.emit(finals)
    return nc


f32 = np.float32


def consts():
    C = 128
    cm = np.zeros((128, 6, 128), f32)
    cm[:, 0] = np.eye(C)
    cm[:, 1] = np.triu(np.ones((C, C)))
    cm[:, 2] = np.tril(np.ones((C, C)))
    cm[:, 3] = np.triu(np.ones((C, C)), 1)
    cm[:, 4] = np.tril(np.ones((C, C)), -1)
    cm[:, 5] = 1.0
    gam = 1.0 - np.exp2(-5.0 - np.arange(4))
    j = np.arange(C)[:, None].astype(np.float64); t = np.arange(C)[None, :].astype(np.float64)
    retD = np.zeros((128, 8, 128), np.float64); retQD = np.zeros((128, 8, 128), np.float64); retkd = np.zeros((128, 8), np.float64)
    cdec = np.zeros((2, 4))
    for d in range(2):
        for h in range(4):
            g = gam[h] if d == 0 else gam[3 - h]
            if d == 0:
                retD[:, d * 4 + h, :] = np.where(t >= j, 0.125 * g ** np.maximum(t - j, 0), 0.0)
                retQD[:, d * 4 + h, :] = np.eye(C) * (g ** (np.arange(C) + 1.0))[None, :]
                retkd[:, d * 4 + h] = 0.125 * g ** (127.0 - np.arange(C))
            else:
                retD[:, d * 4 + h, :] = np.where(j >= t, 0.125 * g ** np.maximum(j - t, 0), 0.0)
                retQD[:, d * 4 + h, :] = np.eye(C) * (g ** (128.0 - np.arange(C)))[None, :]
                retkd[:, d * 4 + h] = 0.125 * g ** (np.arange(C) * 1.0)
            cdec[d, h] = g ** 128.0
    jj = np.arange(C)[:, None]; tt_ = np.arange(C)[None, :]
    lv = np.zeros((128, 7, 384), f32)
    for l in range(7):
        s = 1 << l
        U = ((jj // (2 * s) == tt_ // (2 * s)) & ((jj % (2 * s)) < s) & ((tt_ % (2 * s)) >= s)).astype(f32)
        lv[:, l, 0:128] = U; lv[:, l, 128:256] = U.T; lv[:, l, 256:384] = U
    return dict(cm=cm, retD=retD.astype(f32), retQD=retQD.astype(f32), retkd=retkd.astype(f32), lv=lv), cdec


def rope_tables():
    tok = np.arange(8192)
    row = (tok // 64).astype(np.float64); col = (tok % 64).astype(np.float64)
    inv = 10000.0 ** (-np.arange(16, dtype=np.float64) / 16.0)
    inv32 = (np.float32(10000.0) ** (-np.arange(16, dtype=f32) / f32(16))).astype(f32)
    ar = (row.astype(f32)[:, None] * inv32[None, :]).astype(np.float64)
    ac = (col.astype(f32)[:, None] * inv32[None, :]).astype(np.float64)
    cos = np.concatenate([np.cos(ar), np.cos(ar), np.cos(ac), np.cos(ac)], 1)
    sins = np.concatenate([-np.sin(ar), np.sin(ar), -np.sin(ac), np.sin(ac)], 1)
    return cos.astype(f32), sins.astype(f32)


def core_tokens(c):
    b, q, ci = c // 4, c % 4, c % 2
    return b, q, ci


def split_tiles(xlat, xctx):
    out = []
    for c in range(8):
        b, q, ci = core_tokens(c)
        out.append(np.concatenate([xlat[b, q * 2048:(q + 1) * 2048].reshape(16, 128, 1024),
                                   xctx[b, ci * 128:(ci + 1) * 128][None]], 0))
    return out


def l1_inputs(inp, l, xlat, xctx, K):
    cos, sins = K["rope"]
    ins = []
    xs_all = split_tiles(xlat, xctx)
    conv = inp["rw_conv"][l]
    convp = np.zeros((128, 36), f32)
    for j, (a, b) in enumerate(RCH):
        for tap in range(3):
            convp[0:b - a, 3 * j + tap] = conv[tap, a:b]
    shared = dict(
        wmod=np.ascontiguousarray(inp["w_mod"][l][:, 0:2048]),
        bmod=np.ascontiguousarray(np.tile(inp["b_mod"][l][None, 0:2048], (2, 1))),
        gpre=np.ascontiguousarray(inp["norm_mix_pre"][l].reshape(8, 128).T),
        win=np.ascontiguousarray(inp["w_in"][l]), convp=convp,
        cm=K["c"]["cm"], lv=K["c"]["lv"], retD=K["c"]["retD"], retQD=K["c"]["retQD"], retkd=K["c"]["retkd"],
        gwa=np.ascontiguousarray(np.concatenate([inp["gla_wa2_f"][l], inp["gla_wa2_b"][l]], 1)),
        gba=np.ascontiguousarray(np.concatenate([inp["gla_ba_f"][l], inp["gla_ba_b"][l]])[None]),
        rw2=np.ascontiguousarray(np.concatenate([inp["rw_w2_f"][l], inp["rw_w2_b"][l]], 0)),
        rw0=np.ascontiguousarray(np.concatenate([inp["rw_w0_f"][l], inp["rw_w0_b"][l]])[None]),
        ra2=np.ascontiguousarray(inp["rw_a2"][l]), ra0=np.ascontiguousarray(inp["rw_a0"][l][None]),
        rg2=np.ascontiguousarray(inp["rw_g2"][l]),
        rvec=np.ascontiguousarray(np.stack([inp["rw_k_k"][l], inp["rw_k_a"][l], inp["rw_r_k"][l].reshape(384)])),
    )
    for c in range(8):
        b, q, ci = core_tokens(c)
        xh = np.zeros((34, 1024), f32); fl = np.zeros((34,), f32)
        for i in range(16):
            t0 = q * 2048 + i * 128
            if t0 > 0:
                xh[2 * i] = xlat[b, t0 - 1]; fl[2 * i] = 1
            if t0 + 128 < 8192:
                xh[2 * i + 1] = xlat[b, t0 + 128]; fl[2 * i + 1] = 1
        if ci == 1:
            xh[32] = xctx[b, 127]; fl[32] = 1
        else:
            xh[33] = xctx[b, 128]; fl[33] = 1
        cT = np.zeros((128, 8, 2), f32)
        cT[:, :, 0] = inp["c"][b].reshape(8, 128).T
        cT[:, :, 1] = inp["c_ctx"].reshape(8, 128).T
        rp = np.zeros((NT, 128, 1024), f32)
        sl = slice(q * 2048, (q + 1) * 2048)
        rp[:16, :, 0:512] = np.tile(cos[sl].reshape(16, 128, 64), (1, 1, 8))
        rp[:16, :, 512:1024] = np.tile(sins[sl].reshape(16, 128, 64), (1, 1, 8))
        rp[16, :, 0:512] = 1.0
        d = dict(shared)
        d.update(xs=xs_all[c], xh=xh, hfl=np.ascontiguousarray(np.tile(fl[None], (128, 1))), cT=cT.reshape(128, 16), rope=rp)
        ins.append(d)
    return ins


def build_pre(sums_all, c):
    b, q, ci = core_tokens(c)
    pre = np.zeros((2, 51, 64, SUMW), f32)
    ctx0 = sums_all[4 * b + 0][16]; ctx1 = sums_all[4 * b + 1][16]
    seq = [ctx0[:, 0:SUMW], ctx1[:, 0:SUMW]] + [sums_all[4 * b + j][i][:, 0:SUMW] for j in range(q) for i in range(16)]
    pre[0, 50 - len(seq):50] = np.stack(seq)
    seq = [ctx1[:, SUMW:], ctx0[:, SUMW:]] + [sums_all[4 * b + j][i][:, SUMW:] for j in range(3, q, -1) for i in range(15, -1, -1)]
    pre[1, 50 - len(seq):50] = np.stack(seq)
    if ci == 1:
        pre[0, 50] = ctx0[:, 0:SUMW]
    if ci == 0:
        pre[1, 50] = ctx1[:, SUMW:]
    return pre


def cT_of(inp, b):
    cT = np.zeros((128, 8, 2), f32)
    cT[:, :, 0] = inp["c"][b].reshape(8, 128).T
    cT[:, :, 1] = inp["c_ctx"].reshape(8, 128).T
    return cT.reshape(128, 16)


def sel_const():
    sel = np.zeros((2, 256), f32)
    sel[0, 0:128] = 1.0; sel[1, 128:256] = 1.0
    return sel


def l2_inputs(inp, l, xs_all, r1, K, cdec):
    sums_all = [r1[c]["o_sum"] for c in range(8)]
    cdt = np.zeros((64, 512), f32)
    for d in range(2):
        for h in range(4):
            cdt[:, d * 256 + h * 64:d * 256 + (h + 1) * 64] = cdec[d, h]
    vecs = np.zeros((3, 1024), f32)
    vecs[0] = inp["norm_mix_post"][l]
    vecs[1] = np.concatenate([inp["ret_norm"][l], np.tile(inp["gla_norm"][l], 4), inp["rw_ln_w"][l]])
    vecs[2, 640:] = inp["rw_ln_b"][l]
    shared = dict(wmod=np.ascontiguousarray(inp["w_mod"][l][:, 2048:3072]),
                  bmod=np.ascontiguousarray(np.tile(inp["b_mod"][l][None, 2048:3072], (2, 1))),
                  vecs=vecs, wout=np.ascontiguousarray(inp["w_out"][l]), cm=K["c"]["cm"], sel=sel_const(), cdt=cdt)
    ins = []
    for c in range(8):
        d = dict(shared)
        d.update(xs=xs_all[c], o_tok=r1[c]["o_tok"], o_rt=r1[c]["o_rt"], o_sum=r1[c]["o_sum"], pre=build_pre(sums_all, c),
                 cT=cT_of(inp, c // 4))
        ins.append(d)
    return ins


def l3_inputs(inp, l, r2, K):
    shared = dict(wmod=np.ascontiguousarray(inp["w_mod"][l][:, 3072:6144]),
                  bmod=np.ascontiguousarray(np.tile(inp["b_mod"][l][None, 3072:6144], (2, 1))),
                  gpre=np.ascontiguousarray(inp["norm_ffn_pre"][l].reshape(8, 128).T),
                  vecs=np.ascontiguousarray(inp["norm_ffn_post"][l][None]), cm=K["c"]["cm"], sel=sel_const(),
                  wg=np.ascontiguousarray(inp["w_ffn_gate"][l]), wu=np.ascontiguousarray(inp["w_ffn_up"][l]),
                  wd=np.ascontiguousarray(inp["w_ffn_down"][l]))
    ins = []
    for c in range(8):
        d = dict(shared)
        d.update(xs=r2[c]["xmid"], cT=cT_of(inp, c // 4))
        ins.append(d)
    return ins


def gather(r3, xlat, xctx, key="xout"):
    xlat = xlat.copy(); xctx = xctx.copy()
    for c in range(8):
        b, q, ci = core_tokens(c)
        xo = r3[c][key]
        xlat[b, q * 2048:(q + 1) * 2048] = xo[:16].reshape(2048, 1024)
        if q < 2:
            xctx[b, ci * 128:(ci + 1) * 128] = xo[16]
    return xlat, xctx


from concourse.bass_utils import run_bass_kernel_spmd

_CACHE = {}


def _programs():
    if "p" not in _CACHE:
        Kc, cdec = consts()
        _CACHE["p"] = (build_l1(), build_l2(cdec), build_l3(), dict(c=Kc, rope=rope_tables()), cdec)
    return _CACHE["p"]


def kernel(**inputs):
    inp = {k: np.ascontiguousarray(np.asarray(v, dtype=np.float32)) for k, v in inputs.items()}
    nc1, nc2, nc3, K, cdec = _programs()
    xlat, xctx = inp["x"], inp["ctx"]
    cores = list(range(8))
    for l in range(2):
        xs_all = split_tiles(xlat, xctx)
        r1 = run_bass_kernel_spmd(nc1, l1_inputs(inp, l, xlat, xctx, K), core_ids=cores).results
        r2 = run_bass_kernel_spmd(nc2, l2_inputs(inp, l, xs_all, r1, K, cdec), core_ids=cores).results
        r3 = run_bass_kernel_spmd(nc3, l3_inputs(inp, l, r2, K), core_ids=cores).results
        xlat, xctx = gather(r3, xlat, xctx)
    return xlat.astype(np.float32)
```

```python
import contextlib
import numpy as np
import concourse.bass as bass
import concourse.mybir as mybir

F32 = mybir.dt.float32
BF16 = mybir.dt.bfloat16
AF = mybir.ActivationFunctionType
ALU = mybir.AluOpType
AX = mybir.AxisListType


class Res:
    __slots__ = ("w", "r", "name")

    def __init__(self, name=""):
        self.w = None
        self.r = {}
        self.name = name


class V:
    __slots__ = ("ap", "res")

    def __init__(self, ap, res):
        self.ap = ap
        self.res = res

    def __getitem__(self, idx):
        return V(self.ap[idx], self.res)

    def bc(self, shape):
        return V(self.ap.broadcast_to(list(shape)), self.res)

    def rr(self, pat, **kw):
        return V(self.ap.rearrange(pat, **kw), self.res)

    def sub(self, name=""):
        return V(self.ap, Res(name))


class Prog:
    ENG = ("pe", "act", "dve", "pool", "sp")

    def __init__(self, nc):
        self.nc = nc
        self.es = contextlib.ExitStack()
        self.h = {"pe": nc.tensor, "act": nc.scalar, "dve": nc.vector, "pool": nc.gpsimd, "sp": nc.sync}
        self.sem = {}
        self.cnt = {}
        self.ops = {e: [] for e in self.ENG}
        self.seen = {e: {} for e in self.ENG}
        for e in self.ENG:
            self.sem[e] = self.es.enter_context(nc.semaphore("s_" + e))
            self.cnt[e] = 0
        self.dq = {}
        for q in ("sp", "act", "pool"):
            sems = []
            for i in range(4):
                k = "d_%s%d" % (q, i)
                self.sem[k] = self.es.enter_context(nc.semaphore(k))
                self.cnt[k] = 0
                sems.append(k)
            self.dq[q] = [sems, 0]
        self.n_tiles = 0
        self.banks = []
        self.bi = 0

    def init_banks(self, n=7):
        self.banks = [self.ps([128, 512], F32, name="bank%d" % i) for i in range(n)]

    def bk(self):
        b = self.banks[self.bi % len(self.banks)]
        self.bi += 1
        return b

    def sb(self, shape, dtype=F32, name=None):
        self.n_tiles += 1
        name = name or ("t%d" % self.n_tiles)
        t = self.es.enter_context(self.nc.sbuf_tensor(name, list(shape), dtype))
        return V(t[tuple(slice(None) for _ in shape)], Res(name))

    def ps(self, shape, dtype=F32, name=None):
        self.n_tiles += 1
        name = name or ("p%d" % self.n_tiles)
        t = self.es.enter_context(self.nc.psum_tensor(name, list(shape), dtype))
        return V(t[tuple(slice(None) for _ in shape)], Res(name))

    def dram(self, name, shape, dtype=F32, kind="ExternalInput"):
        t = self.nc.dram_tensor(name, list(shape), dtype, kind=kind)
        return V(t.ap(), Res(name))

    def _deps(self, eng, reads, writes, skip_same=False):
        need = {}

        def req(tok):
            if tok is None:
                return
            k, v = tok
            if skip_same and k == eng:
                return
            if need.get(k, 0) < v:
                need[k] = v

        for r in reads:
            req(r.w)
        for w in writes:
            req(w.w)
            for k, v in w.r.items():
                req((k, v))
        waits = []
        seen = self.seen[eng]
        for k, v in need.items():
            if seen.get(k, 0) < v:
                seen[k] = v
                waits.append((k, v))
        return waits

    def _commit(self, tok, reads, writes):
        k, v = tok
        for r in reads:
            if r.r.get(k, 0) < v:
                r.r[k] = v
        for w in writes:
            w.w = tok
            w.r = {}

    def op(self, eng, fn, reads, writes, skip_same=False):
        reads = [x.res for x in reads if x is not None and isinstance(x, V)]
        writes = [x.res for x in writes]
        waits = self._deps(eng, reads, writes, skip_same)
        self.cnt[eng] += 1
        tok = (eng, self.cnt[eng])
        self.ops[eng].append((waits, fn, (eng, 1)))
        self._commit(tok, reads, writes)

    def dma(self, q, out, in_, **kw):
        sems, i = self.dq[q]
        k = sems[i % len(sems)]
        self.dq[q][1] = i + 1
        reads = [in_.res]
        writes = [out.res]
        waits = self._deps(q, reads, writes)
        if self.cnt[k] > 0 and self.seen[q].get(k, 0) < self.cnt[k]:
            self.seen[q][k] = self.cnt[k]
            waits.append((k, self.cnt[k]))
        self.cnt[k] += 16
        tok = (k, self.cnt[k])
        o, i_ = out.ap, in_.ap
        self.ops[q].append((waits, lambda e: e.dma_start(out=o, in_=i_, **kw), (k, 16)))
        self._commit(tok, reads, writes)

    def dma_like(self, q, fn, reads, writes):
        sems, i = self.dq[q]
        k = sems[i % len(sems)]
        self.dq[q][1] = i + 1
        reads = [x.res for x in reads]; writes = [x.res for x in writes]
        waits = self._deps(q, reads, writes)
        if self.cnt[k] > 0 and self.seen[q].get(k, 0) < self.cnt[k]:
            self.seen[q][k] = self.cnt[k]
            waits.append((k, self.cnt[k]))
        self.cnt[k] += 16
        tok = (k, self.cnt[k])
        self.ops[q].append((waits, fn, (k, 16)))
        self._commit(tok, reads, writes)

    def mm(self, out, lhsT, rhs, start=True, stop=True, **kw):
        o, a, b = out.ap, lhsT.ap, rhs.ap
        self.op("pe", lambda e: e.matmul(o, a, b, start=start, stop=stop, **kw), [lhsT, rhs], [out], skip_same=True)

    def tr(self, out, in_, ident):
        o, a, b = out.ap, in_.ap, ident.ap
        self.op("pe", lambda e: e.transpose(o, a, b), [in_, ident], [out], skip_same=True)

    def act(self, out, in_, func, bias=None, scale=None, accum_out=None, eng="act"):
        o, a = out.ap, in_.ap
        kw = {}
        rd = [in_]
        if bias is not None:
            if isinstance(bias, V):
                rd.append(bias); kw["bias"] = bias.ap
            else:
                kw["bias"] = float(bias)
        if scale is not None:
            if isinstance(scale, V):
                rd.append(scale); kw["scale"] = scale.ap
            else:
                kw["scale"] = float(scale)
        wr = [out]
        if accum_out is not None:
            wr.append(accum_out); kw["accum_out"] = accum_out.ap
        self.op("act", lambda e: e.activation(o, a, func, **kw), rd, wr)

    def tt(self, out, in0, in1, op, eng="dve"):
        o, a, b = out.ap, in0.ap, in1.ap
        self.op(eng, lambda e: e.tensor_tensor(o, a, b, op), [in0, in1], [out])

    def ts(self, out, in0, s1, op0, s2=None, op1=None, accum_out=None, eng="dve"):
        o, a = out.ap, in0.ap
        rd = [in0]
        x1 = s1
        if isinstance(s1, V):
            rd.append(s1); x1 = s1.ap
        x2 = s2
        if isinstance(s2, V):
            rd.append(s2); x2 = s2.ap
        kw = {}
        if op1 is not None:
            kw["op1"] = op1
        wr = [out]
        if accum_out is not None:
            wr.append(accum_out); kw["accum_out"] = accum_out.ap
        self.op(eng, lambda e: e.tensor_scalar(o, a, x1, x2, op0, **kw), rd, wr)

    def stt(self, out, in0, scalar, in1, op0, op1, eng="dve"):
        o, a, b = out.ap, in0.ap, in1.ap
        rd = [in0, in1]
        s = scalar
        if isinstance(scalar, V):
            rd.append(scalar); s = scalar.ap
        self.op(eng, lambda e: e.scalar_tensor_tensor(o, a, s, b, op0, op1), rd, [out])

    def copy(self, out, in_, eng="dve"):
        o, a = out.ap, in_.ap
        if eng == "act":
            self.op("act", lambda e: e.copy(o, a), [in_], [out])
        else:
            self.op(eng, lambda e: e.tensor_copy(o, a), [in_], [out])

    def memset(self, out, val, eng="dve"):
        o = out.ap
        self.op(eng, lambda e: e.memset(o, val), [], [out])

    def reduce(self, out, in_, op=ALU.add, axis=AX.X, eng="dve"):
        o, a = out.ap, in_.ap
        self.op(eng, lambda e: e.tensor_reduce(o, a, axis, op), [in_], [out])

    def recip(self, out, in_):
        o, a = out.ap, in_.ap
        self.op("dve", lambda e: e.reciprocal(o, a), [in_], [out])

    def scan(self, out, d0, d1, initial, op0, op1):
        o, a, b = out.ap, d0.ap, d1.ap
        rd = [d0, d1]
        ini = initial
        if isinstance(initial, V):
            rd.append(initial); ini = initial.ap
        self.op("dve", lambda e: e.tensor_tensor_scan(o, a, b, ini, op0, op1), rd, [out])

    def emit(self, final_tokens):
        nc = self.nc
        fw = {}
        for r in final_tokens:
            if r.w is not None:
                k, v = r.w
                fw[k] = max(fw.get(k, 0), v)
        with nc.Block() as block:
            def mk(e):
                def body(engh):
                    for waits, fn, inc in self.ops[e]:
                        for k, v in waits:
                            engh.wait_ge(self.sem[k], v)
                        ins = fn(engh)
                        ins.then_inc(self.sem[inc[0]], inc[1])
                    if e == "sp":
                        for k, v in fw.items():
                            engh.wait_ge(self.sem[k], v)
                return body
            block.tensor(mk("pe"))
            block.scalar(mk("act"))
            block.vector(mk("dve"))
            block.gpsimd(mk("pool"))
            block.sync(mk("sp"))
        self.es.close()

import math

NT = 17
NCOL = 2180
RW0 = 2208
RCH = [(0, 128), (128, 256), (256, 384), (384, 512), (512, 640), (640, 768), (768, 896), (896, 1024), (1024, 1152),
       (1152, 1280), (1280, 1344), (1344, 1472)]
SUMW = 1412
SOFF_RET, SOFF_GLA, SOFF_RW = 0, 256, 256 + 388


def tile_col0(i):
    return 1 + 128 * i if i < 16 else 2051


def build_l1():
    nc = bass.Bass("TRN2", target_bir_lowering=False)
    P = Prog(nc)
    D = P.dram
    xs = D("xs", [NT, 128, 1024]); xh = D("xh", [34, 1024]); hfl = D("hfl", [128, 34]); cT = D("cT", [128, 16])
    wmod = D("wmod", [1024, 2048]); bmod = D("bmod", [2, 2048]); gpre = D("gpre", [128, 8]); win = D("win", [1024, 3680])
    convp = D("convp", [128, 36]); cm = D("cm", [128, 6, 128]); rope = D("rope", [NT, 128, 1024])
    retD = D("retD", [128, 8, 128]); retQD = D("retQD", [128, 8, 128]); retkd = D("retkd", [128, 8])
    gwa = D("gwa", [16, 384]); gba = D("gba", [1, 384])
    rw2 = D("rw2", [128, 384]); rw0 = D("rw0", [1, 768]); ra2 = D("ra2", [64, 384]); ra0 = D("ra0", [1, 384])
    rg2 = D("rg2", [128, 384]); rvec = D("rvec", [3, 384]); lvd = D("lv", [128, 7, 384])
    o_tok = D("o_tok", [NT, 128, 2432], kind="ExternalOutput")
    o_tokA, o_tokB = o_tok, o_tok
    o_rt = D("o_rt", [NT, 64, 28, 128], kind="ExternalOutput")
    o_sum = D("o_sum", [NT, 64, 2 * SUMW], kind="ExternalOutput")
    P.init_banks(6)
    pO = P.ps([128, 512], F32, name="pO")
    ptb = P.ps([128, 8, 128], BF16, name="ptb")

    cmt = P.sb([128, 6, 128], name="cmt"); P.dma("sp", cmt, cm)
    ident, ones = cmt[:, 0, :], cmt[:, 5, :]
    TRI = [cmt[:, 1, :], cmt[:, 2, :]]
    STR = [cmt[:, 3, :], cmt[:, 4, :]]
    identb = P.sb([128, 128], BF16, name="identb"); P.copy(identb, ident)
    MASK1 = P.sb([128, 2, 256], name="mask1")
    for d in range(2):
        P.copy(MASK1[:, d, 0:128], STR[d], eng="pool"); P.copy(MASK1[:, d, 128:256], TRI[d], eng="pool")
    retDt = P.sb([128, 8, 128], name="retDt"); P.dma("act", retDt, retD)
    retQDt = P.sb([128, 8, 128], name="retQDt"); P.dma("act", retQDt, retQD)
    retkdt = P.sb([128, 8], name="retkdt"); P.dma("act", retkdt, retkd)
    gwat = P.sb([16, 384], name="gwat"); P.dma("sp", gwat, gwa)
    gbat = P.sb([1, 384], name="gbat"); P.dma("sp", gbat, gba)
    rw2t = P.sb([128, 384], name="rw2t"); P.dma("sp", rw2t, rw2)
    rw0t = P.sb([1, 768], name="rw0t"); P.dma("sp", rw0t, rw0)
    ra2t = P.sb([64, 384], name="ra2t"); P.dma("sp", ra2t, ra2)
    ra0t = P.sb([1, 384], name="ra0t"); P.dma("sp", ra0t, ra0)
    rg2t = P.sb([128, 384], name="rg2t"); P.dma("sp", rg2t, rg2)
    rvt = P.sb([128, 3, 384], name="rvt")
    for j in range(3):
        P.dma("act", rvt[:, j, :], V(rvec.ap[j].partition_broadcast(128), rvec.res))
    cvt = P.sb([128, 36], name="cvt"); P.dma("act", cvt, convp)
    hflt = P.sb([128, 34], name="hflt"); P.dma("act", hflt, hfl)
    gpret = P.sb([128, 8], name="gpret"); P.dma("act", gpret, gpre)
    cTt = P.sb([128, 16], name="cTt"); P.dma("sp", cTt, cT)

    scs = P.sb([128, 16], name="scs")
    P.act(scs, cTt, AF.Silu)
    scv = scs.rr("p (k s) -> p k s", s=2)
    u = P.sb([128, 2208], name="u")
    modrow = u[0:2, 0:2048]
    bms = [P.sb([2, 128], name="bms%d" % i) for i in range(2)]
    wst = [P.sb([128, 8, 128], name="wst%d" % i) for i in range(2)]
    wmr = V(wmod.ap.rearrange("(k p) n -> p k n", p=128), wmod.res)
    for cb in range(16):
        st = wst[cb % 2]
        P.dma("sp" if cb % 2 == 0 else "act", st, wmr[:, :, cb * 128:(cb + 1) * 128])
        pb = P.bk()
        for k in range(8):
            P.mm(pb[0:2, 0:128], scv[:, k, :], st[:, k, :], start=(k == 0), stop=(k == 7))
        P.dma("pool", bms[cb % 2], bmod[:, cb * 128:(cb + 1) * 128])
        P.tt(modrow[:, cb * 128:(cb + 1) * 128], pb[0:2, 0:128], bms[cb % 2], ALU.add)
    pm = P.bk()
    for w in range(2):
        for k in range(8):
            P.mm(pm[:, (w * 8 + k) * 2:(w * 8 + k) * 2 + 2], modrow[0:2, w * 1024 + k * 128: w * 1024 + (k + 1) * 128], ident[0:2, 0:2])
    modp = P.sb([128, 2, 8, 2], name="modp")
    P.copy(modp.rr("p a k s -> p (a k s)"), pm[:, 0:32])
    gs = P.sb([128, 8, 2], name="gs")
    P.ts(gs, modp[:, 1, :, :], 1.0, ALU.add)
    P.tt(gs, gs, gpret.rr("p (k o) -> p k o", o=1).bc([128, 8, 2]), ALU.mult)
    sh = modp[:, 0, :, :]

    Wb = P.sb([128, 8, 2208], BF16, name="Wb")
    winr = V(win.ap.rearrange("(k p) n -> p k n", p=128), win.res)

    def load_w(lo, hi):
        n = (hi - lo + 127) // 128
        for cb in range(n):
            c0, c1 = lo + cb * 128, min(hi, lo + cb * 128 + 128)
            st = wst[cb % 2]
            P.dma("sp" if cb % 2 == 0 else "act", st[:, :, 0:c1 - c0], winr[:, :, c0:c1])
            P.copy(Wb[:, :, c0 - lo:c1 - lo], st[:, :, 0:c1 - c0], eng=("pool" if cb % 2 == 0 else "dve"))

    xb = [P.sb([128, 1024], name="xb%d" % i) for i in range(2)]
    xn = P.sb([128, 1024], BF16, name="xn")
    ss = P.sb([128, 1], name="ss"); rstd = P.sb([128, 1], name="rstd")

    def norm_rows(xt, n):
        P.act(xn[0:n, :], xt, AF.Square, accum_out=ss[0:n, :])
        P.act(rstd[0:n, :], ss[0:n, :], AF.Sqrt, bias=1e-6, scale=1.0 / 1024)
        P.recip(rstd[0:n, :], rstd[0:n, :])
        P.ts(xn[0:n, :], xt, rstd[0:n, :], ALU.mult)

    xht = P.sb([34, 1024], name="xht"); P.dma("sp", xht, xh)
    norm_rows(xht, 34)
    for k in range(8):
        P.tr(ptb[:, k, 0:34], xn[0:34, k * 128:(k + 1) * 128], identb[0:34, 0:34])
    hh = P.sb([128, 8, 34], name="hh")
    for s, (j0, j1) in enumerate(((0, 32), (32, 34))):
        P.tt(hh[:, :, j0:j1], ptb[:, :, j0:j1], gs[:, :, s:s + 1].bc([128, 8, j1 - j0]), ALU.mult)
        P.tt(hh[:, :, j0:j1], hh[:, :, j0:j1], sh[:, :, s:s + 1].bc([128, 8, j1 - j0]), ALU.add)
    P.tt(hh, hh, hflt.rr("p (o j) -> p o j", o=1).bc([128, 8, 34]), ALU.mult)
    hTb = [P.sb([128, 8, 130], BF16, name="hT%d" % i) for i in range(2)]
    hcnt = [0]

    def make_hT(i):
        n = hcnt[0]; hcnt[0] += 1
        hT = hTb[n % 2]
        xt = xb[n % 2]
        P.dma("sp" if n % 2 == 0 else "act", xt, xs[i])
        norm_rows(xt, 128)
        for k in range(8):
            P.tr(ptb[:, k, :], xn[:, k * 128:(k + 1) * 128], identb)
        s = 0 if i < 16 else 1
        for k in range(8):
            P.act(hT[:, k, 1:129], ptb[:, k, :], AF.Identity, scale=gs[:, k, s:s + 1], bias=sh[:, k, s:s + 1])
        P.copy(hT[:, :, 0:1], hh[:, :, 2 * i:2 * i + 1], eng="pool")
        P.copy(hT[:, :, 129:130], hh[:, :, 2 * i + 1:2 * i + 2], eng="pool")
        return hT

    finals = []
    ropet = P.sb([128, 1024], name="ropet")
    otok = P.sb([128, 1280], name="otok")
    RTt = P.sb([64, 16, 128], name="RTt")
    SUMt = P.sb([64, 1536], name="SUMt")
    qkT = P.sb([128, 4, 128], name="qkT")
    PT = [P.sb([128, 128], name="PT%d" % i) for i in range(4)]
    lrT = P.sb([16, 2, 128], name="lrT")
    gT = [P.sb([48, 2, 128], name="gT%d" % i) for i in range(4)]
    ucT = P.sb([128, 9, 128], name="ucT"); ctmp = P.sb([128, 128], name="ctmp"); ctmp2 = P.sb([128, 128], name="ctmp2")
    lrw = P.sb([128, 128], name="lrw"); lra = P.sb([64, 128], name="lra"); lrg = P.sb([128, 128], name="lrg")
    r_t, k_t, v_t, a_t, kk = (u[:, j * 384:(j + 1) * 384] for j in range(5))
    km = P.sb([128, 384], name="km")
    beta = P.sb([128, 384], name="beta"); t384 = P.sb([128, 384], name="t384"); s6 = P.sb([128, 6], name="s6")
    ld = P.sb([128, 384], name="ld"); E1 = P.sb([128, 384], name="E1"); E2 = P.sb([128, 384], name="E2")
    E3 = P.sb([128, 384], name="E3"); Et = P.sb([128, 384], name="Et")
    gla_la, gE1, gE2, gEt, gqt, gkh, gKt = (x[:, 0:192] for x in (ld, E1, E2, Et, E3, km, beta))
    Q4 = P.sb([128, 4, 384], name="Q4")
    Q4f = Q4.rr("p a b -> p (a b)")
    qk, rtmp, kd = Q4f[:, 0:512], Q4f[:, 512:1024], Q4f[:, 1024:1280]
    Bt = P.sb([128, 384], name="Bt"); Kt = P.sb([128, 384], name="Kt")
    FT = P.sb([128, 3, 4, 128], name="FT")
    DEZ = P.sb([64, 6, 128], name="DEZ"); P.memset(DEZ, 0.0, eng="pool")
    NM = [P.sb([128, 256], name="NM%d" % i) for i in range(3)]
    BM = [P.sb([128, 256], name="BM%d" % i) for i in range(3)]
    NA = [[P.sb([128, 256], name="NA%d_%d" % (i, j)) for j in range(1)] for i in range(3)]
    NAs = [P.sb([128, 256], name="NAs%d" % i) for i in range(3)]
    WM = [[P.sb([128, 256], name="WM%d_%d" % (i, j)) for j in range(2)] for i in range(3)]
    P12 = [P.sb([128, 256], name="P12_%d" % i) for i in range(3)]
    lvt = P.sb([128, 7, 384], name="lvt"); P.dma("sp", lvt, lvd)
    II = P.sb([128, 256], name="II")
    P.copy(II[:, 0:128], ident, eng="pool"); P.copy(II[:, 128:256], ident, eng="pool")
    Z = [[P.sb([128, 128], name="Z%d_%d" % (i, j)) for j in range(2)] for i in range(3)]
    C1 = math.exp(-0.5)

    def dout(dst, src):
        d2 = dst.sub()
        P.dma("sp", d2, src)
        finals.append(d2.res)

    load_w(0, 2208)
    for i in range(NT):
        hT = make_hT(i)
        P.dma("act", ropet, rope[i])
        for (a, b) in ((0, 512), (512, 1024), (1024, 1536), (1536, 2048), (2048, 2208)):
            pb = P.bk()
            for k in range(8):
                P.mm(pb[:, 0:b - a], hT[:, k, 1:129], Wb[:, k, a:b], start=(k == 0), stop=(k == 7))
            if (a // 512) % 2 == 0:
                P.copy(u[:, a:b], pb[:, 0:b - a], eng="act")
            else:
                P.copy(u[:, a:b], pb[:, 0:b - a])
        uq = u[:, 0:512]
        uq4 = uq.rr("p (a b c) -> p a b c", b=2, c=16)
        sn4 = ropet[:, 512:1024].rr("p (a b c) -> p a b c", b=2, c=16)
        rt4 = rtmp.rr("p (a b c) -> p a b c", b=2, c=16)
        P.tt(rt4[:, :, 0, :], uq4[:, :, 1, :], sn4[:, :, 0, :], ALU.mult, eng="pool")
        P.tt(rt4[:, :, 1, :], uq4[:, :, 0, :], sn4[:, :, 1, :], ALU.mult, eng="pool")
        P.tt(qk, uq, ropet[:, 0:512], ALU.mult, eng="pool")
        P.tt(qk, qk, rtmp, ALU.add, eng="pool")
        pb = P.bk()
        for c in range(4):
            P.tr(pb[:, c * 128:(c + 1) * 128], qk[:, c * 128:(c + 1) * 128], ident)
        P.copy(qkT.rr("p a b -> p (a b)"), pb, eng="act")
        P.act(otok[:, 640:896], u[:, 768:1024], AF.Silu)
        for d in range(2):
            P.tt(kd.rr("p (h c) -> p h c", c=64), qk[:, 256:512].rr("p (h c) -> p h c", c=64),
                 retkdt[:, d * 4:d * 4 + 4].rr("p (h o) -> p h o", o=1).bc([128, 4, 64]), ALU.mult, eng="pool")
            po = pO
            pSs, prs = {}, {}
            for h in range(4):
                c, pbase = h // 2, (h % 2) * 64
                pS = P.bk(); pSs[h] = pS
                P.mm(pS[:, 0:128], qkT[pbase:pbase + 64, 2 + c, :], qkT[pbase:pbase + 64, c, :])
            for h in range(4):
                P.tt(PT[h], pSs[h][:, 0:128], retDt[:, d * 4 + h, :], ALU.mult)
            for h in range(4):
                P.mm(po[:, h * 64:(h + 1) * 64], PT[h], u[:, 512 + h * 64:512 + (h + 1) * 64])
            for h in range(4):
                pr = P.bk(); prs[h] = pr
                P.mm(pr[0:64, 0:128], qk[:, h * 64:(h + 1) * 64], retQDt[:, d * 4 + h, :])
                P.mm(pr[0:64, 128:192], kd[:, h * 64:(h + 1) * 64], u[:, 512 + h * 64:512 + (h + 1) * 64])
            for h in range(4):
                P.copy(RTt[:, d * 8 + h, :], prs[h][0:64, 0:128], eng="act")
                P.copy(SUMt[:, d * 644 + h * 64: d * 644 + (h + 1) * 64], prs[h][0:64, 128:192], eng="act")
            if d == 0:
                P.copy(otok[:, 0:256], po[:, 0:256])
            else:
                P.tt(otok[:, 0:256], otok[:, 0:256], po[:, 0:256], ALU.add)

        pb = P.bk()
        for d in range(2):
            P.tr(pb[0:16, d * 128:(d + 1) * 128], u[:, 2176 + d * 16:2176 + (d + 1) * 16], ident)
        P.copy(lrT.rr("p a b -> p (a b)"), pb[0:16, 0:256], eng="act")
        P.act(otok[:, 896:1280], u[:, 1792:2176], AF.Silu)
        for d in range(2):
            pz = P.bk()
            P.mm(pz[:, 0:192], lrT[:, d, :], gwat[:, d * 192:(d + 1) * 192], start=True, stop=False)
            P.mm(pz[:, 0:192], ones[0:1, :], gbat[0:1, d * 192:(d + 1) * 192], start=False, stop=True)
            P.act(gE1, pz[:, 0:192], AF.Exp, scale=-1.0)
            P.act(gE2, gE1, AF.Ln, bias=1.0)
            P.ts(gla_la, gE2, -1.0 / 16.0, ALU.mult)
            pc = P.bk()
            P.mm(pc[:, 0:192], TRI[d], gla_la)
            P.mm(pc[:, 192:384], ones, gla_la)
            P.act(gE1, pc[:, 0:192], AF.Exp)
            P.act(gE2, pc[:, 0:192], AF.Exp, scale=-1.0)
            P.act(gEt, pc[:, 192:384], AF.Exp)
            P.stt(gqt, u[:, 1024:1216], 48.0 ** -0.5, gE1, ALU.mult, ALU.mult)
            P.tt(gkh, u[:, 1216:1408], gE2, ALU.mult, eng="pool")
            P.tt(gKt, gkh, gEt, ALU.mult, eng="pool")
            po = pO
            for h in range(4):
                hs = slice(h * 48, (h + 1) * 48)
                pq = P.bk()
                P.tr(pq[0:48, 0:128], gqt[:, hs], ident)
                P.tr(pq[0:48, 128:256], gkh[:, hs], ident)
                g2 = gT[h % 2]
                P.copy(g2.rr("p a b -> p (a b)"), pq[0:48, 0:256], eng="act")
                P.copy(RTt[0:48, d * 8 + 4 + h, :], pq[0:48, 0:128], eng="act")
                pS = P.bk()
                P.mm(pS[:, 0:128], g2[:, 1, :], g2[:, 0, :])
                pt = PT[h % 2]
                P.tt(pt, pS[:, 0:128], TRI[d], ALU.mult)
                P.mm(po[:, h * 96:(h + 1) * 96], pt, u[:, 1408 + h * 96:1408 + (h + 1) * 96])
                ph = P.bk()
                P.mm(ph[0:48, 0:96], gKt[:, hs], u[:, 1408 + h * 96:1408 + (h + 1) * 96])
                P.mm(ph[0:48, 96:97], gla_la[:, hs], ones[:, 0:1])
                so = d * 644 + 256 + h * 97
                P.copy(SUMt[0:48, so:so + 96], ph[0:48, 0:96])
                P.act(SUMt[0:48, so + 96:so + 97], ph[0:48, 96:97], AF.Exp)
            if d == 0:
                P.copy(otok[:, 256:640], po[:, 0:384])
            else:
                P.tt(otok[:, 256:640], otok[:, 256:640], po[:, 0:384], ALU.add)

        for (c0, c1), (s0, s1) in (((0, 640), (0, 640)), ((1024, 1664), (640, 1280))):
            dout(o_tok[i][:, c0:c1], otok[:, s0:s1])
        for d in range(2):
            dout(o_rt[i][:, d * 14:d * 14 + 8, :], RTt[:, d * 8:d * 8 + 8, :])
            dout(o_sum[i][:, d * SUMW:d * SUMW + 644], SUMt[:, d * 644:(d + 1) * 644])

    load_w(RW0, 3680)
    for i in range(NT):
        hT = make_hT(i)
        for j, (a, b) in enumerate(RCH):
            M = b - a
            pb = P.bk()
            for k in range(8):
                P.mm(pb[0:M, 0:130], Wb[:, k, a:b], hT[:, k, 0:130], start=(k == 0), stop=(k == 7))
            P.ts(ctmp[0:M, :], pb[0:M, 0:128], cvt[0:M, 3 * j:3 * j + 1], ALU.mult)
            P.stt(ctmp2[0:M, :], pb[0:M, 1:129], cvt[0:M, 3 * j + 1:3 * j + 2], ctmp[0:M, :], ALU.mult, ALU.add)
            if j < 9:
                dst = ucT[:, j, :]
            elif j == 9:
                dst = ctmp[0:M, :]
            elif j == 10:
                dst = lra
            else:
                dst = ctmp[0:M, :]
            P.stt(dst, pb[0:M, 2:130], cvt[0:M, 3 * j + 2:3 * j + 3], ctmp2[0:M, :], ALU.mult, ALU.add)
            if j == 9:
                P.act(lrw, ctmp, AF.Tanh)
            if j == 11:
                P.act(lrg, ctmp, AF.Sigmoid)
        for g3, dstt in enumerate((r_t, k_t, v_t)):
            pb = P.bk()
            for c in range(3):
                P.tr(pb[:, c * 128:(c + 1) * 128], ucT[:, g3 * 3 + c, :], ident)
            P.copy(dstt, pb[:, 0:384], eng=("act" if g3 == 1 else "dve"))

        pa = P.bk()
        P.mm(pa[:, 0:384], lra, ra2t, start=True, stop=False)
        P.mm(pa[:, 0:384], ones[0:1, :], ra0t, start=False, stop=True)
        P.act(a_t, pa[:, 0:384], AF.Sigmoid)
        pg = P.bk()
        P.mm(pg[:, 0:384], lrg, rg2t)
        P.copy(otok[:, 384:768], pg[:, 0:384], eng="act")
        P.tt(kk, k_t, rvt[:, 0, :], ALU.mult, eng="pool")
        P.tt(t384, kk, kk, ALU.mult, eng="pool")
        P.reduce(s6, t384.rr("p (h c) -> p h c", c=64))
        P.ts(s6, s6, 1e-24, ALU.max)
        P.act(s6, s6, AF.Sqrt)
        P.recip(s6, s6)
        P.tt(kk.rr("p (h c) -> p h c", c=64), kk.rr("p (h c) -> p h c", c=64),
             s6.rr("p (h o) -> p h o", o=1).bc([128, 6, 64]), ALU.mult)
        P.stt(t384, a_t, -1.0, rvt[:, 1, :], ALU.add, ALU.mult)
        P.stt(km, t384, 1.0, k_t, ALU.add, ALU.mult)
        P.tt(beta, kk, a_t, ALU.mult, eng="pool")
        P.tt(t384, r_t, km, ALU.mult, eng="pool")
        P.tt(t384, t384, rvt[:, 2, :], ALU.mult, eng="pool")
        P.reduce(s6, t384.rr("p (h c) -> p h c", c=64))
        P.tt(otok[:, 768:1152].rr("p (h c) -> p h c", c=64), v_t.rr("p (h c) -> p h c", c=64),
             s6.rr("p (h o) -> p h o", o=1).bc([128, 6, 64]), ALU.mult)
        for d in range(2):
            pz = P.bk()
            P.mm(pz[:, 0:384], lrw[d * 64:(d + 1) * 64, :], rw2t[d * 64:(d + 1) * 64, :], start=True, stop=False)
            P.mm(pz[:, 0:384], ones[0:1, :], rw0t[0:1, d * 384:(d + 1) * 384], start=False, stop=True)
            P.act(ld, pz[:, 0:384], AF.Sigmoid)
            P.ts(ld, ld, -C1, ALU.mult)
            pc = P.bk(); ptot = P.bk()
            P.mm(pc[:, 0:384], TRI[d], ld)
            P.mm(ptot[:, 0:384], ones, ld)
            P.act(E1, pc[:, 0:384], AF.Exp)
            P.act(E2, pc[:, 0:384], AF.Exp, scale=-1.0)
            P.tt(E3, pc[:, 0:384], ld, ALU.subtract)
            P.act(E3, E3, AF.Exp)
            P.act(Et, ptot[:, 0:384], AF.Exp)
            P.tt(Q4[:, 0, :], beta, E2, ALU.mult)
            P.tt(Q4[:, 1, :], km, E2, ALU.mult, eng="pool")
            P.stt(Q4[:, 2, :], kk, -1.0, E3, ALU.mult, ALU.mult)
            P.tt(Q4[:, 3, :], r_t, E1, ALU.mult, eng="pool")
            P.tt(Bt, Q4[:, 0, :], Et, ALU.mult)
            P.tt(Kt, Q4[:, 1, :], Et, ALU.mult, eng="pool")
            P.tt(DEZ[:, :, 0:64], ident[0:64, 0:64].rr("p (o c) -> p o c", o=1).bc([64, 6, 64]),
                 Et[0:64, :].rr("p (h c) -> p h c", c=64), ALU.mult)
            for c in range(3):
                pb = P.bk()
                for q in range(4):
                    P.tr(pb[:, q * 128:(q + 1) * 128], Q4[:, q, c * 128:(c + 1) * 128], ident)
                P.copy(FT[:, c, :, :].rr("p a b -> p (a b)"), pb, eng=("act" if c % 2 == 0 else "dve"))
            po = pO
            mo = 0 if d == 0 else 128
            for grp in ((0, 1, 2), (3, 4, 5)):
                for h in grp:
                    c, pbase = h // 2, (h % 2) * 64
                    hs = slice(h * 64, (h + 1) * 64)
                    bhT = FT[pbase:pbase + 64, c, 0, :]; khT = FT[pbase:pbase + 64, c, 1, :]
                    alT = FT[pbase:pbase + 64, c, 2, :]
                    alrT = FT[pbase:pbase + 64, c, 2:4, :].rr("p a b -> p (a b)")
                    nm, bm, na, zz = NM[h % 3], BM[h % 3], NA[h % 3], Z[h % 3]
                    p1 = P.bk(); P.mm(p1[:, 0:256], bhT, alrT)
                    P.tt(nm, p1[:, 0:256], MASK1[:, d, :], ALU.mult)
                    p2 = P.bk(); P.mm(p2[:, 0:256], khT, alrT)
                    P.tt(bm, p2[:, 0:256], MASK1[:, d, :], ALU.mult)
                for h in grp:
                    c, pbase = h // 2, (h % 2) * 64
                    hs = slice(h * 64, (h + 1) * 64)
                    bhT = FT[pbase:pbase + 64, c, 0, :]; alT = FT[pbase:pbase + 64, c, 2, :]
                    nm, bm, na, zz = NM[h % 3], BM[h % 3], NA[h % 3], Z[h % 3]
                    p3 = P.bk(); P.mm(p3[:, 0:128], alT, bhT)
                    P.copy(na[0][:, 0:128], nm[:, 0:128], eng="pool")
                    P.tt(na[0][:, 128:256], p3[:, 0:128], STR[1 - d], ALU.mult)
                    p4 = P.bk(); P.mm(p4[:, 0:64], bm[:, 0:128], v_t[:, hs])
                    P.copy(zz[0][:, 0:64], Q4[:, 2, hs], eng="pool")
                    P.copy(zz[0][:, 64:128], p4[:, 0:64], eng="act")
                for h in grp:
                    P.tt(NAs[h % 3], NA[h % 3][0], lvt[:, 0, mo:mo + 256], ALU.mult, eng="pool")
                    P.tt(WM[h % 3][0], NAs[h % 3], II, ALU.add, eng="pool")
                cur = 0
                for lv_ in range(1, 7):
                    last = (lv_ == 6)
                    pPs = {}
                    for h in grp:
                        nas, wm = NAs[h % 3], WM[h % 3]
                        P.tt(nas, NA[h % 3][0], lvt[:, lv_, mo:mo + 256], ALU.mult, eng="pool")
                    for h in grp:
                        nas, wm = NAs[h % 3], WM[h % 3]
                        pP = P.bk(); pPs[h] = pP
                        P.mm(pP[:, 0:128], nas[:, 128:256], wm[cur][:, 0:128])
                        if not last:
                            P.mm(pP[:, 128:256], nas[:, 0:128], wm[cur][:, 128:256])
                    for h in grp:
                        if not last:
                            P.copy(P12[h % 3], pPs[h][:, 0:256], eng="act")
                        else:
                            P.copy(P12[h % 3][:, 0:128], pPs[h][:, 0:128], eng="act")
                    pUs = {}
                    for h in grp:
                        wm, p12 = WM[h % 3], P12[h % 3]
                        pU = P.bk(); pUs[h] = pU
                        P.mm(pU[:, 0:128], wm[cur][:, 128:256], p12[:, 0:128])
                        if not last:
                            P.mm(pU[:, 128:256], wm[cur][:, 0:128], p12[:, 128:256])
                    for h in grp:
                        wm = WM[h % 3]
                        if not last:
                            P.tt(wm[1 - cur], pUs[h][:, 0:256], wm[cur], ALU.add)
                        else:
                            P.tt(wm[1 - cur][:, 0:128], pUs[h][:, 0:128], wm[cur][:, 0:128], ALU.add)
                    cur = 1 - cur
                pzs = {}
                for h in grp:
                    pz2 = P.bk(); pzs[h] = pz2
                    P.mm(pz2[:, 0:128], WM[h % 3][cur][:, 0:128], Z[h % 3][0])
                for h in grp:
                    P.copy(Z[h % 3][1], pzs[h][:, 0:128], eng="act")
                for h in grp:
                    hs = slice(h * 64, (h + 1) * 64)
                    nm, bm = NM[h % 3], BM[h % 3]
                    zf = Z[h % 3][1]
                    X, Y = zf[:, 0:64], zf[:, 64:128]
                    mbT, mkT = nm[:, 128:256], bm[:, 128:256]
                    pr = P.bk()
                    P.mm(pr[0:64, 0:128], Q4[:, 3, hs], ident, start=True, stop=False)
                    P.mm(pr[0:64, 0:128], X, mbT, start=False, stop=True)
                    P.copy(RTt[:, d * 6 + h, :], pr[0:64, 0:128], eng="act")
                    P.mm(po[:, hs], mbT, Y, start=True, stop=False)
                    P.mm(po[:, hs], mkT, v_t[:, hs], start=False, stop=True)
                    pgh = P.bk()
                    P.mm(pgh[0:64, 0:64], X, Bt[:, hs])
                    P.mm(pgh[0:64, 64:128], Bt[:, hs], Y, start=True, stop=False)
                    P.mm(pgh[0:64, 64:128], Kt[:, hs], v_t[:, hs], start=False, stop=True)
                    so = d * 768 + h * 128
                    P.tt(SUMt[:, so:so + 128], pgh[0:64, 0:128], DEZ[:, h, :], ALU.add)
            if d == 0:
                P.copy(otok[:, 0:384], po[:, 0:384])
            else:
                P.tt(otok[:, 0:384], otok[:, 0:384], po[:, 0:384], ALU.add)
        for (c0, c1), (s0, s1) in (((640, 1024), (0, 384)), ((1664, 2432), (384, 1152))):
            dout(o_tok[i][:, c0:c1], otok[:, s0:s1])
        for d in range(2):
            dout(o_rt[i][:, d * 14 + 8:d * 14 + 14, :], RTt[:, d * 6:d * 6 + 6, :])
            dout(o_sum[i][:, d * SUMW + 644:(d + 1) * SUMW], SUMt[:, d * 768:(d + 1) * 768])
    P.emit(finals)
    return nc


NPRE = 50


def mod_setup(P, cT):
    cTt = P.sb([128, 16], name="cTt"); P.dma("sp", cTt, cT)
    scs = P.sb([128, 16], name="scs")
    P.act(scs, cTt, AF.Silu)
    scv = scs.rr("p (k s) -> p k s", s=2)
    modrow = P.sb([2, 1024], name="modrow")
    bms = [P.sb([2, 128], name="bms%d" % i) for i in range(2)]
    return scv, bms, modrow


def mod_block(P, ms, wmod, bmod, c0, wst):
    scv, bms, modrow = ms
    wmr = V(wmod.ap.rearrange("(k p) n -> p k n", p=128), wmod.res)
    for cb in range(8):
        st = wst[cb % 2]
        cc = c0 + cb * 128
        P.dma("sp" if cb % 2 == 0 else "act", st, wmr[:, :, cc:cc + 128])
        pb = P.bk()
        for k in range(8):
            P.mm(pb[0:2, 0:128], scv[:, k, :], st[:, k, :], start=(k == 0), stop=(k == 7))
        P.dma("pool", bms[cb % 2], bmod[:, cc:cc + 128])
        P.tt(modrow[:, cb * 128:(cb + 1) * 128], pb[0:2, 0:128], bms[cb % 2], ALU.add)
    return modrow


def load_w_bf16(P, dst, src, kch, ncols, wst):
    sr = V(src.ap.rearrange("(k p) n -> p k n", p=128), src.res)
    n = 0
    for k0 in range(0, kch, 8):
        k1 = min(kch, k0 + 8)
        for c0 in range(0, ncols, 128):
            c1 = min(ncols, c0 + 128)
            st = wst[n % 2]
            P.dma("sp" if n % 2 == 0 else "act", st[:, 0:k1 - k0, 0:c1 - c0], sr[:, k0:k1, c0:c1])
            P.copy(dst[:, k0:k1, c0:c1], st[:, 0:k1 - k0, 0:c1 - c0], eng=("pool" if n % 2 == 0 else "dve"))
            n += 1


def rms_finish(P, py, xt, gg, xo, tmp, ss2, rstd, junk):
    for cb in range(2):
        P.act(junk[:, cb * 512:(cb + 1) * 512], py[cb], AF.Square, accum_out=ss2[:, cb:cb + 1])
    P.tt(rstd, ss2[:, 0:1], ss2[:, 1:2], ALU.add)
    P.act(rstd, rstd, AF.Sqrt, bias=1e-6, scale=1.0 / 1024)
    P.recip(rstd, rstd)
    for cb in range(2):
        P.stt(tmp[:, cb * 512:(cb + 1) * 512], py[cb], rstd, gg[:, cb * 512:(cb + 1) * 512], ALU.mult, ALU.mult)
    P.tt(xo, tmp, xt, ALU.add, eng="pool")


def build_l2(cdec):
    nc = bass.Bass("TRN2", target_bir_lowering=False)
    P = Prog(nc)
    D = P.dram
    xs = D("xs", [NT, 128, 1024]); o_tok = D("o_tok", [NT, 128, 2432]); o_rt = D("o_rt", [NT, 64, 28, 128])
    o_sum = D("o_sum", [NT, 64, 2 * SUMW]); pre = D("pre", [2, NPRE + 1, 64, SUMW])
    cT = D("cT", [128, 16]); wmod = D("wmod", [1024, 1024]); bmod = D("bmod", [2, 1024])
    vecs = D("vecs", [3, 1024]); wout = D("wout", [1024, 1024]); cm = D("cm", [128, 6, 128]); sel = D("sel", [2, 256])
    cdtd = D("cdt", [64, 512])
    xmid = D("xmid", [NT, 128, 1024], kind="ExternalOutput")
    scr = D("scr", [NT, 128, 1024], kind="ExternalOutput")
    P.init_banks(4)
    pA = P.ps([128, 512], F32, name="pA"); pB = P.ps([128, 512], F32, name="pB"); pC = P.ps([128, 512], F32, name="pC")
    ptb = P.ps([128, 8, 128], BF16, name="ptb")
    finals = []

    cmt = P.sb([128, 6, 128], name="cmt"); P.dma("sp", cmt, cm)
    ident = cmt[:, 0, :]
    identb = P.sb([128, 128], BF16, name="identb"); P.copy(identb, ident)
    selt = P.sb([2, 256], name="selt"); P.dma("sp", selt, sel)
    cdt = P.sb([64, 512], name="cdtt"); P.dma("sp", cdt, cdtd)
    vb = P.sb([128, 3, 1024], name="vb")
    for j in range(3):
        P.dma("act", vb[:, j, :], V(vecs.ap[j].partition_broadcast(128), vecs.res))
    wst = [P.sb([128, 8, 128], name="wst%d" % i) for i in range(2)]
    ms = mod_setup(P, cT)
    modrow = mod_block(P, ms, wmod, bmod, 0, wst)
    gg = P.sb([128, 2, 1024], name="gg")
    for s in range(2):
        for cb in range(2):
            pb = P.bk()
            P.mm(pb, selt[0:2, s * 128:(s + 1) * 128], modrow[0:2, cb * 512:(cb + 1) * 512])
            P.tt(gg[:, s, cb * 512:(cb + 1) * 512], pb, vb[:, 0, cb * 512:(cb + 1) * 512], ALU.mult)
    Wo = P.sb([128, 8, 1024], BF16, name="Wo")
    load_w_bf16(P, Wo, wout, 8, 1024, wst)

    STS = {}
    for seg in ("lat", "ctx"):
        for d in range(2):
            STS[(seg, d)] = (P.sb([64, 4, 64], name="STr_%s%d" % (seg, d)), P.sb([64, 4, 96], name="STg_%s%d" % (seg, d)),
                             P.sb([64, 6, 64], name="STw_%s%d" % (seg, d)))
    SM = [P.sb([64, SUMW], name="SM%d" % i) for i in range(4)]
    RTs = [P.sb([64, 14, 128], name="RTs%d" % i) for i in range(2)]
    accb = [P.sb([128, 1024], name="accb%d" % i) for i in range(2)]
    gts = [P.sb([128, 1408], name="gts%d" % i) for i in range(2)]
    xb = [P.sb([128, 1024], name="xb%d" % i) for i in range(2)]
    cen = P.sb([128, 1024], name="cen"); sq = P.sb([128, 1024], name="sq")
    ycb = P.sb([128, 1024], BF16, name="ycb"); yT = P.sb([128, 8, 128], BF16, name="yT")
    tmp = P.sb([128, 1024], name="tmp"); xo = [P.sb([128, 1024], name="xo%d" % i) for i in range(2)]
    s14 = P.sb([128, 14], name="s14"); r14 = P.sb([128, 14], name="r14")
    ss2 = P.sb([128, 2], name="ss2"); rstd = P.sb([128, 1], name="rstd")
    scrv = [scr[i].sub() for i in range(NT)]
    cnt = {"sm": 0, "rt": 0, "acc": 0, "fin": 0}

    def reset_states(st):
        for t_ in st:
            P.memset(t_, 0.0, eng="pool")

    def update(sm, d, st):
        STr, STg, STw = st
        P.tt(STr, STr, cdt[:, d * 256:(d + 1) * 256].rr("p (h c) -> p h c", c=64), ALU.mult, eng="pool")
        P.tt(STr, STr, sm[:, 0:256].rr("p (h c) -> p h c", c=64), ALU.add, eng="pool")
        for h in range(4):
            o = 256 + h * 97
            P.stt(STg[0:48, h, :], STg[0:48, h, :], sm[0:48, o + 96:o + 97], sm[0:48, o:o + 96], ALU.mult, ALU.add)
        pw = P.bk()
        for h in range(6):
            o = 644 + h * 128
            P.mm(pw[0:64, h * 64:(h + 1) * 64], sm[:, o:o + 64], STw[:, h, :])
        P.tt(STw, pw[0:64, 0:384].rr("p (h c) -> p h c", c=64),
             sm[:, 644:644 + 768].rr("p (h c) -> p h c", c=128)[:, :, 64:128], ALU.add)

    def load_sm(src):
        sm = SM[cnt["sm"] % 4]; cnt["sm"] += 1
        P.dma("act" if cnt["sm"] % 2 == 0 else "sp", sm, src)
        return sm

    def outputs(i, d, acc, st):
        STr, STg, STw = st
        rt = RTs[cnt["rt"] % 2]; cnt["rt"] += 1
        P.dma("pool", rt, o_rt[i][:, d * 14:(d + 1) * 14, :])
        for h in range(4):
            P.mm(pA[:, h * 64:(h + 1) * 64], rt[0:64, h, :], STr[:, h, :])
        for h in range(4):
            P.mm(pB[:, h * 96:(h + 1) * 96], rt[0:48, 4 + h, :], STg[0:48, h, :])
        for h in range(6):
            P.mm(pC[:, h * 64:(h + 1) * 64], rt[0:64, 8 + h, :], STw[:, h, :])
        P.tt(acc[:, 0:256], acc[:, 0:256], pA[:, 0:256], ALU.add)
        P.tt(acc[:, 256:640], acc[:, 256:640], pB[:, 0:384], ALU.add)
        P.tt(acc[:, 640:1024], acc[:, 640:1024], pC[:, 0:384], ALU.add)

    def headnorm(o3, c3, H, dh, center, eps, off):
        sv = s14[:, off:off + H]; rv = r14[:, off:off + H]
        src = o3
        if center:
            P.reduce(sv, o3)
            P.ts(sv, sv, -1.0 / dh, ALU.mult)
            P.tt(c3, o3, sv.rr("p (h o) -> p h o", o=1).bc([128, H, dh]), ALU.add)
            src = c3
        q3 = sq[:, 0:H * dh].rr("p (h c) -> p h c", c=dh)
        P.tt(q3, src, src, ALU.mult, eng="pool")
        P.reduce(rv, q3)
        P.act(rv, rv, AF.Sqrt, bias=eps, scale=1.0 / dh)
        P.recip(rv, rv)
        P.tt(c3, src, rv.rr("p (h o) -> p h o", o=1).bc([128, H, dh]), ALU.mult)

    def finish(i, acc):
        n = cnt["fin"]; cnt["fin"] += 1
        g = gts[n % 2]; xt = xb[n % 2]
        P.dma("sp", g, o_tok[i][:, 1024:2432])
        P.dma("act", xt, xs[i])
        headnorm(acc[:, 0:256].rr("p (h c) -> p h c", c=64), cen[:, 0:256].rr("p (h c) -> p h c", c=64), 4, 64, True, 1e-5, 0)
        headnorm(acc[:, 256:640].rr("p (h c) -> p h c", c=96), cen[:, 256:640].rr("p (h c) -> p h c", c=96), 4, 96, False, 1e-5, 4)
        headnorm(acc[:, 640:1024].rr("p (h c) -> p h c", c=64), cen[:, 640:1024].rr("p (h c) -> p h c", c=64), 6, 64, True, 64e-5, 8)
        P.tt(cen, cen, vb[:, 1, :], ALU.mult)
        P.tt(cen[:, 640:1024], cen[:, 640:1024], vb[:, 2, 640:1024], ALU.add, eng="pool")
        P.tt(cen[:, 640:1024], cen[:, 640:1024], g[:, 1024:1408], ALU.add, eng="pool")
        P.tt(ycb, cen, g[:, 0:1024], ALU.mult)
        for k in range(8):
            P.tr(ptb[:, k, :], ycb[:, k * 128:(k + 1) * 128], identb)
        P.copy(yT.rr("p a b -> p (a b)"), ptb.rr("p a b -> p (a b)"), eng="act")
        py = [pA, pB]
        for cb in range(2):
            for k in range(8):
                P.mm(py[cb], yT[:, k, :], Wo[:, k, cb * 512:(cb + 1) * 512], start=(k == 0), stop=(k == 7))
        s = 0 if i < 16 else 1
        x_o = xo[n % 2]
        rms_finish(P, py, xt, gg[:, s, :], x_o, tmp, ss2, rstd, sq)
        dd = xmid[i].sub()
        P.dma("sp", dd, x_o)
        finals.append(dd.res)

    for st in STS.values():
        reset_states(st)
    for s_ in range(NPRE):
        for d in range(2):
            update(load_sm(pre[d, s_]), d, STS[("lat", d)])
    for d in range(2):
        update(load_sm(pre[d, NPRE]), d, STS[("ctx", d)])
    for d in range(2):
        order = list(range(16)) if d == 0 else list(range(15, -1, -1))
        for seg in ("lat", "ctx"):
            st = STS[(seg, d)]
            tiles = order if seg == "lat" else [16]
            for n_, i in enumerate(tiles):
                acc = accb[cnt["acc"] % 2]; cnt["acc"] += 1
                if d == 0:
                    P.dma("sp", acc, o_tok[i][:, 0:1024])
                else:
                    P.dma("sp", acc, scrv[i])
                outputs(i, d, acc, st)
                if d == 0:
                    P.dma("act", scrv[i], acc)
                else:
                    finish(i, acc)
                if n_ < len(tiles) - 1:
                    update(load_sm(o_sum[i][:, d * SUMW:(d + 1) * SUMW]), d, st)
    finals.extend(v.res for v in scrv)
    P.emit(finals)
    return nc


def build_l3():
    nc = bass.Bass("TRN2", target_bir_lowering=False)
    P = Prog(nc)
    D = P.dram
    xs = D("xs", [NT, 128, 1024]); cT = D("cT", [128, 16]); wmod = D("wmod", [1024, 3072]); bmod = D("bmod", [2, 3072])
    gpre = D("gpre", [128, 8]); vecs = D("vecs", [1, 1024]); cm = D("cm", [128, 6, 128]); sel = D("sel", [2, 256])
    wg = D("wg", [1024, 2816]); wu = D("wu", [1024, 2816]); wd = D("wd", [2816, 1024])
    xout = D("xout", [NT, 128, 1024], kind="ExternalOutput")
    P.init_banks(4)
    pA = P.ps([128, 512], F32, name="pA"); pB = P.ps([128, 512], F32, name="pB")
    ptb = P.ps([128, 8, 128], BF16, name="ptb")
    finals = []
    cmt = P.sb([128, 6, 128], name="cmt"); P.dma("sp", cmt, cm)
    ident = cmt[:, 0, :]
    identb = P.sb([128, 128], BF16, name="identb"); P.copy(identb, ident)
    selt = P.sb([2, 256], name="selt"); P.dma("sp", selt, sel)
    gpret = P.sb([128, 8], name="gpret"); P.dma("act", gpret, gpre)
    vb = P.sb([128, 1024], name="vb"); P.dma("act", vb, V(vecs.ap[0].partition_broadcast(128), vecs.res))
    wst = [P.sb([128, 8, 128], name="wst%d" % i) for i in range(2)]
    ms = mod_setup(P, cT)
    pm = P.ps([128, 32], F32, name="pm")
    for w in range(2):
        modrow = mod_block(P, ms, wmod, bmod, w * 1024, wst)
        for k in range(8):
            P.mm(pm[:, (w * 8 + k) * 2:(w * 8 + k) * 2 + 2], modrow[0:2, k * 128:(k + 1) * 128], ident[0:2, 0:2])
    modp = P.sb([128, 2, 8, 2], name="modp")
    P.copy(modp.rr("p a k s -> p (a k s)"), pm[:, 0:32])
    gs = P.sb([128, 8, 2], name="gs")
    P.ts(gs, modp[:, 1, :, :], 1.0, ALU.add)
    P.tt(gs, gs, gpret.rr("p (k o) -> p k o", o=1).bc([128, 8, 2]), ALU.mult)
    sh = modp[:, 0, :, :]
    modrow = mod_block(P, ms, wmod, bmod, 2048, wst)
    gg = P.sb([128, 2, 1024], name="gg")
    for s in range(2):
        for cb in range(2):
            pb = P.bk()
            P.mm(pb, selt[0:2, s * 128:(s + 1) * 128], modrow[0:2, cb * 512:(cb + 1) * 512])
            P.tt(gg[:, s, cb * 512:(cb + 1) * 512], pb, vb[:, cb * 512:(cb + 1) * 512], ALU.mult)
    Wg = P.sb([128, 8, 2816], BF16, name="Wg"); Wu = P.sb([128, 8, 2816], BF16, name="Wu"); Wd = P.sb([128, 22, 1024], BF16, name="Wd")
    load_w_bf16(P, Wg, wg, 8, 2816, wst)
    load_w_bf16(P, Wu, wu, 8, 2816, wst)
    load_w_bf16(P, Wd, wd, 22, 1024, wst)

    xb = [P.sb([128, 1024], name="xb%d" % i) for i in range(3)]
    xn = P.sb([128, 1024], BF16, name="xn")
    ss = P.sb([128, 1], name="ss"); rstd = P.sb([128, 1], name="rstd"); ss2 = P.sb([128, 2], name="ss2")
    hT = P.sb([128, 8, 256], BF16, name="hT")
    hid = P.sb([128, 22, 256], BF16, name="hid")
    sg = [P.sb([128, 256], name="sg%d" % i) for i in range(2)]
    tmp = P.sb([128, 1024], name="tmp"); junk = tmp
    xo = [P.sb([128, 1024], name="xo%d" % i) for i in range(2)]
    groups = [(2 * g, 2 * g + 1) for g in range(8)] + [(16,)]
    nx = 0
    for grp in groups:
        T = len(grp) * 128
        xts = []
        for j, i in enumerate(grp):
            xt = xb[nx % 3]; nx += 1
            xts.append(xt)
            P.dma("sp" if j == 0 else "act", xt, xs[i])
            P.act(xn, xt, AF.Square, accum_out=ss)
            P.act(rstd, ss, AF.Sqrt, bias=1e-6, scale=1.0 / 1024)
            P.recip(rstd, rstd)
            P.ts(xn, xt, rstd, ALU.mult)
            for k in range(8):
                P.tr(ptb[:, k, :], xn[:, k * 128:(k + 1) * 128], identb)
            s = 0 if i < 16 else 1
            for k in range(8):
                P.act(hT[:, k, j * 128:(j + 1) * 128], ptb[:, k, :], AF.Identity, scale=gs[:, k, s:s + 1], bias=sh[:, k, s:s + 1])
        for hc in range(22):
            pg = P.bk(); pu = P.bk()
            for k in range(8):
                P.mm(pg[:, 0:T], Wg[:, k, hc * 128:(hc + 1) * 128], hT[:, k, 0:T], start=(k == 0), stop=(k == 7))
            for k in range(8):
                P.mm(pu[:, 0:T], Wu[:, k, hc * 128:(hc + 1) * 128], hT[:, k, 0:T], start=(k == 0), stop=(k == 7))
            sgt = sg[hc % 2]
            P.act(sgt[:, 0:T], pg[:, 0:T], AF.Silu)
            P.tt(hid[:, hc, 0:T], sgt[:, 0:T], pu[:, 0:T], ALU.mult)
        for j, i in enumerate(grp):
            py = [pA, pB]
            for cb in range(2):
                for hc in range(22):
                    P.mm(py[cb], hid[:, hc, j * 128:(j + 1) * 128], Wd[:, hc, cb * 512:(cb + 1) * 512], start=(hc == 0), stop=(hc == 21))
            s = 0 if i < 16 else 1
            x_o = xo[i % 2]
            rms_finish(P, py, xts[j], gg[:, s, :], x_o, tmp, ss2, rstd, junk)
            dd = xout[i].sub()
            P.dma("sp", dd, x_o)
            finals.append(dd.res)
    P.emit(finals)
    return nc


f32 = np.float32


def consts():
    C = 128
    cm = np.zeros((128, 6, 128), f32)
    cm[:, 0] = np.eye(C)
    cm[:, 1] = np.triu(np.ones((C, C)))
    cm[:, 2] = np.tril(np.ones((C, C)))
    cm[:, 3] = np.triu(np.ones((C, C)), 1)
    cm[:, 4] = np.tril(np.ones((C, C)), -1)
    cm[:, 5] = 1.0
    gam = 1.0 - np.exp2(-5.0 - np.arange(4))
    j = np.arange(C)[:, None].astype(np.float64); t = np.arange(C)[None, :].astype(np.float64)
    retD = np.zeros((128, 8, 128), np.float64); retQD = np.zeros((128, 8, 128), np.float64); retkd = np.zeros((128, 8), np.float64)
    cdec = np.zeros((2, 4))
    for d in range(2):
        for h in range(4):
            g = gam[h] if d == 0 else gam[3 - h]
            if d == 0:
                retD[:, d * 4 + h, :] = np.where(t >= j, 0.125 * g ** np.maximum(t - j, 0), 0.0)
                retQD[:, d * 4 + h, :] = np.eye(C) * (g ** (np.arange(C) + 1.0))[None, :]
                retkd[:, d * 4 + h] = 0.125 * g ** (127.0 - np.arange(C))
            else:
                retD[:, d * 4 + h, :] = np.where(j >= t, 0.125 * g ** np.maximum(j - t, 0), 0.0)
                retQD[:, d * 4 + h, :] = np.eye(C) * (g ** (128.0 - np.arange(C)))[None, :]
                retkd[:, d * 4 + h] = 0.125 * g ** (np.arange(C) * 1.0)
            cdec[d, h] = g ** 128.0
    jj = np.arange(C)[:, None]; tt_ = np.arange(C)[None, :]
    lv = np.zeros((128, 7, 384), f32)
    for l in range(7):
        s = 1 << l
        U = ((jj // (2 * s) == tt_ // (2 * s)) & ((jj % (2 * s)) < s) & ((tt_ % (2 * s)) >= s)).astype(f32)
        lv[:, l, 0:128] = U; lv[:, l, 128:256] = U.T; lv[:, l, 256:384] = U
    return dict(cm=cm, retD=retD.astype(f32), retQD=retQD.astype(f32), retkd=retkd.astype(f32), lv=lv), cdec


def rope_tables():
    tok = np.arange(8192)
    row = (tok // 64).astype(np.float64); col = (tok % 64).astype(np.float64)
    inv = 10000.0 ** (-np.arange(16, dtype=np.float64) / 16.0)
    inv32 = (np.float32(10000.0) ** (-np.arange(16, dtype=f32) / f32(16))).astype(f32)
    ar = (row.astype(f32)[:, None] * inv32[None, :]).astype(np.float64)
    ac = (col.astype(f32)[:, None] * inv32[None, :]).astype(np.float64)
    cos = np.concatenate([np.cos(ar), np.cos(ar), np.cos(ac), np.cos(ac)], 1)
    sins = np.concatenate([-np.sin(ar), np.sin(ar), -np.sin(ac), np.sin(ac)], 1)
    return cos.astype(f32), sins.astype(f32)


def core_tokens(c):
    b, q, ci = c // 4, c % 4, c % 2
    return b, q, ci


def split_tiles(xlat, xctx):
    out = []
    for c in range(8):
        b, q, ci = core_tokens(c)
        out.append(np.concatenate([xlat[b, q * 2048:(q + 1) * 2048].reshape(16, 128, 1024),
                                   xctx[b, ci * 128:(ci + 1) * 128][None]], 0))
    return out


def l1_inputs(inp, l, xlat, xctx, K):
    cos, sins = K["rope"]
    ins = []
    xs_all = split_tiles(xlat, xctx)
    conv = inp["rw_conv"][l]
    convp = np.zeros((128, 36), f32)
    for j, (a, b) in enumerate(RCH):
        for tap in range(3):
            convp[0:b - a, 3 * j + tap] = conv[tap, a:b]
    shared = dict(
        wmod=np.ascontiguousarray(inp["w_mod"][l][:, 0:2048]),
        bmod=np.ascontiguousarray(np.tile(inp["b_mod"][l][None, 0:2048], (2, 1))),
        gpre=np.ascontiguousarray(inp["norm_mix_pre"][l].reshape(8, 128).T),
        win=np.ascontiguousarray(inp["w_in"][l]), convp=convp,
        cm=K["c"]["cm"], lv=K["c"]["lv"], retD=K["c"]["retD"], retQD=K["c"]["retQD"], retkd=K["c"]["retkd"],
        gwa=np.ascontiguousarray(np.concatenate([inp["gla_wa2_f"][l], inp["gla_wa2_b"][l]], 1)),
        gba=np.ascontiguousarray(np.concatenate([inp["gla_ba_f"][l], inp["gla_ba_b"][l]])[None]),
        rw2=np.ascontiguousarray(np.concatenate([inp["rw_w2_f"][l], inp["rw_w2_b"][l]], 0)),
        rw0=np.ascontiguousarray(np.concatenate([inp["rw_w0_f"][l], inp["rw_w0_b"][l]])[None]),
        ra2=np.ascontiguousarray(inp["rw_a2"][l]), ra0=np.ascontiguousarray(inp["rw_a0"][l][None]),
        rg2=np.ascontiguousarray(inp["rw_g2"][l]),
        rvec=np.ascontiguousarray(np.stack([inp["rw_k_k"][l], inp["rw_k_a"][l], inp["rw_r_k"][l].reshape(384)])),
    )
    for c in range(8):
        b, q, ci = core_tokens(c)
        xh = np.zeros((34, 1024), f32); fl = np.zeros((34,), f32)
        for i in range(16):
            t0 = q * 2048 + i * 128
            if t0 > 0:
                xh[2 * i] = xlat[b, t0 - 1]; fl[2 * i] = 1
            if t0 + 128 < 8192:
                xh[2 * i + 1] = xlat[b, t0 + 128]; fl[2 * i + 1] = 1
        if ci == 1:
            xh[32] = xctx[b, 127]; fl[32] = 1
        else:
            xh[33] = xctx[b, 128]; fl[33] = 1
        cT = np.zeros((128, 8, 2), f32)
        cT[:, :, 0] = inp["c"][b].reshape(8, 128).T
        cT[:, :, 1] = inp["c_ctx"].reshape(8, 128).T
        rp = np.zeros((NT, 128, 1024), f32)
        sl = slice(q * 2048, (q + 1) * 2048)
        rp[:16, :, 0:512] = np.tile(cos[sl].reshape(16, 128, 64), (1, 1, 8))
        rp[:16, :, 512:1024] = np.tile(sins[sl].reshape(16, 128, 64), (1, 1, 8))
        rp[16, :, 0:512] = 1.0
        d = dict(shared)
        d.update(xs=xs_all[c], xh=xh, hfl=np.ascontiguousarray(np.tile(fl[None], (128, 1))), cT=cT.reshape(128, 16), rope=rp)
        ins.append(d)
    return ins


def build_pre(sums_all, c):
    b, q, ci = core_tokens(c)
    pre = np.zeros((2, 51, 64, SUMW), f32)
    ctx0 = sums_all[4 * b + 0][16]; ctx1 = sums_all[4 * b + 1][16]
    seq = [ctx0[:, 0:SUMW], ctx1[:, 0:SUMW]] + [sums_all[4 * b + j][i][:, 0:SUMW] for j in range(q) for i in range(16)]
    pre[0, 50 - len(seq):50] = np.stack(seq)
    seq = [ctx1[:, SUMW:], ctx0[:, SUMW:]] + [sums_all[4 * b + j][i][:, SUMW:] for j in range(3, q, -1) for i in range(15, -1, -1)]
    pre[1, 50 - len(seq):50] = np.stack(seq)
    if ci == 1:
        pre[0, 50] = ctx0[:, 0:SUMW]
    if ci == 0:
        pre[1, 50] = ctx1[:, SUMW:]
    return pre


def cT_of(inp, b):
    cT = np.zeros((128, 8, 2), f32)
    cT[:, :, 0] = inp["c"][b].reshape(8, 128).T
    cT[:, :, 1] = inp["c_ctx"].reshape(8, 128).T
    return cT.reshape(128, 16)


def sel_const():
    sel = np.zeros((2, 256), f32)
    sel[0, 0:128] = 1.0; sel[1, 128:256] = 1.0
    return sel


def l2_inputs(inp, l, xs_all, r1, K, cdec):
    sums_all = [r1[c]["o_sum"] for c in range(8)]
    cdt = np.zeros((64, 512), f32)
    for d in range(2):
        for h in range(4):
            cdt[:, d * 256 + h * 64:d * 256 + (h + 1) * 64] = cdec[d, h]
    vecs = np.zeros((3, 1024), f32)
    vecs[0] = inp["norm_mix_post"][l]
    vecs[1] = np.concatenate([inp["ret_norm"][l], np.tile(inp["gla_norm"][l], 4), inp["rw_ln_w"][l]])
    vecs[2, 640:] = inp["rw_ln_b"][l]
    shared = dict(wmod=np.ascontiguousarray(inp["w_mod"][l][:, 2048:3072]),
                  bmod=np.ascontiguousarray(np.tile(inp["b_mod"][l][None, 2048:3072], (2, 1))),
                  vecs=vecs, wout=np.ascontiguousarray(inp["w_out"][l]), cm=K["c"]["cm"], sel=sel_const(), cdt=cdt)
    ins = []
    for c in range(8):
        d = dict(shared)
        d.update(xs=xs_all[c], o_tok=r1[c]["o_tok"], o_rt=r1[c]["o_rt"], o_sum=r1[c]["o_sum"], pre=build_pre(sums_all, c),
                 cT=cT_of(inp, c // 4))
        ins.append(d)
    return ins


def l3_inputs(inp, l, r2, K):
    shared = dict(wmod=np.ascontiguousarray(inp["w_mod"][l][:, 3072:6144]),
                  bmod=np.ascontiguousarray(np.tile(inp["b_mod"][l][None, 3072:6144], (2, 1))),
                  gpre=np.ascontiguousarray(inp["norm_ffn_pre"][l].reshape(8, 128).T),
                  vecs=np.ascontiguousarray(inp["norm_ffn_post"][l][None]), cm=K["c"]["cm"], sel=sel_const(),
                  wg=np.ascontiguousarray(inp["w_ffn_gate"][l]), wu=np.ascontiguousarray(inp["w_ffn_up"][l]),
                  wd=np.ascontiguousarray(inp["w_ffn_down"][l]))
    ins = []
    for c in range(8):
        d = dict(shared)
        d.update(xs=r2[c]["xmid"], cT=cT_of(inp, c // 4))
        ins.append(d)
    return ins


def gather(r3, xlat, xctx, key="xout"):
    xlat = xlat.copy(); xctx = xctx.copy()
    for c in range(8):
        b, q, ci = core_tokens(c)
        xo = r3[c][key]
        xlat[b, q * 2048:(q + 1) * 2048] = xo[:16].reshape(2048, 1024)
        if q < 2:
            xctx[b, ci * 128:(ci + 1) * 128] = xo[16]
    return xlat, xctx


from concourse.bass_utils import run_bass_kernel_spmd

_CACHE = {}


def _programs():
    if "p" not in _CACHE:
        Kc, cdec = consts()
        _CACHE["p"] = (build_l1(), build_l2(cdec), build_l3(), dict(c=Kc, rope=rope_tables()), cdec)
    return _CACHE["p"]


def kernel(**inputs):
    inp = {k: np.ascontiguousarray(np.asarray(v, dtype=np.float32)) for k, v in inputs.items()}
    nc1, nc2, nc3, K, cdec = _programs()
    xlat, xctx = inp["x"], inp["ctx"]
    cores = list(range(8))
    for l in range(2):
        xs_all = split_tiles(xlat, xctx)
        r1 = run_bass_kernel_spmd(nc1, l1_inputs(inp, l, xlat, xctx, K), core_ids=cores).results
        r2 = run_bass_kernel_spmd(nc2, l2_inputs(inp, l, xs_all, r1, K, cdec), core_ids=cores).results
        r3 = run_bass_kernel_spmd(nc3, l3_inputs(inp, l, r2, K), core_ids=cores).results
        xlat, xctx = gather(r3, xlat, xctx)
    return xlat.astype(np.float32)
```

```python
import contextlib
import numpy as np
import concourse.bass as bass
import concourse.mybir as mybir

F32 = mybir.dt.float32
BF16 = mybir.dt.bfloat16
AF = mybir.ActivationFunctionType
ALU = mybir.AluOpType
AX = mybir.AxisListType


class Res:
    __slots__ = ("w", "r", "name")

    def __init__(self, name=""):
        self.w = None
        self.r = {}
        self.name = name


class V:
    __slots__ = ("ap", "res")

    def __init__(self, ap, res):
        self.ap = ap
        self.res = res

    def __getitem__(self, idx):
        return V(self.ap[idx], self.res)

    def bc(self, shape):
        return V(self.ap.broadcast_to(list(shape)), self.res)

    def rr(self, pat, **kw):
        return V(self.ap.rearrange(pat, **kw), self.res)

    def sub(self, name=""):
        return V(self.ap, Res(name))


class Prog:
    ENG = ("pe", "act", "dve", "pool", "sp")

    def __init__(self, nc):
        self.nc = nc
        self.es = contextlib.ExitStack()
        self.h = {"pe": nc.tensor, "act": nc.scalar, "dve": nc.vector, "pool": nc.gpsimd, "sp": nc.sync}
        self.sem = {}
        self.cnt = {}
        self.ops = {e: [] for e in self.ENG}
        self.seen = {e: {} for e in self.ENG}
        for e in self.ENG:
            self.sem[e] = self.es.enter_context(nc.semaphore("s_" + e))
            self.cnt[e] = 0
        self.dq = {}
        for q in ("sp", "act", "pool"):
            sems = []
            for i in range(4):
                k = "d_%s%d" % (q, i)
                self.sem[k] = self.es.enter_context(nc.semaphore(k))
                self.cnt[k] = 0
                sems.append(k)
            self.dq[q] = [sems, 0]
        self.n_tiles = 0
        self.banks = []
        self.bi = 0

    def init_banks(self, n=7):
        self.banks = [self.ps([128, 512], F32, name="bank%d" % i) for i in range(n)]

    def bk(self):
        b = self.banks[self.bi % len(self.banks)]
        self.bi += 1
        return b

    def sb(self, shape, dtype=F32, name=None):
        self.n_tiles += 1
        name = name or ("t%d" % self.n_tiles)
        t = self.es.enter_context(self.nc.sbuf_tensor(name, list(shape), dtype))
        return V(t[tuple(slice(None) for _ in shape)], Res(name))

    def ps(self, shape, dtype=F32, name=None):
        self.n_tiles += 1
        name = name or ("p%d" % self.n_tiles)
        t = self.es.enter_context(self.nc.psum_tensor(name, list(shape), dtype))
        return V(t[tuple(slice(None) for _ in shape)], Res(name))

    def dram(self, name, shape, dtype=F32, kind="ExternalInput"):
        t = self.nc.dram_tensor(name, list(shape), dtype, kind=kind)
        return V(t.ap(), Res(name))

    def _deps(self, eng, reads, writes, skip_same=False):
        need = {}

        def req(tok):
            if tok is None:
                return
            k, v = tok
            if skip_same and k == eng:
                return
            if need.get(k, 0) < v:
                need[k] = v

        for r in reads:
            req(r.w)
        for w in writes:
            req(w.w)
            for k, v in w.r.items():
                req((k, v))
        waits = []
        seen = self.seen[eng]
        for k, v in need.items():
            if seen.get(k, 0) < v:
                seen[k] = v
                waits.append((k, v))
        return waits

    def _commit(self, tok, reads, writes):
        k, v = tok
        for r in reads:
            if r.r.get(k, 0) < v:
                r.r[k] = v
        for w in writes:
            w.w = tok
            w.r = {}

    def op(self, eng, fn, reads, writes, skip_same=False):
        reads = [x.res for x in reads if x is not None and isinstance(x, V)]
        writes = [x.res for x in writes]
        waits = self._deps(eng, reads, writes, skip_same)
        self.cnt[eng] += 1
        tok = (eng, self.cnt[eng])
        self.ops[eng].append((waits, fn, (eng, 1)))
        self._commit(tok, reads, writes)

    def dma(self, q, out, in_, **kw):
        sems, i = self.dq[q]
        k = sems[i % len(sems)]
        self.dq[q][1] = i + 1
        reads = [in_.res]
        writes = [out.res]
        waits = self._deps(q, reads, writes)
        if self.cnt[k] > 0 and self.seen[q].get(k, 0) < self.cnt[k]:
            self.seen[q][k] = self.cnt[k]
            waits.append((k, self.cnt[k]))
        self.cnt[k] += 16
        tok = (k, self.cnt[k])
        o, i_ = out.ap, in_.ap
        self.ops[q].append((waits, lambda e: e.dma_start(out=o, in_=i_, **kw), (k, 16)))
        self._commit(tok, reads, writes)

    def dma_like(self, q, fn, reads, writes):
        sems, i = self.dq[q]
        k = sems[i % len(sems)]
        self.dq[q][1] = i + 1
        reads = [x.res for x in reads]; writes = [x.res for x in writes]
        waits = self._deps(q, reads, writes)
        if self.cnt[k] > 0 and self.seen[q].get(k, 0) < self.cnt[k]:
            self.seen[q][k] = self.cnt[k]
            waits.append((k, self.cnt[k]))
        self.cnt[k] += 16
        tok = (k, self.cnt[k])
        self.ops[q].append((waits, fn, (k, 16)))
        self._commit(tok, reads, writes)

    def mm(self, out, lhsT, rhs, start=True, stop=True, **kw):
        o, a, b = out.ap, lhsT.ap, rhs.ap
        self.op("pe", lambda e: e.matmul(o, a, b, start=start, stop=stop, **kw), [lhsT, rhs], [out], skip_same=True)

    def tr(self, out, in_, ident):
        o, a, b = out.ap, in_.ap, ident.ap
        self.op("pe", lambda e: e.transpose(o, a, b), [in_, ident], [out], skip_same=True)

    def act(self, out, in_, func, bias=None, scale=None, accum_out=None, eng="act"):
        o, a = out.ap, in_.ap
        kw = {}
        rd = [in_]
        if bias is not None:
            if isinstance(bias, V):
                rd.append(bias); kw["bias"] = bias.ap
            else:
                kw["bias"] = float(bias)
        if scale is not None:
            if isinstance(scale, V):
                rd.append(scale); kw["scale"] = scale.ap
            else:
                kw["scale"] = float(scale)
        wr = [out]
        if accum_out is not None:
            wr.append(accum_out); kw["accum_out"] = accum_out.ap
        self.op("act", lambda e: e.activation(o, a, func, **kw), rd, wr)

    def tt(self, out, in0, in1, op, eng="dve"):
        o, a, b = out.ap, in0.ap, in1.ap
        self.op(eng, lambda e: e.tensor_tensor(o, a, b, op), [in0, in1], [out])

    def ts(self, out, in0, s1, op0, s2=None, op1=None, accum_out=None, eng="dve"):
        o, a = out.ap, in0.ap
        rd = [in0]
        x1 = s1
        if isinstance(s1, V):
            rd.append(s1); x1 = s1.ap
        x2 = s2
        if isinstance(s2, V):
            rd.append(s2); x2 = s2.ap
        kw = {}
        if op1 is not None:
            kw["op1"] = op1
        wr = [out]
        if accum_out is not None:
            wr.append(accum_out); kw["accum_out"] = accum_out.ap
        self.op(eng, lambda e: e.tensor_scalar(o, a, x1, x2, op0, **kw), rd, wr)

    def stt(self, out, in0, scalar, in1, op0, op1, eng="dve"):
        o, a, b = out.ap, in0.ap, in1.ap
        rd = [in0, in1]
        s = scalar
        if isinstance(scalar, V):
            rd.append(scalar); s = scalar.ap
        self.op(eng, lambda e: e.scalar_tensor_tensor(o, a, s, b, op0, op1), rd, [out])

    def copy(self, out, in_, eng="dve"):
        o, a = out.ap, in_.ap
        if eng == "act":
            self.op("act", lambda e: e.copy(o, a), [in_], [out])
        else:
            self.op(eng, lambda e: e.tensor_copy(o, a), [in_], [out])

    def memset(self, out, val, eng="dve"):
        o = out.ap
        self.op(eng, lambda e: e.memset(o, val), [], [out])

    def reduce(self, out, in_, op=ALU.add, axis=AX.X, eng="dve"):
        o, a = out.ap, in_.ap
        self.op(eng, lambda e: e.tensor_reduce(o, a, axis, op), [in_], [out])

    def recip(self, out, in_):
        o, a = out.ap, in_.ap
        self.op("dve", lambda e: e.reciprocal(o, a), [in_], [out])

    def scan(self, out, d0, d1, initial, op0, op1):
        o, a, b = out.ap, d0.ap, d1.ap
        rd = [d0, d1]
        ini = initial
        if isinstance(initial, V):
            rd.append(initial); ini = initial.ap
        self.op("dve", lambda e: e.tensor_tensor_scan(o, a, b, ini, op0, op1), rd, [out])

    def emit(self, final_tokens):
        nc = self.nc
        fw = {}
        for r in final_tokens:
            if r.w is not None:
                k, v = r.w
                fw[k] = max(fw.get(k, 0), v)
        with nc.Block() as block:
            def mk(e):
                def body(engh):
                    for waits, fn, inc in self.ops[e]:
                        for k, v in waits:
                            engh.wait_ge(self.sem[k], v)
                        ins = fn(engh)
                        ins.then_inc(self.sem[inc[0]], inc[1])
                    if e == "sp":
                        for k, v in fw.items():
                            engh.wait_ge(self.sem[k], v)
                return body
            block.tensor(mk("pe"))
            block.scalar(mk("act"))
            block.vector(mk("dve"))
            block.gpsimd(mk("pool"))
            block.sync(mk("sp"))
        self.es.close()

import math

NT = 17
NCOL = 2180
RW0 = 2208
RCH = [(0, 128), (128, 256), (256, 384), (384, 512), (512, 640), (640, 768), (768, 896), (896, 1024), (1024, 1152),
       (1152, 1280), (1280, 1344), (1344, 1472)]
SUMW = 1412
SOFF_RET, SOFF_GLA, SOFF_RW = 0, 256, 256 + 388


def tile_col0(i):
    return 1 + 128 * i if i < 16 else 2051


def build_l1():
    nc = bass.Bass("TRN2", target_bir_lowering=False)
    P = Prog(nc)
    D = P.dram
    xs = D("xs", [NT, 128, 1024]); xh = D("xh", [34, 1024]); hfl = D("hfl", [128, 34]); cT = D("cT", [128, 16])
    wmod = D("wmod", [1024, 2048]); bmod = D("bmod", [2, 2048]); gpre = D("gpre", [128, 8]); win = D("win", [1024, 3680])
    convp = D("convp", [128, 36]); cm = D("cm", [128, 6, 128]); rope = D("rope", [NT, 128, 1024])
    retD = D("retD", [128, 8, 128]); retQD = D("retQD", [128, 8, 128]); retkd = D("retkd", [128, 8])
    gwa = D("gwa", [16, 384]); gba = D("gba", [1, 384])
    rw2 = D("rw2", [128, 384]); rw0 = D("rw0", [1, 768]); ra2 = D("ra2", [64, 384]); ra0 = D("ra0", [1, 384])
    rg2 = D("rg2", [128, 384]); rvec = D("rvec", [3, 384]); lvd = D("lv", [128, 7, 384])
    o_tok = D("o_tok", [NT, 128, 2432], kind="ExternalOutput")
    o_tokA, o_tokB = o_tok, o_tok
    o_rt = D("o_rt", [NT, 64, 28, 128], kind="ExternalOutput")
    o_sum = D("o_sum", [NT, 64, 2 * SUMW], kind="ExternalOutput")
    P.init_banks(6)
    pO = P.ps([128, 512], F32, name="pO")
    ptb = P.ps([128, 8, 128], BF16, name="ptb")

    cmt = P.sb([128, 6, 128], name="cmt"); P.dma("sp", cmt, cm)
    ident, ones = cmt[:, 0, :], cmt[:, 5, :]
    TRI = [cmt[:, 1, :], cmt[:, 2, :]]
    STR = [cmt[:, 3, :], cmt[:, 4, :]]
    identb = P.sb([128, 128], BF16, name="identb"); P.copy(identb, ident)
    MASK1 = P.sb([128, 2, 256], name="mask1")
    for d in range(2):
        P.copy(MASK1[:, d, 0:128], STR[d], eng="pool"); P.copy(MASK1[:, d, 128:256], TRI[d], eng="pool")
    retDt = P.sb([128, 8, 128], name="retDt"); P.dma("act", retDt, retD)
    retQDt = P.sb([128, 8, 128], name="retQDt"); P.dma("act", retQDt, retQD)
    retkdt = P.sb([128, 8], name="retkdt"); P.dma("act", retkdt, retkd)
    gwat = P.sb([16, 384], name="gwat"); P.dma("sp", gwat, gwa)
    gbat = P.sb([1, 384], name="gbat"); P.dma("sp", gbat, gba)
    rw2t = P.sb([128, 384], name="rw2t"); P.dma("sp", rw2t, rw2)
    rw0t = P.sb([1, 768], name="rw0t"); P.dma("sp", rw0t, rw0)
    ra2t = P.sb([64, 384], name="ra2t"); P.dma("sp", ra2t, ra2)
    ra0t = P.sb([1, 384], name="ra0t"); P.dma("sp", ra0t, ra0)
    rg2t = P.sb([128, 384], name="rg2t"); P.dma("sp", rg2t, rg2)
    rvt = P.sb([128, 3, 384], name="rvt")
    for j in range(3):
        P.dma("act", rvt[:, j, :], V(rvec.ap[j].partition_broadcast(128), rvec.res))
    cvt = P.sb([128, 36], name="cvt"); P.dma("act", cvt, convp)
    hflt = P.sb([128, 34], name="hflt"); P.dma("act", hflt, hfl)
    gpret = P.sb([128, 8], name="gpret"); P.dma("act", gpret, gpre)
    cTt = P.sb([128, 16], name="cTt"); P.dma("sp", cTt, cT)

    scs = P.sb([128, 16], name="scs")
    P.act(scs, cTt, AF.Silu)
    scv = scs.rr("p (k s) -> p k s", s=2)
    u = P.sb([128, 2208], name="u")
    modrow = u[0:2, 0:2048]
    bms = [P.sb([2, 128], name="bms%d" % i) for i in range(2)]
    wst = [P.sb([128, 8, 128], name="wst%d" % i) for i in range(2)]
    wmr = V(wmod.ap.rearrange("(k p) n -> p k n", p=128), wmod.res)
    for cb in range(16):
        st = wst[cb % 2]
        P.dma("sp" if cb % 2 == 0 else "act", st, wmr[:, :, cb * 128:(cb + 1) * 128])
        pb = P.bk()
        for k in range(8):
            P.mm(pb[0:2, 0:128], scv[:, k, :], st[:, k, :], start=(k == 0), stop=(k == 7))
        P.dma("pool", bms[cb % 2], bmod[:, cb * 128:(cb + 1) * 128])
        P.tt(modrow[:, cb * 128:(cb + 1) * 128], pb[0:2, 0:128], bms[cb % 2], ALU.add)
    pm = P.bk()
    for w in range(2):
        for k in range(8):
            P.mm(pm[:, (w * 8 + k) * 2:(w * 8 + k) * 2 + 2], modrow[0:2, w * 1024 + k * 128: w * 1024 + (k + 1) * 128], ident[0:2, 0:2])
    modp = P.sb([128, 2, 8, 2], name="modp")
    P.copy(modp.rr("p a k s -> p (a k s)"), pm[:, 0:32])
    gs = P.sb([128, 8, 2], name="gs")
    P.ts(gs, modp[:, 1, :, :], 1.0, ALU.add)
    P.tt(gs, gs, gpret.rr("p (k o) -> p k o", o=1).bc([128, 8, 2]), ALU.mult)
    sh = modp[:, 0, :, :]

    Wb = P.sb([128, 8, 2208], BF16, name="Wb")
    winr = V(win.ap.rearrange("(k p) n -> p k n", p=128), win.res)

    def load_w(lo, hi):
        n = (hi - lo + 127) // 128
        for cb in range(n):
            c0, c1 = lo + cb * 128, min(hi, lo + cb * 128 + 128)
            st = wst[cb % 2]
            P.dma("sp" if cb % 2 == 0 else "act", st[:, :, 0:c1 - c0], winr[:, :, c0:c1])
            P.copy(Wb[:, :, c0 - lo:c1 - lo], st[:, :, 0:c1 - c0], eng=("pool" if cb % 2 == 0 else "dve"))

    xb = [P.sb([128, 1024], name="xb%d" % i) for i in range(2)]
    xn = P.sb([128, 1024], BF16, name="xn")
    ss = P.sb([128, 1], name="ss"); rstd = P.sb([128, 1], name="rstd")

    def norm_rows(xt, n):
        P.act(xn[0:n, :], xt, AF.Square, accum_out=ss[0:n, :])
        P.act(rstd[0:n, :], ss[0:n, :], AF.Sqrt, bias=1e-6, scale=1.0 / 1024)
        P.recip(rstd[0:n, :], rstd[0:n, :])
        P.ts(xn[0:n, :], xt, rstd[0:n, :], ALU.mult)

    xht = P.sb([34, 1024], name="xht"); P.dma("sp", xht, xh)
    norm_rows(xht, 34)
    for k in range(8):
        P.tr(ptb[:, k, 0:34], xn[0:34, k * 128:(k + 1) * 128], identb[0:34, 0:34])
    hh = P.sb([128, 8, 34], name="hh")
    for s, (j0, j1) in enumerate(((0, 32), (32, 34))):
        P.tt(hh[:, :, j0:j1], ptb[:, :, j0:j1], gs[:, :, s:s + 1].bc([128, 8, j1 - j0]), ALU.mult)
        P.tt(hh[:, :, j0:j1], hh[:, :, j0:j1], sh[:, :, s:s + 1].bc([128, 8, j1 - j0]), ALU.add)
    P.tt(hh, hh, hflt.rr("p (o j) -> p o j", o=1).bc([128, 8, 34]), ALU.mult)
    hTb = [P.sb([128, 8, 130], BF16, name="hT%d" % i) for i in range(2)]
    hcnt = [0]

    def make_hT(i):
        n = hcnt[0]; hcnt[0] += 1
        hT = hTb[n % 2]
        xt = xb[n % 2]
        P.dma("sp" if n % 2 == 0 else "act", xt, xs[i])
        norm_rows(xt, 128)
        for k in range(8):
            P.tr(ptb[:, k, :], xn[:, k * 128:(k + 1) * 128], identb)
        s = 0 if i < 16 else 1
        for k in range(8):
            P.act(hT[:, k, 1:129], ptb[:, k, :], AF.Identity, scale=gs[:, k, s:s + 1], bias=sh[:, k, s:s + 1])
        P.copy(hT[:, :, 0:1], hh[:, :, 2 * i:2 * i + 1], eng="pool")
        P.copy(hT[:, :, 129:130], hh[:, :, 2 * i + 1:2 * i + 2], eng="pool")
        return hT

    finals = []
    ropet = P.sb([128, 1024], name="ropet")
    otok = P.sb([128, 1280], name="otok")
    RTt = P.sb([64, 16, 128], name="RTt")
    SUMt = P.sb([64, 1536], name="SUMt")
    qkT = P.sb([128, 4, 128], name="qkT")
    PT = [P.sb([128, 128], name="PT%d" % i) for i in range(4)]
    lrT = P.sb([16, 2, 128], name="lrT")
    gT = [P.sb([48, 2, 128], name="gT%d" % i) for i in range(4)]
    ucT = P.sb([128, 9, 128], name="ucT"); ctmp = P.sb([128, 128], name="ctmp"); ctmp2 = P.sb([128, 128], name="ctmp2")
    lrw = P.sb([128, 128], name="lrw"); lra = P.sb([64, 128], name="lra"); lrg = P.sb([128, 128], name="lrg")
    r_t, k_t, v_t, a_t, kk = (u[:, j * 384:(j + 1) * 384] for j in range(5))
    km = P.sb([128, 384], name="km")
    beta = P.sb([128, 384], name="beta"); t384 = P.sb([128, 384], name="t384"); s6 = P.sb([128, 6], name="s6")
    ld = P.sb([128, 384], name="ld"); E1 = P.sb([128, 384], name="E1"); E2 = P.sb([128, 384], name="E2")
    E3 = P.sb([128, 384], name="E3"); Et = P.sb([128, 384], name="Et")
    gla_la, gE1, gE2, gEt, gqt, gkh, gKt = (x[:, 0:192] for x in (ld, E1, E2, Et, E3, km, beta))
    Q4 = P.sb([128, 4, 384], name="Q4")
    Q4f = Q4.rr("p a b -> p (a b)")
    qk, rtmp, kd = Q4f[:, 0:512], Q4f[:, 512:1024], Q4f[:, 1024:1280]
    Bt = P.sb([128, 384], name="Bt"); Kt = P.sb([128, 384], name="Kt")
    FT = P.sb([128, 3, 4, 128], name="FT")
    DEZ = P.sb([64, 6, 128], name="DEZ"); P.memset(DEZ, 0.0, eng="pool")
    NM = [P.sb([128, 256], name="NM%d" % i) for i in range(3)]
    BM = [P.sb([128, 256], name="BM%d" % i) for i in range(3)]
    NA = [[P.sb([128, 256], name="NA%d_%d" % (i, j)) for j in range(1)] for i in range(3)]
    NAs = [P.sb([128, 256], name="NAs%d" % i) for i in range(3)]
    WM = [[P.sb([128, 256], name="WM%d_%d" % (i, j)) for j in range(2)] for i in range(3)]
    P12 = [P.sb([128, 256], name="P12_%d" % i) for i in range(3)]
    lvt = P.sb([128, 7, 384], name="lvt"); P.dma("sp", lvt, lvd)
    II = P.sb([128, 256], name="II")
    P.copy(II[:, 0:128], ident, eng="pool"); P.copy(II[:, 128:256], ident, eng="pool")
    Z = [[P.sb([128, 128], name="Z%d_%d" % (i, j)) for j in range(2)] for i in range(3)]
    C1 = math.exp(-0.5)

    def dout(dst, src):
        d2 = dst.sub()
        P.dma("sp", d2, src)
        finals.append(d2.res)

    load_w(0, 2208)
    for i in range(NT):
        hT = make_hT(i)
        P.dma("act", ropet, rope[i])
        for (a, b) in ((0, 512), (512, 1024), (1024, 1536), (1536, 2048), (2048, 2208)):
            pb = P.bk()
            for k in range(8):
                P.mm(pb[:, 0:b - a], hT[:, k, 1:129], Wb[:, k, a:b], start=(k == 0), stop=(k == 7))
            if (a // 512) % 2 == 0:
                P.copy(u[:, a:b], pb[:, 0:b - a], eng="act")
            else:
                P.copy(u[:, a:b], pb[:, 0:b - a])
        uq = u[:, 0:512]
        uq4 = uq.rr("p (a b c) -> p a b c", b=2, c=16)
        sn4 = ropet[:, 512:1024].rr("p (a b c) -> p a b c", b=2, c=16)
        rt4 = rtmp.rr("p (a b c) -> p a b c", b=2, c=16)
        P.tt(rt4[:, :, 0, :], uq4[:, :, 1, :], sn4[:, :, 0, :], ALU.mult, eng="pool")
        P.tt(rt4[:, :, 1, :], uq4[:, :, 0, :], sn4[:, :, 1, :], ALU.mult, eng="pool")
        P.tt(qk, uq, ropet[:, 0:512], ALU.mult, eng="pool")
        P.tt(qk, qk, rtmp, ALU.add, eng="pool")
        pb = P.bk()
        for c in range(4):
            P.tr(pb[:, c * 128:(c + 1) * 128], qk[:, c * 128:(c + 1) * 128], ident)
        P.copy(qkT.rr("p a b -> p (a b)"), pb, eng="act")
        P.act(otok[:, 640:896], u[:, 768:1024], AF.Silu)
        for d in range(2):
            P.tt(kd.rr("p (h c) -> p h c", c=64), qk[:, 256:512].rr("p (h c) -> p h c", c=64),
                 retkdt[:, d * 4:d * 4 + 4].rr("p (h o) -> p h o", o=1).bc([128, 4, 64]), ALU.mult, eng="pool")
            po = pO
            pSs, prs = {}, {}
            for h in range(4):
                c, pbase = h // 2, (h % 2) * 64
                pS = P.bk(); pSs[h] = pS
                P.mm(pS[:, 0:128], qkT[pbase:pbase + 64, 2 + c, :], qkT[pbase:pbase + 64, c, :])
            for h in range(4):
                P.tt(PT[h], pSs[h][:, 0:128], retDt[:, d * 4 + h, :], ALU.mult)
            for h in range(4):
                P.mm(po[:, h * 64:(h + 1) * 64], PT[h], u[:, 512 + h * 64:512 + (h + 1) * 64])
            for h in range(4):
                pr = P.bk(); prs[h] = pr
                P.mm(pr[0:64, 0:128], qk[:, h * 64:(h + 1) * 64], retQDt[:, d * 4 + h, :])
                P.mm(pr[0:64, 128:192], kd[:, h * 64:(h + 1) * 64], u[:, 512 + h * 64:512 + (h + 1) * 64])
            for h in range(4):
                P.copy(RTt[:, d * 8 + h, :], prs[h][0:64, 0:128], eng="act")
                P.copy(SUMt[:, d * 644 + h * 64: d * 644 + (h + 1) * 64], prs[h][0:64, 128:192], eng="act")
            if d == 0:
                P.copy(otok[:, 0:256], po[:, 0:256])
            else:
                P.tt(otok[:, 0:256], otok[:, 0:256], po[:, 0:256], ALU.add)

        pb = P.bk()
        for d in range(2):
            P.tr(pb[0:16, d * 128:(d + 1) * 128], u[:, 2176 + d * 16:2176 + (d + 1) * 16], ident)
        P.copy(lrT.rr("p a b -> p (a b)"), pb[0:16, 0:256], eng="act")
        P.act(otok[:, 896:1280], u[:, 1792:2176], AF.Silu)
        for d in range(2):
            pz = P.bk()
            P.mm(pz[:, 0:192], lrT[:, d, :], gwat[:, d * 192:(d + 1) * 192], start=True, stop=False)
            P.mm(pz[:, 0:192], ones[0:1, :], gbat[0:1, d * 192:(d + 1) * 192], start=False, stop=True)
            P.act(gE1, pz[:, 0:192], AF.Exp, scale=-1.0)
            P.act(gE2, gE1, AF.Ln, bias=1.0)
            P.ts(gla_la, gE2, -1.0 / 16.0, ALU.mult)
            pc = P.bk()
            P.mm(pc[:, 0:192], TRI[d], gla_la)
            P.mm(pc[:, 192:384], ones, gla_la)
            P.act(gE1, pc[:, 0:192], AF.Exp)
            P.act(gE2, pc[:, 0:192], AF.Exp, scale=-1.0)
            P.act(gEt, pc[:, 192:384], AF.Exp)
            P.stt(gqt, u[:, 1024:1216], 48.0 ** -0.5, gE1, ALU.mult, ALU.mult)
            P.tt(gkh, u[:, 1216:1408], gE2, ALU.mult, eng="pool")
            P.tt(gKt, gkh, gEt, ALU.mult, eng="pool")
            po = pO
            pqs, pSs, phs = {}, {}, {}
            for h in range(4):
                hs = slice(h * 48, (h + 1) * 48)
                pq = P.bk(); pqs[h] = pq
                P.tr(pq[0:48, 0:128], gqt[:, hs], ident)
                P.tr(pq[0:48, 128:256], gkh[:, hs], ident)
            for h in range(4):
                P.copy(gT[h].rr("p a b -> p (a b)"), pqs[h][0:48, 0:256], eng="act")
                P.copy(RTt[0:48, d * 8 + 4 + h, :], pqs[h][0:48, 0:128], eng="act")
            for h in range(4):
                pS = P.bk(); pSs[h] = pS
                P.mm(pS[:, 0:128], gT[h][:, 1, :], gT[h][:, 0, :])
            for h in range(4):
                P.tt(PT[h], pSs[h][:, 0:128], TRI[d], ALU.mult)
            for h in range(4):
                hs = slice(h * 48, (h + 1) * 48)
                P.mm(po[:, h * 96:(h + 1) * 96], PT[h], u[:, 1408 + h * 96:1408 + (h + 1) * 96])
                ph = P.bk(); phs[h] = ph
                P.mm(ph[0:48, 0:96], gKt[:, hs], u[:, 1408 + h * 96:1408 + (h + 1) * 96])
                P.mm(ph[0:48, 96:97], gla_la[:, hs], ones[:, 0:1])
            for h in range(4):
                so = d * 644 + 256 + h * 97
                P.copy(SUMt[0:48, so:so + 96], phs[h][0:48, 0:96], eng="act")
                P.act(SUMt[0:48, so + 96:so + 97], phs[h][0:48, 96:97], AF.Exp)
            if d == 0:
                P.copy(otok[:, 256:640], po[:, 0:384])
            else:
                P.tt(otok[:, 256:640], otok[:, 256:640], po[:, 0:384], ALU.add)

        for (c0, c1), (s0, s1) in (((0, 640), (0, 640)), ((1024, 1664), (640, 1280))):
            dout(o_tok[i][:, c0:c1], otok[:, s0:s1])
        for d in range(2):
            dout(o_rt[i][:, d * 14:d * 14 + 8, :], RTt[:, d * 8:d * 8 + 8, :])
            dout(o_sum[i][:, d * SUMW:d * SUMW + 644], SUMt[:, d * 644:(d + 1) * 644])

    load_w(RW0, 3680)
    for i in range(NT):
        hT = make_hT(i)
        for j, (a, b) in enumerate(RCH):
            M = b - a
            pb = P.bk()
            for k in range(8):
                P.mm(pb[0:M, 0:130], Wb[:, k, a:b], hT[:, k, 0:130], start=(k == 0), stop=(k == 7))
            P.ts(ctmp[0:M, :], pb[0:M, 0:128], cvt[0:M, 3 * j:3 * j + 1], ALU.mult)
            P.stt(ctmp2[0:M, :], pb[0:M, 1:129], cvt[0:M, 3 * j + 1:3 * j + 2], ctmp[0:M, :], ALU.mult, ALU.add)
            if j < 9:
                dst = ucT[:, j, :]
            elif j == 9:
                dst = ctmp[0:M, :]
            elif j == 10:
                dst = lra
            else:
                dst = ctmp[0:M, :]
            P.stt(dst, pb[0:M, 2:130], cvt[0:M, 3 * j + 2:3 * j + 3], ctmp2[0:M, :], ALU.mult, ALU.add)
            if j == 9:
                P.act(lrw, ctmp, AF.Tanh)
            if j == 11:
                P.act(lrg, ctmp, AF.Sigmoid)
        for g3, dstt in enumerate((r_t, k_t, v_t)):
            pb = P.bk()
            for c in range(3):
                P.tr(pb[:, c * 128:(c + 1) * 128], ucT[:, g3 * 3 + c, :], ident)
            P.copy(dstt, pb[:, 0:384], eng=("act" if g3 == 1 else "dve"))

        pa = P.bk()
        P.mm(pa[:, 0:384], lra, ra2t, start=True, stop=False)
        P.mm(pa[:, 0:384], ones[0:1, :], ra0t, start=False, stop=True)
        P.act(a_t, pa[:, 0:384], AF.Sigmoid)
        pg = P.bk()
        P.mm(pg[:, 0:384], lrg, rg2t)
        P.copy(otok[:, 384:768], pg[:, 0:384], eng="act")
        P.tt(kk, k_t, rvt[:, 0, :], ALU.mult, eng="pool")
        P.tt(t384, kk, kk, ALU.mult, eng="pool")
        P.reduce(s6, t384.rr("p (h c) -> p h c", c=64))
        P.ts(s6, s6, 1e-24, ALU.max)
        P.act(s6, s6, AF.Sqrt)
        P.recip(s6, s6)
        P.tt(kk.rr("p (h c) -> p h c", c=64), kk.rr("p (h c) -> p h c", c=64),
             s6.rr("p (h o) -> p h o", o=1).bc([128, 6, 64]), ALU.mult)
        P.stt(t384, a_t, -1.0, rvt[:, 1, :], ALU.add, ALU.mult)
        P.stt(km, t384, 1.0, k_t, ALU.add, ALU.mult)
        P.tt(beta, kk, a_t, ALU.mult, eng="pool")
        P.tt(t384, r_t, km, ALU.mult, eng="pool")
        P.tt(t384, t384, rvt[:, 2, :], ALU.mult, eng="pool")
        P.reduce(s6, t384.rr("p (h c) -> p h c", c=64))
        P.tt(otok[:, 768:1152].rr("p (h c) -> p h c", c=64), v_t.rr("p (h c) -> p h c", c=64),
             s6.rr("p (h o) -> p h o", o=1).bc([128, 6, 64]), ALU.mult)
        for d in range(2):
            pz = P.bk()
            P.mm(pz[:, 0:384], lrw[d * 64:(d + 1) * 64, :], rw2t[d * 64:(d + 1) * 64, :], start=True, stop=False)
            P.mm(pz[:, 0:384], ones[0:1, :], rw0t[0:1, d * 384:(d + 1) * 384], start=False, stop=True)
            P.act(ld, pz[:, 0:384], AF.Sigmoid)
            P.ts(ld, ld, -C1, ALU.mult)
            pc = P.bk(); ptot = P.bk()
            P.mm(pc[:, 0:384], TRI[d], ld)
            P.mm(ptot[:, 0:384], ones, ld)
            P.act(E1, pc[:, 0:384], AF.Exp)
            P.act(E2, pc[:, 0:384], AF.Exp, scale=-1.0)
            P.tt(E3, pc[:, 0:384], ld, ALU.subtract)
            P.act(E3, E3, AF.Exp)
            P.act(Et, ptot[:, 0:384], AF.Exp)
            P.tt(Q4[:, 0, :], beta, E2, ALU.mult)
            P.tt(Q4[:, 1, :], km, E2, ALU.mult, eng="pool")
            P.stt(Q4[:, 2, :], kk, -1.0, E3, ALU.mult, ALU.mult)
            P.tt(Q4[:, 3, :], r_t, E1, ALU.mult, eng="pool")
            P.tt(Bt, Q4[:, 0, :], Et, ALU.mult)
            P.tt(Kt, Q4[:, 1, :], Et, ALU.mult, eng="pool")
            P.tt(DEZ[:, :, 0:64], ident[0:64, 0:64].rr("p (o c) -> p o c", o=1).bc([64, 6, 64]),
                 Et[0:64, :].rr("p (h c) -> p h c", c=64), ALU.mult)
            for c in range(3):
                pb = P.bk()
                for q in range(4):
                    P.tr(pb[:, q * 128:(q + 1) * 128], Q4[:, q, c * 128:(c + 1) * 128], ident)
                P.copy(FT[:, c, :, :].rr("p a b -> p (a b)"), pb, eng=("act" if c % 2 == 0 else "dve"))
            po = pO
            mo = 0 if d == 0 else 128
            for grp in ((0, 1, 2), (3, 4, 5)):
                for h in grp:
                    c, pbase = h // 2, (h % 2) * 64
                    hs = slice(h * 64, (h + 1) * 64)
                    bhT = FT[pbase:pbase + 64, c, 0, :]; khT = FT[pbase:pbase + 64, c, 1, :]
                    alT = FT[pbase:pbase + 64, c, 2, :]
                    alrT = FT[pbase:pbase + 64, c, 2:4, :].rr("p a b -> p (a b)")
                    nm, bm, na, zz = NM[h % 3], BM[h % 3], NA[h % 3], Z[h % 3]
                    p1 = P.bk(); P.mm(p1[:, 0:256], bhT, alrT)
                    P.tt(nm, p1[:, 0:256], MASK1[:, d, :], ALU.mult)
                    p2 = P.bk(); P.mm(p2[:, 0:256], khT, alrT)
                    P.tt(bm, p2[:, 0:256], MASK1[:, d, :], ALU.mult)
                for h in grp:
                    c, pbase = h // 2, (h % 2) * 64
                    hs = slice(h * 64, (h + 1) * 64)
                    bhT = FT[pbase:pbase + 64, c, 0, :]; alT = FT[pbase:pbase + 64, c, 2, :]
                    nm, bm, na, zz = NM[h % 3], BM[h % 3], NA[h % 3], Z[h % 3]
                    p3 = P.bk(); P.mm(p3[:, 0:128], alT, bhT)
                    P.copy(na[0][:, 0:128], nm[:, 0:128], eng="pool")
                    P.tt(na[0][:, 128:256], p3[:, 0:128], STR[1 - d], ALU.mult)
                    p4 = P.bk(); P.mm(p4[:, 0:64], bm[:, 0:128], v_t[:, hs])
                    P.copy(zz[0][:, 0:64], Q4[:, 2, hs], eng="pool")
                    P.copy(zz[0][:, 64:128], p4[:, 0:64], eng="act")
                for h in grp:
                    P.tt(NAs[h % 3], NA[h % 3][0], lvt[:, 0, mo:mo + 256], ALU.mult, eng="pool")
                    P.tt(WM[h % 3][0], NAs[h % 3], II, ALU.add, eng="pool")
                cur = 0
                for lv_ in range(1, 7):
                    last = (lv_ == 6)
                    pPs = {}
                    for h in grp:
                        nas, wm = NAs[h % 3], WM[h % 3]
                        P.tt(nas, NA[h % 3][0], lvt[:, lv_, mo:mo + 256], ALU.mult, eng="pool")
                    for h in grp:
                        nas, wm = NAs[h % 3], WM[h % 3]
                        pP = P.bk(); pPs[h] = pP
                        P.mm(pP[:, 0:128], nas[:, 128:256], wm[cur][:, 0:128])
                        if not last:
                            P.mm(pP[:, 128:256], nas[:, 0:128], wm[cur][:, 128:256])
                    for h in grp:
                        if not last:
                            P.copy(P12[h % 3], pPs[h][:, 0:256], eng="act")
                        else:
                            P.copy(P12[h % 3][:, 0:128], pPs[h][:, 0:128], eng="act")
                    pUs = {}
                    for h in grp:
                        wm, p12 = WM[h % 3], P12[h % 3]
                        pU = P.bk(); pUs[h] = pU
                        P.mm(pU[:, 0:128], wm[cur][:, 128:256], p12[:, 0:128])
                        if not last:
                            P.mm(pU[:, 128:256], wm[cur][:, 0:128], p12[:, 128:256])
                    for h in grp:
                        wm = WM[h % 3]
                        if not last:
                            P.tt(wm[1 - cur], pUs[h][:, 0:256], wm[cur], ALU.add)
                        else:
                            P.tt(wm[1 - cur][:, 0:128], pUs[h][:, 0:128], wm[cur][:, 0:128], ALU.add)
                    cur = 1 - cur
                pzs = {}
                for h in grp:
                    pz2 = P.bk(); pzs[h] = pz2
                    P.mm(pz2[:, 0:128], WM[h % 3][cur][:, 0:128], Z[h % 3][0])
                for h in grp:
                    P.copy(Z[h % 3][1], pzs[h][:, 0:128], eng="act")
                for h in grp:
                    hs = slice(h * 64, (h + 1) * 64)
                    nm, bm = NM[h % 3], BM[h % 3]
                    zf = Z[h % 3][1]
                    X, Y = zf[:, 0:64], zf[:, 64:128]
                    mbT, mkT = nm[:, 128:256], bm[:, 128:256]
                    pr = P.bk()
                    P.mm(pr[0:64, 0:128], Q4[:, 3, hs], ident, start=True, stop=False)
                    P.mm(pr[0:64, 0:128], X, mbT, start=False, stop=True)
                    P.copy(RTt[:, d * 6 + h, :], pr[0:64, 0:128], eng="act")
                    P.mm(po[:, hs], mbT, Y, start=True, stop=False)
                    P.mm(po[:, hs], mkT, v_t[:, hs], start=False, stop=True)
                    pgh = P.bk()
                    P.mm(pgh[0:64, 0:64], X, Bt[:, hs])
                    P.mm(pgh[0:64, 64:128], Bt[:, hs], Y, start=True, stop=False)
                    P.mm(pgh[0:64, 64:128], Kt[:, hs], v_t[:, hs], start=False, stop=True)
                    so = d * 768 + h * 128
                    P.tt(SUMt[:, so:so + 128], pgh[0:64, 0:128], DEZ[:, h, :], ALU.add)
            if d == 0:
                P.copy(otok[:, 0:384], po[:, 0:384])
            else:
                P.tt(otok[:, 0:384], otok[:, 0:384], po[:, 0:384], ALU.add)
        for (c0, c1), (s0, s1) in (((640, 1024), (0, 384)), ((1664, 2432), (384, 1152))):
            dout(o_tok[i][:, c0:c1], otok[:, s0:s1])
        for d in range(2):
            dout(o_rt[i][:, d * 14 + 8:d * 14 + 14, :], RTt[:, d * 6:d * 6 + 6, :])
            dout(o_sum[i][:, d * SUMW + 644:(d + 1) * SUMW], SUMt[:, d * 768:(d + 1) * 768])
    P.emit(finals)
    return nc


NPRE = 50


def mod_setup(P, cT):
    cTt = P.sb([128, 16], name="cTt"); P.dma("sp", cTt, cT)
    scs = P.sb([128, 16], name="scs")
    P.act(scs, cTt, AF.Silu)
    scv = scs.rr("p (k s) -> p k s", s=2)
    modrow = P.sb([2, 1024], name="modrow")
    bms = [P.sb([2, 128], name="bms%d" % i) for i in range(2)]
    return scv, bms, modrow


def mod_block(P, ms, wmod, bmod, c0, wst):
    scv, bms, modrow = ms
    wmr = V(wmod.ap.rearrange("(k p) n -> p k n", p=128), wmod.res)
    for cb in range(8):
        st = wst[cb % 2]
        cc = c0 + cb * 128
        P.dma("sp" if cb % 2 == 0 else "act", st, wmr[:, :, cc:cc + 128])
        pb = P.bk()
        for k in range(8):
            P.mm(pb[0:2, 0:128], scv[:, k, :], st[:, k, :], start=(k == 0), stop=(k == 7))
        P.dma("pool", bms[cb % 2], bmod[:, cc:cc + 128])
        P.tt(modrow[:, cb * 128:(cb + 1) * 128], pb[0:2, 0:128], bms[cb % 2], ALU.add)
    return modrow


def load_w_bf16(P, dst, src, kch, ncols, wst):
    sr = V(src.ap.rearrange("(k p) n -> p k n", p=128), src.res)
    n = 0
    for k0 in range(0, kch, 8):
        k1 = min(kch, k0 + 8)
        for c0 in range(0, ncols, 128):
            c1 = min(ncols, c0 + 128)
            st = wst[n % 2]
            P.dma("sp" if n % 2 == 0 else "act", st[:, 0:k1 - k0, 0:c1 - c0], sr[:, k0:k1, c0:c1])
            P.copy(dst[:, k0:k1, c0:c1], st[:, 0:k1 - k0, 0:c1 - c0], eng=("pool" if n % 2 == 0 else "dve"))
            n += 1


def rms_finish(P, py, xt, gg, xo, tmp, ss2, rstd, junk):
    for cb in range(2):
        P.act(junk[:, cb * 512:(cb + 1) * 512], py[cb], AF.Square, accum_out=ss2[:, cb:cb + 1])
    P.tt(rstd, ss2[:, 0:1], ss2[:, 1:2], ALU.add)
    P.act(rstd, rstd, AF.Sqrt, bias=1e-6, scale=1.0 / 1024)
    P.recip(rstd, rstd)
    for cb in range(2):
        P.stt(tmp[:, cb * 512:(cb + 1) * 512], py[cb], rstd, gg[:, cb * 512:(cb + 1) * 512], ALU.mult, ALU.mult)
    P.tt(xo, tmp, xt, ALU.add, eng="pool")


def build_l2(cdec):
    nc = bass.Bass("TRN2", target_bir_lowering=False)
    P = Prog(nc)
    D = P.dram
    xs = D("xs", [NT, 128, 1024]); o_tok = D("o_tok", [NT, 128, 2432]); o_rt = D("o_rt", [NT, 64, 28, 128])
    o_sum = D("o_sum", [NT, 64, 2 * SUMW]); pre = D("pre", [2, NPRE + 1, 64, SUMW])
    cT = D("cT", [128, 16]); wmod = D("wmod", [1024, 1024]); bmod = D("bmod", [2, 1024])
    vecs = D("vecs", [3, 1024]); wout = D("wout", [1024, 1024]); cm = D("cm", [128, 6, 128]); sel = D("sel", [2, 256])
    cdtd = D("cdt", [64, 512])
    xmid = D("xmid", [NT, 128, 1024], kind="ExternalOutput")
    scr = D("scr", [NT, 128, 1024], kind="ExternalOutput")
    P.init_banks(4)
    pA = P.ps([128, 512], F32, name="pA"); pB = P.ps([128, 512], F32, name="pB"); pC = P.ps([128, 512], F32, name="pC")
    ptb = P.ps([128, 8, 128], BF16, name="ptb")
    finals = []

    cmt = P.sb([128, 6, 128], name="cmt"); P.dma("sp", cmt, cm)
    ident = cmt[:, 0, :]
    identb = P.sb([128, 128], BF16, name="identb"); P.copy(identb, ident)
    selt = P.sb([2, 256], name="selt"); P.dma("sp", selt, sel)
    cdt = P.sb([64, 512], name="cdtt"); P.dma("sp", cdt, cdtd)
    vb = P.sb([128, 3, 1024], name="vb")
    for j in range(3):
        P.dma("act", vb[:, j, :], V(vecs.ap[j].partition_broadcast(128), vecs.res))
    wst = [P.sb([128, 8, 128], name="wst%d" % i) for i in range(2)]
    ms = mod_setup(P, cT)
    modrow = mod_block(P, ms, wmod, bmod, 0, wst)
    gg = P.sb([128, 2, 1024], name="gg")
    for s in range(2):
        for cb in range(2):
            pb = P.bk()
            P.mm(pb, selt[0:2, s * 128:(s + 1) * 128], modrow[0:2, cb * 512:(cb + 1) * 512])
            P.tt(gg[:, s, cb * 512:(cb + 1) * 512], pb, vb[:, 0, cb * 512:(cb + 1) * 512], ALU.mult)
    Wo = P.sb([128, 8, 1024], BF16, name="Wo")
    load_w_bf16(P, Wo, wout, 8, 1024, wst)

    STS = {}
    for seg in ("lat", "ctx"):
        for d in range(2):
            STS[(seg, d)] = (P.sb([64, 4, 64], name="STr_%s%d" % (seg, d)), P.sb([64, 4, 96], name="STg_%s%d" % (seg, d)),
                             P.sb([64, 6, 64], name="STw_%s%d" % (seg, d)))
    SM = [P.sb([64, SUMW], name="SM%d" % i) for i in range(4)]
    RTs = [P.sb([64, 14, 128], name="RTs%d" % i) for i in range(2)]
    accb = [P.sb([128, 1024], name="accb%d" % i) for i in range(2)]
    gts = [P.sb([128, 1408], name="gts%d" % i) for i in range(2)]
    xb = [P.sb([128, 1024], name="xb%d" % i) for i in range(2)]
    cen = P.sb([128, 1024], name="cen"); sq = P.sb([128, 1024], name="sq")
    ycb = P.sb([128, 1024], BF16, name="ycb"); yT = P.sb([128, 8, 128], BF16, name="yT")
    tmp = P.sb([128, 1024], name="tmp"); xo = [P.sb([128, 1024], name="xo%d" % i) for i in range(2)]
    s14 = P.sb([128, 14], name="s14"); r14 = P.sb([128, 14], name="r14")
    ss2 = P.sb([128, 2], name="ss2"); rstd = P.sb([128, 1], name="rstd")
    scrv = [scr[i].sub() for i in range(NT)]
    cnt = {"sm": 0, "rt": 0, "acc": 0, "fin": 0}

    def reset_states(st):
        for t_ in st:
            P.memset(t_, 0.0, eng="pool")

    def update(sm, d, st):
        STr, STg, STw = st
        P.tt(STr, STr, cdt[:, d * 256:(d + 1) * 256].rr("p (h c) -> p h c", c=64), ALU.mult, eng="pool")
        P.tt(STr, STr, sm[:, 0:256].rr("p (h c) -> p h c", c=64), ALU.add, eng="pool")
        for h in range(4):
            o = 256 + h * 97
            P.stt(STg[0:48, h, :], STg[0:48, h, :], sm[0:48, o + 96:o + 97], sm[0:48, o:o + 96], ALU.mult, ALU.add)
        pw = P.bk()
        for h in range(6):
            o = 644 + h * 128
            P.mm(pw[0:64, h * 64:(h + 1) * 64], sm[:, o:o + 64], STw[:, h, :])
        P.tt(STw, pw[0:64, 0:384].rr("p (h c) -> p h c", c=64),
             sm[:, 644:644 + 768].rr("p (h c) -> p h c", c=128)[:, :, 64:128], ALU.add)

    def load_sm(src):
        sm = SM[cnt["sm"] % 4]; cnt["sm"] += 1
        P.dma("act" if cnt["sm"] % 2 == 0 else "sp", sm, src)
        return sm

    def outputs(i, d, acc, st):
        STr, STg, STw = st
        rt = RTs[cnt["rt"] % 2]; cnt["rt"] += 1
        P.dma("pool", rt, o_rt[i][:, d * 14:(d + 1) * 14, :])
        for h in range(4):
            P.mm(pA[:, h * 64:(h + 1) * 64], rt[0:64, h, :], STr[:, h, :])
        for h in range(4):
            P.mm(pB[:, h * 96:(h + 1) * 96], rt[0:48, 4 + h, :], STg[0:48, h, :])
        for h in range(6):
            P.mm(pC[:, h * 64:(h + 1) * 64], rt[0:64, 8 + h, :], STw[:, h, :])
        P.tt(acc[:, 0:256], acc[:, 0:256], pA[:, 0:256], ALU.add)
        P.tt(acc[:, 256:640], acc[:, 256:640], pB[:, 0:384], ALU.add)
        P.tt(acc[:, 640:1024], acc[:, 640:1024], pC[:, 0:384], ALU.add)

    def headnorm(o3, c3, H, dh, center, eps, off):
        sv = s14[:, off:off + H]; rv = r14[:, off:off + H]
        src = o3
        if center:
            P.reduce(sv, o3)
            P.ts(sv, sv, -1.0 / dh, ALU.mult)
            P.tt(c3, o3, sv.rr("p (h o) -> p h o", o=1).bc([128, H, dh]), ALU.add)
            src = c3
        q3 = sq[:, 0:H * dh].rr("p (h c) -> p h c", c=dh)
        P.tt(q3, src, src, ALU.mult, eng="pool")
        P.reduce(rv, q3)
        P.act(rv, rv, AF.Sqrt, bias=eps, scale=1.0 / dh)
        P.recip(rv, rv)
        P.tt(c3, src, rv.rr("p (h o) -> p h o", o=1).bc([128, H, dh]), ALU.mult)

    def finish(i, acc):
        n = cnt["fin"]; cnt["fin"] += 1
        g = gts[n % 2]; xt = xb[n % 2]
        P.dma("sp", g, o_tok[i][:, 1024:2432])
        P.dma("act", xt, xs[i])
        headnorm(acc[:, 0:256].rr("p (h c) -> p h c", c=64), cen[:, 0:256].rr("p (h c) -> p h c", c=64), 4, 64, True, 1e-5, 0)
        headnorm(acc[:, 256:640].rr("p (h c) -> p h c", c=96), cen[:, 256:640].rr("p (h c) -> p h c", c=96), 4, 96, False, 1e-5, 4)
        headnorm(acc[:, 640:1024].rr("p (h c) -> p h c", c=64), cen[:, 640:1024].rr("p (h c) -> p h c", c=64), 6, 64, True, 64e-5, 8)
        P.tt(cen, cen, vb[:, 1, :], ALU.mult)
        P.tt(cen[:, 640:1024], cen[:, 640:1024], vb[:, 2, 640:1024], ALU.add, eng="pool")
        P.tt(cen[:, 640:1024], cen[:, 640:1024], g[:, 1024:1408], ALU.add, eng="pool")
        P.tt(ycb, cen, g[:, 0:1024], ALU.mult)
        for k in range(8):
            P.tr(ptb[:, k, :], ycb[:, k * 128:(k + 1) * 128], identb)
        P.copy(yT.rr("p a b -> p (a b)"), ptb.rr("p a b -> p (a b)"), eng="act")
        py = [pA, pB]
        for cb in range(2):
            for k in range(8):
                P.mm(py[cb], yT[:, k, :], Wo[:, k, cb * 512:(cb + 1) * 512], start=(k == 0), stop=(k == 7))
        s = 0 if i < 16 else 1
        x_o = xo[n % 2]
        rms_finish(P, py, xt, gg[:, s, :], x_o, tmp, ss2, rstd, sq)
        dd = xmid[i].sub()
        P.dma("sp", dd, x_o)
        finals.append(dd.res)

    for st in STS.values():
        reset_states(st)
    for s_ in range(NPRE):
        for d in range(2):
            update(load_sm(pre[d, s_]), d, STS[("lat", d)])
    for d in range(2):
        update(load_sm(pre[d, NPRE]), d, STS[("ctx", d)])
    for d in range(2):
        order = list(range(16)) if d == 0 else list(range(15, -1, -1))
        for seg in ("lat", "ctx"):
            st = STS[(seg, d)]
            tiles = order if seg == "lat" else [16]
            for n_, i in enumerate(tiles):
                acc = accb[cnt["acc"] % 2]; cnt["acc"] += 1
                if d == 0:
                    P.dma("sp", acc, o_tok[i][:, 0:1024])
                else:
                    P.dma("sp", acc, scrv[i])
                outputs(i, d, acc, st)
                if d == 0:
                    P.dma("act", scrv[i], acc)
                else:
                    finish(i, acc)
                if n_ < len(tiles) - 1:
                    update(load_sm(o_sum[i][:, d * SUMW:(d + 1) * SUMW]), d, st)
    finals.extend(v.res for v in scrv)
    P.emit(finals)
    return nc


def build_l3():
    nc = bass.Bass("TRN2", target_bir_lowering=False)
    P = Prog(nc)
    D = P.dram
    xs = D("xs", [NT, 128, 1024]); cT = D("cT", [128, 16]); wmod = D("wmod", [1024, 3072]); bmod = D("bmod", [2, 3072])
    gpre = D("gpre", [128, 8]); vecs = D("vecs", [1, 1024]); cm = D("cm", [128, 6, 128]); sel = D("sel", [2, 256])
    wg = D("wg", [1024, 2816]); wu = D("wu", [1024, 2816]); wd = D("wd", [2816, 1024])
    xout = D("xout", [NT, 128, 1024], kind="ExternalOutput")
    P.init_banks(4)
    pA = P.ps([128, 512], F32, name="pA"); pB = P.ps([128, 512], F32, name="pB")
    ptb = P.ps([128, 8, 128], BF16, name="ptb")
    finals = []
    cmt = P.sb([128, 6, 128], name="cmt"); P.dma("sp", cmt, cm)
    ident = cmt[:, 0, :]
    identb = P.sb([128, 128], BF16, name="identb"); P.copy(identb, ident)
    selt = P.sb([2, 256], name="selt"); P.dma("sp", selt, sel)
    gpret = P.sb([128, 8], name="gpret"); P.dma("act", gpret, gpre)
    vb = P.sb([128, 1024], name="vb"); P.dma("act", vb, V(vecs.ap[0].partition_broadcast(128), vecs.res))
    wst = [P.sb([128, 8, 128], name="wst%d" % i) for i in range(2)]
    ms = mod_setup(P, cT)
    pm = P.ps([128, 32], F32, name="pm")
    for w in range(2):
        modrow = mod_block(P, ms, wmod, bmod, w * 1024, wst)
        for k in range(8):
            P.mm(pm[:, (w * 8 + k) * 2:(w * 8 + k) * 2 + 2], modrow[0:2, k * 128:(k + 1) * 128], ident[0:2, 0:2])
    modp = P.sb([128, 2, 8, 2], name="modp")
    P.copy(modp.rr("p a k s -> p (a k s)"), pm[:, 0:32])
    gs = P.sb([128, 8, 2], name="gs")
    P.ts(gs, modp[:, 1, :, :], 1.0, ALU.add)
    P.tt(gs, gs, gpret.rr("p (k o) -> p k o", o=1).bc([128, 8, 2]), ALU.mult)
    sh = modp[:, 0, :, :]
    modrow = mod_block(P, ms, wmod, bmod, 2048, wst)
    gg = P.sb([128, 2, 1024], name="gg")
    for s in range(2):
        for cb in range(2):
            pb = P.bk()
            P.mm(pb, selt[0:2, s * 128:(s + 1) * 128], modrow[0:2, cb * 512:(cb + 1) * 512])
            P.tt(gg[:, s, cb * 512:(cb + 1) * 512], pb, vb[:, cb * 512:(cb + 1) * 512], ALU.mult)
    Wg = P.sb([128, 8, 2816], BF16, name="Wg"); Wu = P.sb([128, 8, 2816], BF16, name="Wu"); Wd = P.sb([128, 22, 1024], BF16, name="Wd")
    load_w_bf16(P, Wg, wg, 8, 2816, wst)
    load_w_bf16(P, Wu, wu, 8, 2816, wst)
    load_w_bf16(P, Wd, wd, 22, 1024, wst)

    xb = [P.sb([128, 1024], name="xb%d" % i) for i in range(3)]
    xn = P.sb([128, 1024], BF16, name="xn")
    ss = P.sb([128, 1], name="ss"); rstd = P.sb([128, 1], name="rstd"); ss2 = P.sb([128, 2], name="ss2")
    hT = P.sb([128, 8, 256], BF16, name="hT")
    hid = P.sb([128, 22, 256], BF16, name="hid")
    sg = [P.sb([128, 256], name="sg%d" % i) for i in range(2)]
    tmp = P.sb([128, 1024], name="tmp"); junk = tmp
    xo = [P.sb([128, 1024], name="xo%d" % i) for i in range(2)]
    groups = [(2 * g, 2 * g + 1) for g in range(8)] + [(16,)]
    nx = 0
    for grp in groups:
        T = len(grp) * 128
        xts = []
        for j, i in enumerate(grp):
            xt = xb[nx % 3]; nx += 1
            xts.append(xt)
            P.dma("sp" if j == 0 else "act", xt, xs[i])
            P.act(xn, xt, AF.Square, accum_out=ss)
            P.act(rstd, ss, AF.Sqrt, bias=1e-6, scale=1.0 / 1024)
            P.recip(rstd, rstd)
            P.ts(xn, xt, rstd, ALU.mult)
            for k in range(8):
                P.tr(ptb[:, k, :], xn[:, k * 128:(k + 1) * 128], identb)
            s = 0 if i < 16 else 1
            for k in range(8):
                P.act(hT[:, k, j * 128:(j + 1) * 128], ptb[:, k, :], AF.Identity, scale=gs[:, k, s:s + 1], bias=sh[:, k, s:s + 1])
        for hc in range(22):
            pg = P.bk(); pu = P.bk()
            for k in range(8):
                P.mm(pg[:, 0:T], Wg[:, k, hc * 128:(hc + 1) * 128], hT[:, k, 0:T], start=(k == 0), stop=(k == 7))
            for k in range(8):
                P.mm(pu[:, 0:T], Wu[:, k, hc * 128:(hc + 1) * 128], hT[:, k, 0:T], start=(k == 0), stop=(k == 7))
            sgt = sg[hc % 2]
            P.act(sgt[:, 0:T], pg[:, 0:T], AF.Silu)
            P.tt(hid[:, hc, 0:T], sgt[:, 0:T], pu[:, 0:T], ALU.mult)
        for j, i in enumerate(grp):
            py = [pA, pB]
            for cb in range(2):
                for hc in range(22):
                    P.mm(py[cb], hid[:, hc, j * 128:(j + 1) * 128], Wd[:, hc, cb * 512:(cb + 1) * 512], start=(hc == 0), stop=(hc == 21))
            s = 0 if i < 16 else 1
            x_o = xo[i % 2]
            rms_finish(P, py, xts[j], gg[:, s, :], x_o, tmp, ss2, rstd, junk)
            dd = xout[i].sub()
            P.dma("sp", dd, x_o)
            finals.append(dd.res)
    P.emit(finals)
    return nc


f32 = np.float32


def consts():
    C = 128
    cm = np.zeros((128, 6, 128), f32)
    cm[:, 0] = np.eye(C)
    cm[:, 1] = np.triu(np.ones((C, C)))
    cm[:, 2] = np.tril(np.ones((C, C)))
    cm[:, 3] = np.triu(np.ones((C, C)), 1)
    cm[:, 4] = np.tril(np.ones((C, C)), -1)
    cm[:, 5] = 1.0
    gam = 1.0 - np.exp2(-5.0 - np.arange(4))
    j = np.arange(C)[:, None].astype(np.float64); t = np.arange(C)[None, :].astype(np.float64)
    retD = np.zeros((128, 8, 128), np.float64); retQD = np.zeros((128, 8, 128), np.float64); retkd = np.zeros((128, 8), np.float64)
    cdec = np.zeros((2, 4))
    for d in range(2):
        for h in range(4):
            g = gam[h] if d == 0 else gam[3 - h]
            if d == 0:
                retD[:, d * 4 + h, :] = np.where(t >= j, 0.125 * g ** np.maximum(t - j, 0), 0.0)
                retQD[:, d * 4 + h, :] = np.eye(C) * (g ** (np.arange(C) + 1.0))[None, :]
                retkd[:, d * 4 + h] = 0.125 * g ** (127.0 - np.arange(C))
            else:
                retD[:, d * 4 + h, :] = np.where(j >= t, 0.125 * g ** np.maximum(j - t, 0), 0.0)
                retQD[:, d * 4 + h, :] = np.eye(C) * (g ** (128.0 - np.arange(C)))[None, :]
                retkd[:, d * 4 + h] = 0.125 * g ** (np.arange(C) * 1.0)
            cdec[d, h] = g ** 128.0
    jj = np.arange(C)[:, None]; tt_ = np.arange(C)[None, :]
    lv = np.zeros((128, 7, 384), f32)
    for l in range(7):
        s = 1 << l
        U = ((jj // (2 * s) == tt_ // (2 * s)) & ((jj % (2 * s)) < s) & ((tt_ % (2 * s)) >= s)).astype(f32)
        lv[:, l, 0:128] = U; lv[:, l, 128:256] = U.T; lv[:, l, 256:384] = U
    return dict(cm=cm, retD=retD.astype(f32), retQD=retQD.astype(f32), retkd=retkd.astype(f32), lv=lv), cdec


def rope_tables():
    tok = np.arange(8192)
    row = (tok // 64).astype(np.float64); col = (tok % 64).astype(np.float64)
    inv = 10000.0 ** (-np.arange(16, dtype=np.float64) / 16.0)
    inv32 = (np.float32(10000.0) ** (-np.arange(16, dtype=f32) / f32(16))).astype(f32)
    ar = (row.astype(f32)[:, None] * inv32[None, :]).astype(np.float64)
    ac = (col.astype(f32)[:, None] * inv32[None, :]).astype(np.float64)
    cos = np.concatenate([np.cos(ar), np.cos(ar), np.cos(ac), np.cos(ac)], 1)
    sins = np.concatenate([-np.sin(ar), np.sin(ar), -np.sin(ac), np.sin(ac)], 1)
    return cos.astype(f32), sins.astype(f32)


def core_tokens(c):
    b, q, ci = c // 4, c % 4, c % 2
    return b, q, ci


def split_tiles(xlat, xctx):
    out = []
    for c in range(8):
        b, q, ci = core_tokens(c)
        out.append(np.concatenate([xlat[b, q * 2048:(q + 1) * 2048].reshape(16, 128, 1024),
                                   xctx[b, ci * 128:(ci + 1) * 128][None]], 0))
    return out


def l1_inputs(inp, l, xlat, xctx, K):
    cos, sins = K["rope"]
    ins = []
    xs_all = split_tiles(xlat, xctx)
    conv = inp["rw_conv"][l]
    convp = np.zeros((128, 36), f32)
    for j, (a, b) in enumerate(RCH):
        for tap in range(3):
            convp[0:b - a, 3 * j + tap] = conv[tap, a:b]
    shared = dict(
        wmod=np.ascontiguousarray(inp["w_mod"][l][:, 0:2048]),
        bmod=np.ascontiguousarray(np.tile(inp["b_mod"][l][None, 0:2048], (2, 1))),
        gpre=np.ascontiguousarray(inp["norm_mix_pre"][l].reshape(8, 128).T),
        win=np.ascontiguousarray(inp["w_in"][l]), convp=convp,
        cm=K["c"]["cm"], lv=K["c"]["lv"], retD=K["c"]["retD"], retQD=K["c"]["retQD"], retkd=K["c"]["retkd"],
        gwa=np.ascontiguousarray(np.concatenate([inp["gla_wa2_f"][l], inp["gla_wa2_b"][l]], 1)),
        gba=np.ascontiguousarray(np.concatenate([inp["gla_ba_f"][l], inp["gla_ba_b"][l]])[None]),
        rw2=np.ascontiguousarray(np.concatenate([inp["rw_w2_f"][l], inp["rw_w2_b"][l]], 0)),
        rw0=np.ascontiguousarray(np.concatenate([inp["rw_w0_f"][l], inp["rw_w0_b"][l]])[None]),
        ra2=np.ascontiguousarray(inp["rw_a2"][l]), ra0=np.ascontiguousarray(inp["rw_a0"][l][None]),
        rg2=np.ascontiguousarray(inp["rw_g2"][l]),
        rvec=np.ascontiguousarray(np.stack([inp["rw_k_k"][l], inp["rw_k_a"][l], inp["rw_r_k"][l].reshape(384)])),
    )
    for c in range(8):
        b, q, ci = core_tokens(c)
        xh = np.zeros((34, 1024), f32); fl = np.zeros((34,), f32)
        for i in range(16):
            t0 = q * 2048 + i * 128
            if t0 > 0:
                xh[2 * i] = xlat[b, t0 - 1]; fl[2 * i] = 1
            if t0 + 128 < 8192:
                xh[2 * i + 1] = xlat[b, t0 + 128]; fl[2 * i + 1] = 1
        if ci == 1:
            xh[32] = xctx[b, 127]; fl[32] = 1
        else:
            xh[33] = xctx[b, 128]; fl[33] = 1
        cT = np.zeros((128, 8, 2), f32)
        cT[:, :, 0] = inp["c"][b].reshape(8, 128).T
        cT[:, :, 1] = inp["c_ctx"].reshape(8, 128).T
        rp = np.zeros((NT, 128, 1024), f32)
        sl = slice(q * 2048, (q + 1) * 2048)
        rp[:16, :, 0:512] = np.tile(cos[sl].reshape(16, 128, 64), (1, 1, 8))
        rp[:16, :, 512:1024] = np.tile(sins[sl].reshape(16, 128, 64), (1, 1, 8))
        rp[16, :, 0:512] = 1.0
        d = dict(shared)
        d.update(xs=xs_all[c], xh=xh, hfl=np.ascontiguousarray(np.tile(fl[None], (128, 1))), cT=cT.reshape(128, 16), rope=rp)
        ins.append(d)
    return ins


def build_pre(sums_all, c):
    b, q, ci = core_tokens(c)
    pre = np.zeros((2, 51, 64, SUMW), f32)
    ctx0 = sums_all[4 * b + 0][16]; ctx1 = sums_all[4 * b + 1][16]
    seq = [ctx0[:, 0:SUMW], ctx1[:, 0:SUMW]] + [sums_all[4 * b + j][i][:, 0:SUMW] for j in range(q) for i in range(16)]
    pre[0, 50 - len(seq):50] = np.stack(seq)
    seq = [ctx1[:, SUMW:], ctx0[:, SUMW:]] + [sums_all[4 * b + j][i][:, SUMW:] for j in range(3, q, -1) for i in range(15, -1, -1)]
    pre[1, 50 - len(seq):50] = np.stack(seq)
    if ci == 1:
        pre[0, 50] = ctx0[:, 0:SUMW]
    if ci == 0:
        pre[1, 50] = ctx1[:, SUMW:]
    return pre


def cT_of(inp, b):
    cT = np.zeros((128, 8, 2), f32)
    cT[:, :, 0] = inp["c"][b].reshape(8, 128).T
    cT[:, :, 1] = inp["c_ctx"].reshape(8, 128).T
    return cT.reshape(128, 16)


def sel_const():
    sel = np.zeros((2, 256), f32)
    sel[0, 0:128] = 1.0; sel[1, 128:256] = 1.0
    return sel


def l2_inputs(inp, l, xs_all, r1, K, cdec):
    sums_all = [r1[c]["o_sum"] for c in range(8)]
    cdt = np.zeros((64, 512), f32)
    for d in range(2):
        for h in range(4):
            cdt[:, d * 256 + h * 64:d * 256 + (h + 1) * 64] = cdec[d, h]
    vecs = np.zeros((3, 1024), f32)
    vecs[0] = inp["norm_mix_post"][l]
    vecs[1] = np.concatenate([inp["ret_norm"][l], np.tile(inp["gla_norm"][l], 4), inp["rw_ln_w"][l]])
    vecs[2, 640:] = inp["rw_ln_b"][l]
    shared = dict(wmod=np.ascontiguousarray(inp["w_mod"][l][:, 2048:3072]),
                  bmod=np.ascontiguousarray(np.tile(inp["b_mod"][l][None, 2048:3072], (2, 1))),
                  vecs=vecs, wout=np.ascontiguousarray(inp["w_out"][l]), cm=K["c"]["cm"], sel=sel_const(), cdt=cdt)
    ins = []
    for c in range(8):
        d = dict(shared)
        d.update(xs=xs_all[c], o_tok=r1[c]["o_tok"], o_rt=r1[c]["o_rt"], o_sum=r1[c]["o_sum"], pre=build_pre(sums_all, c),
                 cT=cT_of(inp, c // 4))
        ins.append(d)
    return ins


def l3_inputs(inp, l, r2, K):
    shared = dict(wmod=np.ascontiguousarray(inp["w_mod"][l][:, 3072:6144]),
                  bmod=np.ascontiguousarray(np.tile(inp["b_mod"][l][None, 3072:6144], (2, 1))),
                  gpre=np.ascontiguousarray(inp["norm_ffn_pre"][l].reshape(8, 128).T),
                  vecs=np.ascontiguousarray(inp["norm_ffn_post"][l][None]), cm=K["c"]["cm"], sel=sel_const(),
                  wg=np.ascontiguousarray(inp["w_ffn_gate"][l]), wu=np.ascontiguousarray(inp["w_ffn_up"][l]),
                  wd=np.ascontiguousarray(inp["w_ffn_down"][l]))
    ins = []
    for c in range(8):
        d = dict(shared)
        d.update(xs=r2[c]["xmid"], cT=cT_of(inp, c // 4))
        ins.append(d)
    return ins


def gather(r3, xlat, xctx, key="xout"):
    xlat = xlat.copy(); xctx = xctx.copy()
    for c in range(8):
        b, q, ci = core_tokens(c)
        xo = r3[c][key]
        xlat[b, q * 2048:(q + 1) * 2048] = xo[:16].reshape(2048, 1024)
        if q < 2:
            xctx[b, ci * 128:(ci + 1) * 128] = xo[16]
    return xlat, xctx


from concourse.bass_utils import run_bass_kernel_spmd

_CACHE = {}


def _programs():
    if "p" not in _CACHE:
        Kc, cdec = consts()
        _CACHE["p"] = (build_l1(), build_l2(cdec), build_l3(), dict(c=Kc, rope=rope_tables()), cdec)
    return _CACHE["p"]


def kernel(**inputs):
    inp = {k: np.ascontiguousarray(np.asarray(v, dtype=np.float32)) for k, v in inputs.items()}
    nc1, nc2, nc3, K, cdec = _programs()
    xlat, xctx = inp["x"], inp["ctx"]
    cores = list(range(8))
    for l in range(2):
        xs_all = split_tiles(xlat, xctx)
        r1 = run_bass_kernel_spmd(nc1, l1_inputs(inp, l, xlat, xctx, K), core_ids=cores).results
        r2 = run_bass_kernel_spmd(nc2, l2_inputs(inp, l, xs_all, r1, K, cdec), core_ids=cores).results
        r3 = run_bass_kernel_spmd(nc3, l3_inputs(inp, l, r2, K), core_ids=cores).results
        xlat, xctx = gather(r3, xlat, xctx)
    return xlat.astype(np.float32)
```
